# Optimizing a Trainium2 kernel written in Bass

```python
import jax, jax.numpy as jnp
from jax import lax
import numpy as np

D_MODEL = 2048
BATCH = 1
SEQ = 8192
DEPTH = 1

PLE_DIM = 256
CONV_A_WIDTH = 1024
CONV_A_K = 3
CONF_WIDTH = 1024
CONF_K = 31
D_FF = -(-8 * D_MODEL // (3 * 256)) * 256
EPS = 1e-6
LN_EPS = 1e-5

IN_SPLIT_SIZES = (
    CONV_A_WIDTH,
    CONV_A_WIDTH,
    CONV_A_WIDTH,
    CONF_WIDTH,
    CONF_WIDTH,
    D_MODEL,
    D_MODEL,
)
IN_COLS = sum(IN_SPLIT_SIZES)

kernel_name = "hybrid_gated_conv_conformer_block"


def rmsnorm(x, g):
    xf = x.astype(jnp.float32)
    y = xf * lax.rsqrt(jnp.mean(xf * xf, axis=-1, keepdims=True) + EPS)
    return (y * g.astype(jnp.float32)).astype(x.dtype)


def layernorm(x, g, b):
    xf = x.astype(jnp.float32)
    mu = jnp.mean(xf, axis=-1, keepdims=True)
    xc = xf - mu
    var = jnp.mean(xc * xc, axis=-1, keepdims=True)
    y = xc * lax.rsqrt(var + LN_EPS)
    return (y * g.astype(jnp.float32) + b.astype(jnp.float32)).astype(x.dtype)


def causal_depthwise_conv(u, w):
    k, c = w.shape
    return lax.conv_general_dilated(
        u, w[:, None, :].astype(u.dtype),
        window_strides=(1,), padding=[(k - 1, 0)],
        dimension_numbers=("NWC", "WIO", "NWC"),
        feature_group_count=c)


def setup_inputs(seed: int = 0) -> dict:
    key = jax.random.key(seed)
    ks = jax.random.split(key, 24)
    f32 = jnp.float32
    L = DEPTH

    def nrm(k, shape, scale):
        return jax.random.normal(k, shape, f32) * scale

    def gain(k, shape):
        return 1.0 + 0.02 * jax.random.normal(k, shape, f32)

    return {
        "x": jax.random.normal(ks[0], (BATCH, SEQ, D_MODEL), f32),
        "p": jax.random.normal(ks[1], (DEPTH, BATCH, SEQ, PLE_DIM), f32),
        "g_mix": gain(ks[2], (L, D_MODEL)),
        "w_in": nrm(ks[3], (L, D_MODEL, IN_COLS), D_MODEL ** -0.5),
        "conv_a_w": nrm(ks[4], (L, CONV_A_K, CONV_A_WIDTH), CONV_A_K ** -0.5),
        "w_out_a": nrm(ks[5], (L, CONV_A_WIDTH, D_MODEL), CONV_A_WIDTH ** -0.5),
        "b_glu": nrm(ks[6], (L, 2 * CONF_WIDTH), 0.02),
        "conf_dw_w": nrm(ks[7], (L, CONF_K, CONF_WIDTH), CONF_K ** -0.5),
        "conf_dw_b": nrm(ks[8], (L, CONF_WIDTH), 0.02),
        "conf_ln_g": gain(ks[9], (L, CONF_WIDTH)),
        "conf_ln_b": nrm(ks[10], (L, CONF_WIDTH), 0.02),
        "w_pw_b": nrm(ks[11], (L, CONF_WIDTH, D_MODEL), CONF_WIDTH ** -0.5),
        "b_pw_b": nrm(ks[12], (L, D_MODEL), 0.02),
        "w_o": nrm(ks[13], (L, D_MODEL, D_MODEL), D_MODEL ** -0.5),
        "g_ffn": gain(ks[14], (L, D_MODEL)),
        "w_gate": nrm(ks[15], (L, D_MODEL, D_FF), D_MODEL ** -0.5),
        "w_up": nrm(ks[16], (L, D_MODEL, D_FF), D_MODEL ** -0.5),
        "w_down": nrm(ks[17], (L, D_FF, D_MODEL), D_FF ** -0.5),
        "g_ple": gain(ks[18], (L, D_MODEL)),
        "w_ple_gate": nrm(ks[19], (L, D_MODEL, D_MODEL), D_MODEL ** -0.5),
        "w_ple_proj": nrm(ks[20], (L, PLE_DIM, D_MODEL), PLE_DIM ** -0.5),
        "g_final": gain(ks[21], (D_MODEL,)),
    }


def reference(x, p, g_mix, w_in, conv_a_w, w_out_a, b_glu, conf_dw_w, conf_dw_b,
              conf_ln_g, conf_ln_b, w_pw_b, b_pw_b, w_o, g_ffn, w_gate, w_up, w_down,
              g_ple, w_ple_gate, w_ple_proj, g_final):
    h = x
    cuts = np.cumsum(IN_SPLIT_SIZES)[:-1].tolist()
    for i in range(DEPTH):
        n = rmsnorm(h, g_mix[i])
        proj = jnp.einsum("bsd,dc->bsc", n, w_in[i])
        a_h, a_b, a_c, glu_v, glu_g, gate_a, gate_b = jnp.split(proj, cuts, axis=-1)

        y_a = a_b * causal_depthwise_conv(a_c * a_h, conv_a_w[i])
        y_a = jnp.einsum("bsc,cd->bsd", y_a, w_out_a[i])

        bv, bg = jnp.split(b_glu[i], 2)
        u = (glu_v + bv) * jax.nn.sigmoid(glu_g + bg)
        v = causal_depthwise_conv(u, conf_dw_w[i]) + conf_dw_b[i]
        v = jax.nn.silu(layernorm(v, conf_ln_g[i], conf_ln_b[i]))
        y_b = jnp.einsum("bsc,cd->bsd", v, w_pw_b[i]) + b_pw_b[i]

        m = jax.nn.sigmoid(gate_a) * y_a + jax.nn.sigmoid(gate_b) * y_b
        h = h + jnp.einsum("bsd,de->bse", m, w_o[i])

        n2 = rmsnorm(h, g_ffn[i])
        f = jax.nn.silu(jnp.einsum("bsd,df->bsf", n2, w_gate[i])) * jnp.einsum("bsd,df->bsf", n2, w_up[i])
        h = h + jnp.einsum("bsf,fd->bsd", f, w_down[i])

        n3 = rmsnorm(h, g_ple[i])
        ple = jnp.einsum("bse,ed->bsd", p[i].astype(h.dtype), w_ple_proj[i])
        h = h + jax.nn.sigmoid(jnp.einsum("bsd,de->bse", n3, w_ple_gate[i])) * ple

    return rmsnorm(h, g_final)
```

```python
import numpy as np
import concourse.bass as bass
import concourse.mybir as mybir
from concourse.bass_utils import run_bass_kernel_spmd

F32 = mybir.dt.float32
BF16 = mybir.dt.bfloat16
AF = mybir.ActivationFunctionType
ALU = mybir.AluOpType

NCORES = 8
D = 2048
SEQ = 8192
T = SEQ // NCORES
HALO = 32
TE = T + HALO
NT_TILES = T // 128
KC = D // 128
CW = 1024
NJ = CW // 128
DFF = 5632
NFB = DFF // 512
PLE = 256
EPS = 1e-6
LN_EPS = 1e-5
SLOT = 8192
NSLOT = 4

COMPUTE = ("pe", "act", "dve", "pool")
ENGS = ("pe", "act", "dve", "pool", "sp")


class Buf:
    __slots__ = ("name", "writer", "readers", "sem", "cnt")

    def __init__(self, name):
        self.name = name
        self.writer = None
        self.readers = []
        self.sem = None
        self.cnt = 0


class Op:
    __slots__ = ("eng", "fn", "deps", "idx", "needed", "val", "dma_buf", "dma_val", "is_dma")


class Prog:
    def __init__(self, nc):
        self.nc = nc
        self.streams = {e: [] for e in ENGS}
        self.dma_bufs = []

    def op(self, eng, fn, reads=(), writes=(), dma=None):
        o = Op()
        o.eng = eng
        o.fn = fn
        o.needed = False
        o.val = None
        o.is_dma = dma is not None
        o.dma_buf = dma
        o.dma_val = None
        deps = []
        for b in reads:
            if b.writer is not None:
                deps.append(b.writer)
            b.readers.append(o)
        for b in writes:
            if b.writer is not None:
                deps.append(b.writer)
            deps.extend(r for r in b.readers if r is not o)
            b.writer = o
            b.readers = []
        if dma is not None:
            if dma.sem is None:
                self.dma_bufs.append(dma)
                dma.sem = True
            dma.cnt += 16
            o.dma_val = dma.cnt
        o.deps = [d for d in deps if d is not o and not (eng == "pe" and d.eng == "pe" and not d.is_dma)]
        o.idx = len(self.streams[eng])
        self.streams[eng].append(o)
        return o

    def emit(self):
        nc = self.nc
        for e in ENGS:
            for o in self.streams[e]:
                for d in o.deps:
                    if not d.is_dma:
                        d.needed = True
        for e in COMPUTE:
            c = 0
            for o in self.streams[e]:
                if o.needed and not o.is_dma:
                    c += 1
                    o.val = c
        sems = {e: nc.alloc_semaphore("S_" + e) for e in COMPUTE}
        for b in self.dma_bufs:
            b.sem = nc.alloc_semaphore("D_" + b.name)
        streams = self.streams

        def run(e, eng):
            waited = {}
            for o in streams[e]:
                w = {}
                for d in o.deps:
                    if d.is_dma:
                        s, v = d.dma_buf.sem, d.dma_val
                    else:
                        s, v = sems[d.eng], d.val
                    k = id(s)
                    if k not in w or w[k][1] < v:
                        w[k] = (s, v)
                for k, (s, v) in w.items():
                    if waited.get(k, 0) >= v:
                        continue
                    waited[k] = v
                    eng.wait_ge(s, v)
                ins = o.fn(eng)
                if o.is_dma:
                    ins.then_inc(o.dma_buf.sem, 16)
                elif o.needed:
                    ins.then_inc(sems[e], 1)

        with nc.Block() as block:
            @block.tensor
            def _(eng):
                run("pe", eng)

            @block.scalar
            def _(eng):
                run("act", eng)

            @block.vector
            def _(eng):
                run("dve", eng)

            @block.gpsimd
            def _(eng):
                run("pool", eng)

            @block.sync
            def _(eng):
                run("sp", eng)


def _kc(w):
    k, c = w.shape
    return np.ascontiguousarray(w.reshape(k // 128, 128, c).transpose(1, 0, 2)).reshape(128, -1)


def _ch(group, j):
    base = {"ah": 0, "ab": 8, "ac": 16, "gv": 24, "gg": 32, "ga": 40, "gb": 56}[group]
    return base + j


B_ORDER = []
for _i in range(0, NJ, 2):
    B_ORDER += [("gg", _i), ("gv", _i), ("gg", _i + 1), ("gv", _i + 1)]
A_ORDER = []
for _j in range(NJ):
    A_ORDER += [("ah", _j), ("ac", _j), ("ab", _j)]


def _load_plan():
    plan = []
    for i in range(4):
        plan.append((f"B{i}", 16 * 512))
    for i in range(6):
        plan.append((f"A{i}", 16 * 512))
    for c in range(16):
        plan.append((f"G{c}", 6144))
    for n in range(4):
        plan.append((f"O{n}", 16 * 512))
    for b in range(NFB):
        plan.append((f"FG{b}", 16 * 512))
        plan.append((f"FU{b}", 16 * 512))
        plan.append((f"FD{b}", 4 * 2048))
    for n in range(4):
        plan.append((f"PP{n}", 2 * 512))
        plan.append((f"PG{n}", 16 * 512))
    return plan


PLAN = _load_plan()
WTOT = sum(n for _, n in PLAN)


def _prep_weights(w_in, w_out_a, w_pw_b, w_o, w_gate, w_up, w_down, w_ple_gate, w_ple_proj):
    wflat = np.empty((128, WTOT), np.float32)
    pos = 0

    def put(a):
        nonlocal pos
        n = a.shape[1]
        wflat[:, pos:pos + n] = a
        pos += n

    def cols(chunks):
        idx = np.concatenate([np.arange(_ch(g, j) * 128, _ch(g, j) * 128 + 128) for g, j in chunks])
        return _kc(w_in[:, idx])

    for i in range(4):
        put(cols(B_ORDER[4 * i:4 * i + 4]))
    for i in range(6):
        put(cols(A_ORDER[4 * i:4 * i + 4]))
    for c in range(16):
        put(_kc(w_out_a[:, c * 128:(c + 1) * 128]))
        put(_kc(w_pw_b[:, c * 128:(c + 1) * 128]))
        put(cols([("ga", c)]))
        put(cols([("gb", c)]))
    for n in range(4):
        put(_kc(w_o[:, n * 512:(n + 1) * 512]))
    for b in range(NFB):
        put(_kc(w_gate[:, b * 512:(b + 1) * 512]))
        put(_kc(w_up[:, b * 512:(b + 1) * 512]))
        put(_kc(w_down[b * 512:(b + 1) * 512, :]))
    for n in range(4):
        put(_kc(w_ple_proj[:, n * 512:(n + 1) * 512]))
        put(_kc(w_ple_gate[:, n * 512:(n + 1) * 512]))
    assert pos == WTOT
    return wflat


P_BV, P_BG, P_CA, P_CW, P_DWB, P_LNG, P_LNB, P_BPW = 0, 8, 16, 40, 288, 296, 304, 312
NPRM = 328


def _fm(v, n):
    return np.ascontiguousarray(np.asarray(v, np.float32).reshape(n, 128).T)


def _prep_prm(conv_a_w, b_glu, conf_dw_w, conf_dw_b, conf_ln_g, conf_ln_b, b_pw_b):
    prm = np.zeros((128, NPRM), np.float32)
    prm[:, P_BV:P_BV + 8] = _fm(b_glu[:CW], 8)
    prm[:, P_BG:P_BG + 8] = _fm(b_glu[CW:], 8)
    prm[:, P_CA:P_CA + 24] = np.ascontiguousarray(conv_a_w.reshape(3, 8, 128).transpose(2, 1, 0)).reshape(128, 24)
    prm[:, P_CW:P_CW + 248] = np.ascontiguousarray(conf_dw_w.reshape(31, 8, 128).transpose(2, 1, 0)).reshape(128, 248)
    prm[:, P_DWB:P_DWB + 8] = _fm(conf_dw_b, 8)
    prm[:, P_LNG:P_LNG + 8] = _fm(conf_ln_g, 8)
    prm[:, P_LNB:P_LNB + 8] = _fm(conf_ln_b, 8)
    prm[:, P_BPW:P_BPW + 16] = _fm(b_pw_b, 16)
    return prm


def build(stop=None):
    nc = bass.Bass("TRN2", target_bir_lowering=False)
    x_d = nc.dram_tensor("x_ext", [TE, D], F32, kind="ExternalInput").ap()
    p_d = nc.dram_tensor("p_c", [128, NT_TILES * PLE], F32, kind="ExternalInput").ap()
    hm_d = nc.dram_tensor("hmask", [128, HALO], F32, kind="ExternalInput").ap()
    w_d = nc.dram_tensor("wflat", [128, WTOT], F32, kind="ExternalInput").ap()
    prm_d = nc.dram_tensor("prm", [128, NPRM], F32, kind="ExternalInput").ap()
    g_d = nc.dram_tensor("gains", [4, D], F32, kind="ExternalInput").ap()
    out_d = nc.dram_tensor("out", [T, D], F32, kind="ExternalOutput").ap()
    dbg_d = None
    if stop is not None:
        dbg_d = nc.dram_tensor("dbg", [128, 16384], F32, kind="ExternalOutput").ap()

    NB4 = 52992
    big = nc.alloc_sbuf_tensor("big", [128, NB4], F32)

    def reg(off, nbytes, dt=F32):
        assert off % 4 == 0 and nbytes % 4 == 0 and (off + nbytes) <= NB4 * 4, (off, nbytes)
        v = big[:, off // 4:(off + nbytes) // 4]
        return v.bitcast(BF16) if dt == BF16 else v

    ring = [reg(i * 16384, 16384, BF16) for i in range(NSLOT)]
    O_NT, O_YA, O_VT, O_H, O_VPRE, O_GB, O_SM, O_SC = 65536, 99328, 115712, 65536, 132096, 164864, 173056, 181248
    SC_SIZE = NB4 * 4 - O_SC
    nT1 = reg(O_NT, 16 * TE * 2, BF16).rearrange("p (k t) -> p k t", k=16)
    yaT = reg(O_YA, 16384, BF16).rearrange("p (k t) -> p k t", k=8)
    vT = reg(O_VT, 16384, BF16).rearrange("p (k t) -> p k t", k=8)
    xs = reg(O_YA, 65536).rearrange("p (t d) -> p t d", t=8)
    h = reg(O_H, 65536).rearrange("p (t d) -> p t d", t=8)
    vpre = reg(O_VPRE, 32768).rearrange("p (k t) -> p k t", k=8)
    mT = reg(O_VPRE, 32768, BF16).rearrange("p (k t) -> p k t", k=16)
    nT2 = mT
    gb = reg(O_GB, 8192)
    prm = reg(O_SM, 1312)
    hmask = reg(O_SM + 1344, 128)
    ident = reg(O_SM + 1472, 256, BF16)
    ones32 = reg(O_SM + 1728, 512)
    ssq = reg(O_SM + 2240, 160)
    rstd = reg(O_SM + 2400, 160)
    stmp = reg(O_SM + 2560, 160)
    epsr = reg(O_SM + 2720, 4)
    epsl = reg(O_SM + 2724, 4)
    identf = reg(O_SM + 2752, 512)
    pT = reg(O_SM + 3328, 4096, BF16).rearrange("p (k t) -> p k t", k=2)
    mhalf1 = reg(O_SM + 7424, 4)

    banks = [nc.alloc_psum_tensor(f"bank{i}", [128, 512], F32) for i in range(8)]
    banks_bf = [b[:].bitcast(BF16) for b in banks]

    P = Prog(nc)
    RS = [Buf(f"ring{i}") for i in range(NSLOT)]
    PB = [Buf(f"pb{i}") for i in range(8)]
    PAGE = 512
    SCP = [Buf(f"sc{i}") for i in range(SC_SIZE // PAGE + 1)]
    B_NT, B_YA, B_VT, B_VPRE, B_GB = Buf("nT"), Buf("ya"), Buf("vT"), Buf("vpre"), Buf("gb")
    B_TT = [Buf(f"tt{t}") for t in range(8)]
    B_HT = [Buf(f"h{t}") for t in range(8)]
    B_XS = [Buf(f"xs{t}") for t in range(8)]
    B_SS = [Buf(f"ss{i}") for i in range(40)]
    B_PRM, B_CONST, B_PT = Buf("prm"), Buf("const"), Buf("pT")
    B_OUT = Buf("outst")
    B_ID, B_IDF, B_ONES, B_EPS = Buf("ident"), Buf("identf"), Buf("ones"), Buf("eps")

    class Scr:
        def __init__(self, off, nbytes, dt=F32):
            assert off + nbytes <= SC_SIZE, (off, nbytes, SC_SIZE)
            self.ap = reg(O_SC + off, nbytes, dt)
            self.bufs = SCP[off // PAGE:(off + nbytes - 1) // PAGE + 1]

    wstate = {"next": 0, "off": 0}
    loaded = {}

    def issue_load(after=()):
        i = wstate["next"]
        if i >= len(PLAN):
            return
        name, n = PLAN[i]
        s = i % NSLOT
        off = wstate["off"]
        P.op("pool", lambda e, s=s, off=off, n=n: e.dma_start(out=ring[s][:, 0:n], in_=w_d[:, off:off + n]),
             reads=list(after), writes=[RS[s]], dma=RS[s])
        loaded[name] = s
        wstate["next"] = i + 1
        wstate["off"] = off + n

    PIDX = {nm: i for i, (nm, _) in enumerate(PLAN)}

    def use(*names, after=()):
        k0 = min(PIDX[n] for n in names)
        while wstate["next"] < len(PLAN) and wstate["next"] <= k0 + NSLOT - 1:
            issue_load(after)
        return [loaded[n] for n in names]

    rr = {"i": 0, "t": 0}

    def next_bank():
        b = rr["i"] % 6
        rr["i"] += 1
        return b

    def next_tbank():
        b = 6 + rr["t"] % 2
        rr["t"] += 1
        return b

    def mm_group(out_ap, pairs, reads, writes):
        def fn(e):
            n = len(pairs)
            r = None
            for i, (l, rh) in enumerate(pairs):
                r = e.matmul(out_ap, lhsT=l, rhs=rh, start=(i == 0), stop=(i == n - 1))
            return r
        return P.op("pe", fn, reads=reads, writes=writes)

    def dump_bf(view3, nq, kq, reads):
        cv = Scr(0, 16384)
        for q in range(nq):
            P.op("act", lambda e, q=q: e.activation(out=cv.ap.rearrange("p (k t) -> p k t", k=4),
                                                    in_=view3[:, 4 * q:4 * q + 4, :], func=AF.Copy),
                 reads=reads, writes=cv.bufs)
            P.op("sp", lambda e, q=q: e.dma_start(out=dbg_d[:, q * 4096:(q + 1) * 4096], in_=cv.ap), reads=cv.bufs, dma=B_OUT)
        P.op("sp", lambda e: e.nop(), writes=[B_OUT] + cv.bufs)
        P.emit()
        return nc

    def dump_h():
        for t in range(8):
            P.op("sp", lambda e, t=t: e.dma_start(out=dbg_d[:, t * 2048:(t + 1) * 2048], in_=h[:, t, :]), reads=[B_HT[t]], dma=B_OUT)
        P.op("sp", lambda e: e.nop(), writes=[B_OUT])
        P.emit()
        return nc

    pst = Scr(8192, 8192)
    P.op("sp", lambda e: e.dma_start(out=pst.ap, in_=p_d), writes=pst.bufs, dma=pst.bufs[0])
    xh = Scr(0, 8192)
    P.op("sp", lambda e: e.dma_start(out=xh.ap[0:HALO, :], in_=x_d[0:HALO, :]), writes=xh.bufs, dma=xh.bufs[0])
    P.op("sp", lambda e: e.dma_start(out=gb, in_=g_d[0:1, :].partition_broadcast(128)), writes=[B_GB], dma=B_GB)
    for t in range(NT_TILES):
        P.op("sp", lambda e, t=t: e.dma_start(out=xs[:, t, :], in_=x_d[HALO + t * 128:HALO + (t + 1) * 128, :]),
             writes=[B_XS[t]], dma=B_XS[t])
    P.op("sp", lambda e: e.dma_start(out=prm, in_=prm_d), writes=[B_PRM], dma=B_PRM)
    P.op("sp", lambda e: e.dma_start(out=hmask, in_=hm_d), writes=[B_CONST], dma=B_CONST)

    P.op("pool", lambda e: e.memset(identf, 0.0), writes=[B_IDF])
    P.op("pool", lambda e: e.affine_select(out=identf, in_=identf, pattern=[[-1, 128]], compare_op=ALU.not_equal,
                                           fill=1.0, base=0, channel_multiplier=1), reads=[B_IDF], writes=[B_IDF])
    P.op("pool", lambda e: e.memset(ones32, 1.0 / CW), writes=[B_ONES])

    def mk_eps(e):
        e.memset(epsr, EPS)
        e.memset(mhalf1, -0.5)
        return e.memset(epsl, LN_EPS)
    P.op("pool", mk_eps, writes=[B_EPS])
    P.op("pool", lambda e: e.memset(ssq, 0.0), writes=B_SS)
    P.op("dve", lambda e: e.tensor_copy(out=ident, in_=identf), reads=[B_IDF], writes=[B_ID])


    ntb = [Scr(16384, 4096, BF16), Scr(20480, 4096, BF16)]

    def norm_p1(src_ap, src_bufs, col, npart, ntile, out_ap=None, junk=None):
        dst = ntile.ap[0:npart, :] if out_ap is None else out_ap
        jb = ntile.bufs
        if junk is not None:
            dst, jb = junk.ap, junk.bufs
        sb = B_SS[col]
        P.op("act", lambda e: e.activation(out=dst, in_=src_ap, func=AF.Square, accum_out=ssq[0:npart, col:col + 1]),
             reads=src_bufs, writes=jb + [sb])
        P.op("dve", lambda e: e.tensor_scalar(out=stmp[0:npart, col:col + 1], in0=ssq[0:npart, col:col + 1],
                                              scalar1=1.0 / D, scalar2=EPS, op0=ALU.mult, op1=ALU.add),
             reads=[sb], writes=[sb])
        P.op("pool", lambda e: e.tensor_tensor(out=rstd[0:npart, col:col + 1], in0=stmp[0:npart, col:col + 1],
                                               in1=mhalf1[0:npart, :], op=ALU.pow),
             reads=[sb, B_EPS], writes=[sb])

    def norm_p2(src_ap, src_bufs, col, npart, ntile, out_ap=None):
        dst = ntile.ap[0:npart, :] if out_ap is None else out_ap
        sb = B_SS[col]
        P.op("dve", lambda e: e.scalar_tensor_tensor(out=dst, in0=src_ap, scalar=rstd[0:npart, col:col + 1],
                                                     in1=gb[0:npart, :], op0=ALU.mult, op1=ALU.mult),
             reads=src_bufs + [sb, B_GB], writes=ntile.bufs)

    def norm_ops(src_ap, src_bufs, col, npart, ntile, out_ap=None):
        norm_p1(src_ap, src_bufs, col, npart, ntile, out_ap)
        norm_p2(src_ap, src_bufs, col, npart, ntile, out_ap)

    def transposes(ntile, dst_views, dst_bufs, engs=("act", "dve")):
        for half in range(2):
            tb = next_tbank()

            def fn(e, half=half, tb=tb):
                r = None
                for i in range(8):
                    kc = half * 8 + i
                    r = e.transpose(out=banks_bf[tb][:, i * 128:(i + 1) * 128],
                                    in_=ntile.ap[:, kc * 128:(kc + 1) * 128], identity=ident)
                return r
            P.op("pe", fn, reads=ntile.bufs + [B_ID], writes=[PB[tb]])
            src = banks_bf[tb].rearrange("p (k t) -> p k t", k=8)
            dst = dst_views[half]
            if engs[half] == "act":
                P.op("act", lambda e, dst=dst, src=src: e.activation(out=dst, in_=src, func=AF.Copy),
                     reads=[PB[tb]], writes=dst_bufs)
            else:
                P.op("dve", lambda e, dst=dst, src=src: e.tensor_copy(out=dst, in_=src),
                     reads=[PB[tb]], writes=dst_bufs)

    nth = Scr(24576, 4096, BF16)
    norm_ops(xh.ap[0:HALO, :], xh.bufs, 8, HALO, nth)
    tb = next_tbank()

    def fn_h(e, tb=tb):
        r = None
        for kc in range(16):
            r = e.transpose(out=banks_bf[tb][:, kc * HALO:(kc + 1) * HALO],
                            in_=nth.ap[0:HALO, kc * 128:(kc + 1) * 128], identity=ident[0:HALO, 0:HALO])
        return r
    P.op("pe", fn_h, reads=nth.bufs + [B_ID], writes=[PB[tb]])
    src_h = banks_bf[tb][:, 0:16 * HALO].rearrange("p (k t) -> p k t", k=16)
    P.op("act", lambda e: e.activation(out=nT1[:, :, 0:HALO], in_=src_h, func=AF.Copy), reads=[PB[tb]], writes=[B_NT])
    ntb3 = ntb + [Scr(24576, 4096, BF16)]

    def tr0(t):
        c0 = HALO + t * 128
        transposes(ntb3[t % 3], [nT1[:, 0:8, c0:c0 + 128], nT1[:, 8:16, c0:c0 + 128]], [B_NT])
    for t in range(NT_TILES + 2):
        if t < NT_TILES:
            norm_p1(xs[:, t, :], [B_XS[t]], t, 128, ntb3[t % 3])
        if 3 <= t < 3 + NSLOT:
            issue_load()
        if 0 <= t - 1 < NT_TILES:
            norm_p2(xs[:, t - 1, :], [B_XS[t - 1]], t - 1, 128, ntb3[(t - 1) % 3])
        if 0 <= t - 2 < NT_TILES:
            tr0(t - 2)

    pbf = Scr(0, 4096, BF16)
    P.op("dve", lambda e: e.tensor_copy(out=pbf.ap, in_=pst.ap), reads=pst.bufs, writes=pbf.bufs)
    for half in range(2):
        tb = next_tbank()

        def fn(e, half=half, tb=tb):
            r = None
            for i in range(8):
                q = half * 8 + i
                r = e.transpose(out=banks_bf[tb][:, i * 128:(i + 1) * 128],
                                in_=pbf.ap[:, q * 128:(q + 1) * 128], identity=ident)
            return r
        P.op("pe", fn, reads=pbf.bufs + [B_ID], writes=[PB[tb]])
        src = banks_bf[tb].rearrange("p (t c k) -> p c t k", t=4, c=2)
        dst = pT[:, :, half * 512:(half + 1) * 512].rearrange("p c (t k) -> p c t k", t=4)
        P.op("act", lambda e, dst=dst, src=src: e.activation(out=dst, in_=src, func=AF.Copy),
             reads=[PB[tb]], writes=[B_PT])

    P.op("sp", lambda e: e.dma_start(out=gb, in_=g_d[1:2, :].partition_broadcast(128)), writes=[B_GB], dma=B_GB)

    if stop == "n1T":
        return dump_bf(nT1[:, :, HALO:TE], 4, 4, [B_NT])

    TT3 = [(i * 352, 352) for i in range(3)]
    KD = 10
    NPE = 31 - KD
    sg1 = Scr(0, 4224)
    ub_ = [Scr(4608, 2112, BF16), Scr(7168, 2112, BF16)]
    DGS = -(-NPE * 256 // 512) * 512
    dg_ = [Scr(9728, NPE * 256, BF16), Scr(9728 + DGS, NPE * 256, BF16)]
    acc1 = Scr(9728 + 2 * DGS, 4096)

    def w_in_chunk(slot_i, q):
        v = ring[slot_i].rearrange("p (k c) -> p k c", k=16)
        return [v[:, kc, q * 128:(q + 1) * 128] for kc in range(16)]

    pos_in = {}
    for i, ch in enumerate(B_ORDER):
        pos_in[ch] = (f"B{i // 4}", i % 4)
    for i, ch in enumerate(A_ORDER):
        pos_in[ch] = (f"A{i // 4}", i % 4)

    def inproj_ext(group, j, consume):
        name, q = pos_in[(group, j)]
        s, = use(name)
        lhs = w_in_chunk(s, q)
        for (c0, n) in TT3:
            b = next_bank()
            mm_group(banks[b][:, 0:n], [(lhs[kc], nT1[:, kc, c0:c0 + n]) for kc in range(16)],
                     reads=[RS[s], B_NT], writes=[PB[b]])
            consume(b, c0, n)

    first = {"vpre": True, "ya": True, "vT": True}

    def xs_guard(key):
        if first[key]:
            first[key] = False
            return list(B_XS)
        return []

    for j in range(NJ):
        sg, ub, dg = sg1, ub_[j % 2], dg_[j % 2]
        d3 = dg.ap.rearrange("p (k c) -> p k c", k=NPE)
        id3 = ident.unsqueeze(1).to_broadcast([128, NPE, 128])
        cw3 = prm[:, P_CW + j * 31 + KD:P_CW + (j + 1) * 31].unsqueeze(2).to_broadcast([128, NPE, 128])
        P.op("pool", lambda e, d3=d3, cw3=cw3, id3=id3: e.tensor_tensor(out=d3, in0=id3, in1=cw3, op=ALU.mult),
             reads=[B_ID, B_PRM], writes=dg.bufs)

        def cons_g(b, c0, n, sg=sg, j=j):
            P.op("act", lambda e: e.activation(out=sg.ap[:, c0:c0 + n], in_=banks[b][:, 0:n], func=AF.Sigmoid,
                                               bias=prm[:, P_BG + j:P_BG + j + 1]),
                 reads=[PB[b], B_PRM], writes=sg.bufs)
        inproj_ext("gg", j, cons_g)

        def cons_v(b, c0, n, sg=sg, ub=ub, j=j):
            P.op("dve", lambda e: e.scalar_tensor_tensor(out=ub.ap[:, c0:c0 + n], in0=banks[b][:, 0:n],
                                                         scalar=prm[:, P_BV + j:P_BV + j + 1], in1=sg.ap[:, c0:c0 + n],
                                                         op0=ALU.add, op1=ALU.mult),
                 reads=[PB[b], B_PRM] + sg.bufs, writes=ub.bufs)
            if c0 == 0:
                P.op("dve", lambda e: e.tensor_tensor(out=ub.ap[:, 0:HALO], in0=ub.ap[:, 0:HALO], in1=hmask, op=ALU.mult),
                     reads=ub.bufs + [B_CONST], writes=ub.bufs)
        inproj_ext("gv", j, cons_v)

        cbanks = []
        for tt in range(2):
            b = next_bank()
            cbanks.append(b)
            base = HALO + tt * 512 - 30
            mm_group(banks[b][:, :], [(d3[:, k - KD, :], ub.ap[:, base + k:base + k + 512]) for k in range(KD, 31)],
                     reads=dg.bufs + ub.bufs, writes=[PB[b]])
        for k in range(KD):
            src = ub.ap[:, HALO - 30 + k:HALO - 30 + k + T]
            wk = prm[:, P_CW + j * 31 + k:P_CW + j * 31 + k + 1]
            if k == 0:
                P.op("dve", lambda e, src=src, wk=wk: e.tensor_scalar(out=acc1.ap, in0=src, scalar1=wk, scalar2=None, op0=ALU.mult),
                     reads=ub.bufs + [B_PRM], writes=acc1.bufs)
            else:
                P.op("dve", lambda e, src=src, wk=wk: e.scalar_tensor_tensor(out=acc1.ap, in0=src, scalar=wk, in1=acc1.ap,
                                                                           op0=ALU.mult, op1=ALU.add),
                     reads=ub.bufs + [B_PRM] + acc1.bufs, writes=acc1.bufs)
        for tt in range(2):
            b = cbanks[tt]
            P.op("act", lambda e, b=b, j=j, tt=tt: e.activation(out=vpre[:, j, tt * 512:(tt + 1) * 512], in_=banks[b][:, :],
                                                               func=AF.Identity, bias=prm[:, P_DWB + j:P_DWB + j + 1]),
                 reads=[PB[b], B_PRM], writes=[B_VPRE] + xs_guard("vpre"))
        P.op("pool", lambda e, j=j: e.tensor_tensor(out=vpre[:, j, :], in0=vpre[:, j, :], in1=acc1.ap, op=ALU.add),
             reads=acc1.bufs + [B_VPRE], writes=[B_VPRE])

    if stop == "vpre":
        P.op("sp", lambda e: e.dma_start(out=dbg_d[:, 0:8192], in_=vpre.rearrange("p k t -> p (k t)")), reads=[B_VPRE], dma=B_OUT)
        P.op("sp", lambda e: e.nop(), writes=[B_OUT])
        P.emit()
        return nc

    z_sb = Scr(0, 4096)
    mean_sb = Scr(4096, 4096)
    var_sb = Scr(8192, 4096)
    def ln_stats():
        vsq_ = [Scr(12288, 4096), Scr(16384, 4096)]
        for j in range(NJ):
            vs = vsq_[j % 2]
            P.op("act", lambda e, vs=vs, j=j: e.activation(out=vs.ap, in_=vpre[:, j, :], func=AF.Square),
                 reads=[B_VPRE], writes=vs.bufs)

            def fn(e, vs=vs, j=j):
                r = None
                for tt in range(2):
                    e.matmul(banks[tt][:, :], lhsT=ones32, rhs=vpre[:, j, tt * 512:(tt + 1) * 512], start=(j == 0), stop=(j == NJ - 1))
                    r = e.matmul(banks[2 + tt][:, :], lhsT=ones32, rhs=vs.ap[:, tt * 512:(tt + 1) * 512], start=(j == 0), stop=(j == NJ - 1))
                return r
            P.op("pe", fn, reads=[B_VPRE, B_ONES] + vs.bufs, writes=[PB[0], PB[1], PB[2], PB[3]])
        rr["i"] = 4
        for tt in range(2):
            sl_ = slice(tt * 512, (tt + 1) * 512)
            P.op("act", lambda e, tt=tt, sl_=sl_: e.activation(out=mean_sb.ap[:, sl_], in_=banks[tt][:, :], func=AF.Copy),
                 reads=[PB[tt]], writes=mean_sb.bufs)
            P.op("act", lambda e, tt=tt, sl_=sl_: e.activation(out=var_sb.ap[:, sl_], in_=banks[tt][:, :], func=AF.Square),
                 reads=[PB[tt]], writes=var_sb.bufs)
            P.op("dve", lambda e, tt=tt, sl_=sl_: e.tensor_tensor(out=var_sb.ap[:, sl_], in0=banks[2 + tt][:, :], in1=var_sb.ap[:, sl_],
                                                                  op=ALU.subtract),
                 reads=[PB[2 + tt]] + var_sb.bufs, writes=var_sb.bufs)
        P.op("act", lambda e: e.activation(out=var_sb.ap, in_=var_sb.ap, func=AF.Sqrt, bias=epsl, scale=1.0),
             reads=var_sb.bufs + [B_EPS], writes=var_sb.bufs)
        P.op("dve", lambda e: e.reciprocal(out=var_sb.ap, in_=var_sb.ap), reads=var_sb.bufs, writes=var_sb.bufs)
        P.op("dve", lambda e: e.scalar_tensor_tensor(out=mean_sb.ap, in0=mean_sb.ap, scalar=-1.0, in1=var_sb.ap,
                                                     op0=ALU.mult, op1=ALU.mult),
             reads=mean_sb.bufs + var_sb.bufs, writes=mean_sb.bufs)


    def ln_chunk(j):
        z = z_sb
        P.op("pool", lambda e: e.tensor_tensor(out=z.ap, in0=vpre[:, j, :], in1=var_sb.ap, op=ALU.mult),
             reads=[B_VPRE] + var_sb.bufs, writes=z.bufs)
        P.op("dve", lambda e: e.tensor_tensor(out=z.ap, in0=z.ap, in1=mean_sb.ap, op=ALU.add),
             reads=z.bufs + mean_sb.bufs, writes=z.bufs)
        P.op("act", lambda e: e.activation(out=vT[:, j, :], in_=z.ap, func=AF.Silu,
                                           bias=prm[:, P_LNB + j:P_LNB + j + 1], scale=prm[:, P_LNG + j:P_LNG + j + 1]),
             reads=z.bufs + [B_PRM], writes=[B_VT] + xs_guard("vT"))

    ah = Scr(12288, 4224)
    ab = Scr(16896, 4096)
    ca = Scr(20992, 2112, BF16)
    dd = Scr(23552, 768, BF16)
    d3a = dd.ap.rearrange("p (k c) -> p k c", k=3)
    id3a = ident.unsqueeze(1).to_broadcast([128, 3, 128])
    for j in range(NJ):
        cw3 = prm[:, P_CA + j * 3:P_CA + (j + 1) * 3].unsqueeze(2).to_broadcast([128, 3, 128])
        P.op("pool", lambda e, cw3=cw3: e.tensor_tensor(out=d3a, in0=id3a, in1=cw3, op=ALU.mult),
             reads=[B_ID, B_PRM], writes=dd.bufs)

        def cons_h(b, c0, n):
            P.op("act", lambda e: e.activation(out=ah.ap[:, c0:c0 + n], in_=banks[b][:, 0:n], func=AF.Copy),
                 reads=[PB[b]], writes=ah.bufs)
        inproj_ext("ah", j, cons_h)

        def cons_c(b, c0, n):
            P.op("dve", lambda e: e.tensor_tensor(out=ca.ap[:, c0:c0 + n], in0=banks[b][:, 0:n], in1=ah.ap[:, c0:c0 + n], op=ALU.mult),
                 reads=[PB[b]] + ah.bufs, writes=ca.bufs)
        inproj_ext("ac", j, cons_c)

        name, q = pos_in[("ab", j)]
        s, = use(name)
        lhs = w_in_chunk(s, q)
        for tt in range(2):
            b = next_bank()
            mm_group(banks[b][:, :], [(lhs[kc], nT1[:, kc, HALO + tt * 512:HALO + (tt + 1) * 512]) for kc in range(16)],
                     reads=[RS[s], B_NT], writes=[PB[b]])
            P.op("act", lambda e, b=b, tt=tt: e.activation(out=ab.ap[:, tt * 512:(tt + 1) * 512], in_=banks[b][:, :], func=AF.Copy),
                 reads=[PB[b]], writes=ab.bufs)

        for tt in range(2):
            b = next_bank()
            base = HALO + tt * 512 - 2
            mm_group(banks[b][:, :], [(d3a[:, k, :], ca.ap[:, base + k:base + k + 512]) for k in range(3)],
                     reads=dd.bufs + ca.bufs, writes=[PB[b]])
            P.op("dve", lambda e, b=b, j=j, tt=tt: e.tensor_tensor(out=yaT[:, j, tt * 512:(tt + 1) * 512], in0=banks[b][:, :],
                                                                 in1=ab.ap[:, tt * 512:(tt + 1) * 512], op=ALU.mult),
                 reads=[PB[b]] + ab.bufs, writes=[B_YA] + xs_guard("ya"))
        if j == 0:
            ln_stats()
        else:
            ln_chunk(j - 1)
    ln_chunk(NJ - 1)

    if stop == "vT":
        return dump_bf(vT, 2, 4, [B_VT])
    if stop == "ya":
        return dump_bf(yaT, 2, 4, [B_YA])

    sga_ = [Scr(0, 2048), Scr(2048, 2048)]
    sgb2_ = [Scr(4096, 2048), Scr(6144, 2048)]
    t1_ = [Scr(8192, 2048), Scr(10240, 2048)]
    t2_ = [Scr(12288, 2048), Scr(14336, 2048)]
    it = 0
    first_mt = True
    x0s = Scr(16384, 8192)
    P.op("sp", lambda e: e.dma_start(out=x0s.ap, in_=x_d[HALO:HALO + 128, :]), writes=x0s.bufs, dma=x0s.bufs[0])
    for c in range(16):
        s, = use(f"G{c}")
        woa = ring[s][:, 0:1024].rearrange("p (k c) -> p k c", k=8)
        wpw = ring[s][:, 1024:2048].rearrange("p (k c) -> p k c", k=8)
        wga = ring[s][:, 2048:4096].rearrange("p (k c) -> p k c", k=16)
        wgb = ring[s][:, 4096:6144].rearrange("p (k c) -> p k c", k=16)
        for tt in range(2):
            sga, sgb2, t1, t2 = sga_[it % 2], sgb2_[it % 2], t1_[it % 2], t2_[it % 2]
            it += 1
            ts_ = slice(tt * 512, (tt + 1) * 512)
            te_ = slice(HALO + tt * 512, HALO + (tt + 1) * 512)
            b_ga = next_bank()
            mm_group(banks[b_ga][:, :], [(wga[:, kc, :], nT1[:, kc, te_]) for kc in range(16)], reads=[RS[s], B_NT], writes=[PB[b_ga]])
            P.op("act", lambda e, b=b_ga, sga=sga: e.activation(out=sga.ap, in_=banks[b][:, :], func=AF.Sigmoid),
                 reads=[PB[b_ga]], writes=sga.bufs)
            b_ya = next_bank()
            mm_group(banks[b_ya][:, :], [(woa[:, kc, :], yaT[:, kc, ts_]) for kc in range(8)], reads=[RS[s], B_YA], writes=[PB[b_ya]])
            P.op("dve", lambda e, b=b_ya, sga=sga, t1=t1: e.tensor_tensor(out=t1.ap, in0=banks[b][:, :], in1=sga.ap, op=ALU.mult),
                 reads=[PB[b_ya]] + sga.bufs, writes=t1.bufs)
            b_gb = next_bank()
            mm_group(banks[b_gb][:, :], [(wgb[:, kc, :], nT1[:, kc, te_]) for kc in range(16)], reads=[RS[s], B_NT], writes=[PB[b_gb]])
            P.op("act", lambda e, b=b_gb, sgb2=sgb2: e.activation(out=sgb2.ap, in_=banks[b][:, :], func=AF.Sigmoid),
                 reads=[PB[b_gb]], writes=sgb2.bufs)
            b_yb = next_bank()
            mm_group(banks[b_yb][:, :], [(wpw[:, kc, :], vT[:, kc, ts_]) for kc in range(8)], reads=[RS[s], B_VT], writes=[PB[b_yb]])
            P.op("dve", lambda e, b=b_yb, sgb2=sgb2, t2=t2, c=c: e.scalar_tensor_tensor(
                out=t2.ap, in0=banks[b][:, :], scalar=prm[:, P_BPW + c:P_BPW + c + 1], in1=sgb2.ap, op0=ALU.add, op1=ALU.mult),
                reads=[PB[b_yb], B_PRM] + sgb2.bufs, writes=t2.bufs)
            extra = [B_VPRE] if first_mt else []
            first_mt = False
            P.op("pool", lambda e, t1=t1, t2=t2, c=c, ts_=ts_: e.tensor_tensor(out=mT[:, c, ts_], in0=t1.ap, in1=t2.ap, op=ALU.add),
                 reads=t1.bufs + t2.bufs, writes=B_TT[4 * tt:4 * tt + 4] + extra)

    if stop == "mT":
        return dump_bf(mT, 4, 4, B_TT)

    P.op("sp", lambda e: e.nop(), writes=[B_NT, B_YA, B_VT])
    for t in range(1, NT_TILES):
        P.op("sp", lambda e, t=t: e.dma_start(out=h[:, t, :], in_=x_d[HALO + t * 128:HALO + (t + 1) * 128, :]),
             writes=[B_HT[t]], dma=B_HT[t])

    ntc = ntb + [Scr(24576, 4096, BF16)]

    class NormPipe:
        def __init__(self, col0):
            self.col0 = col0
            self.q1 = []
            self.q2 = []

        def push(self, t):
            norm_p1(h[:, t, :], [B_HT[t]], self.col0 + t, 128, ntc[t % 3])
            if self.q2:
                self._tr(self.q2.pop(0))
            if self.q1:
                t1 = self.q1.pop(0)
                norm_p2(h[:, t1, :], [B_HT[t1]], self.col0 + t1, 128, ntc[t1 % 3])
                self.q2.append(t1)
            self.q1.append(t)

        def _tr(self, t):
            transposes(ntc[t % 3], [nT2[:, 0:8, t * 128:(t + 1) * 128], nT2[:, 8:16, t * 128:(t + 1) * 128]], [B_TT[t]],
                       engs=("act", "act"))

        def flush(self):
            while self.q1 or self.q2:
                if self.q2:
                    self._tr(self.q2.pop(0))
                if self.q1:
                    t1 = self.q1.pop(0)
                    norm_p2(h[:, t1, :], [B_HT[t1]], self.col0 + t1, 128, ntc[t1 % 3])
                    self.q2.append(t1)

    np2 = NormPipe(9)
    for n in range(4):
        s, = use(f"O{n}", after=([B_HT[7]] if n == 0 else ()))
        wv = ring[s].rearrange("p (k c) -> p k c", k=16)
        for t in range(NT_TILES):
            b = next_bank()
            mm_group(banks[b][:, :], [(mT[:, kc, t * 128:(t + 1) * 128], wv[:, kc, :]) for kc in range(16)],
                     reads=[RS[s], B_TT[t]], writes=[PB[b]])
            if t == 0:
                P.op("dve", lambda e, b=b, n=n: e.tensor_tensor(out=h[:, 0, n * 512:(n + 1) * 512], in0=x0s.ap[:, n * 512:(n + 1) * 512],
                                                              in1=banks[b][:, :], op=ALU.add),
                     reads=[PB[b]] + x0s.bufs, writes=[B_HT[0]])
            else:
                P.op("dve", lambda e, b=b, t=t, n=n: e.tensor_tensor(out=h[:, t, n * 512:(n + 1) * 512], in0=h[:, t, n * 512:(n + 1) * 512],
                                                                   in1=banks[b][:, :], op=ALU.add),
                     reads=[PB[b], B_HT[t]], writes=[B_HT[t]])
            if n == 3 and stop != "h1":
                np2.push(t)
    if stop == "h1":
        return dump_h()
    np2.flush()
    P.op("sp", lambda e: e.dma_start(out=gb, in_=g_d[2:3, :].partition_broadcast(128)), writes=[B_GB], dma=B_GB)

    fT_ = [Scr(0, 8192, BF16), Scr(8192, 8192, BF16)]
    sl_ = [Scr(24576, 2048), Scr(26624, 2048)]
    it = 0
    np3 = NormPipe(17)
    for blk in range(NFB):
        fT = fT_[blk % 2]
        f3 = fT.ap.rearrange("p (k t) -> p k t", k=4)
        sg_, su_ = use(f"FG{blk}", f"FU{blk}")
        wg = ring[sg_].rearrange("p (k c) -> p k c", k=16)
        wu = ring[su_].rearrange("p (k c) -> p k c", k=16)
        for cc in range(4):
            for tt in range(2):
                sl = sl_[it % 2]
                it += 1
                ts_ = slice(tt * 512, (tt + 1) * 512)
                bg = next_bank()
                mm_group(banks[bg][:, :], [(wg[:, kc, cc * 128:(cc + 1) * 128], nT2[:, kc, ts_]) for kc in range(16)],
                         reads=[RS[sg_]] + B_TT[4 * tt:4 * tt + 4], writes=[PB[bg]])
                P.op("act", lambda e, b=bg, sl=sl: e.activation(out=sl.ap, in_=banks[b][:, :], func=AF.Silu),
                     reads=[PB[bg]], writes=sl.bufs)
                bu = next_bank()
                mm_group(banks[bu][:, :], [(wu[:, kc, cc * 128:(cc + 1) * 128], nT2[:, kc, ts_]) for kc in range(16)],
                         reads=[RS[su_]] + B_TT[4 * tt:4 * tt + 4], writes=[PB[bu]])
                P.op("dve", lambda e, b=bu, sl=sl, f3=f3, cc=cc, ts_=ts_: e.tensor_tensor(out=f3[:, cc, ts_], in0=banks[b][:, :], in1=sl.ap, op=ALU.mult),
                     reads=[PB[bu]] + sl.bufs, writes=fT.bufs)
        sd_, = use(f"FD{blk}")
        wd = ring[sd_].rearrange("p (k c) -> p k c", k=4)
        for t in range(NT_TILES):
            for n in range(4):
                b = next_bank()
                mm_group(banks[b][:, :], [(f3[:, kc, t * 128:(t + 1) * 128], wd[:, kc, n * 512:(n + 1) * 512]) for kc in range(4)],
                         reads=[RS[sd_]] + fT.bufs, writes=[PB[b]])
                P.op("dve", lambda e, b=b, t=t, n=n: e.tensor_tensor(out=h[:, t, n * 512:(n + 1) * 512], in0=h[:, t, n * 512:(n + 1) * 512],
                                                                   in1=banks[b][:, :], op=ALU.add),
                     reads=[PB[b], B_HT[t]], writes=[B_HT[t]])
            if blk == NFB - 1 and stop != "h2":
                np3.push(t)
    if stop == "h2":
        return dump_h()
    np3.flush()
    P.op("sp", lambda e: e.dma_start(out=gb, in_=g_d[3:4, :].partition_broadcast(128)), writes=[B_GB], dma=B_GB)

    sgp_ = [Scr(0, 2048), Scr(2048, 2048)]
    tp_ = [Scr(4096, 2048), Scr(6144, 2048)]
    ost_ = [Scr(8192, 8192), Scr(16384, 8192)]
    fq1, fq2 = [], []
    fjunk = Scr(24576, 4096, BF16)

    def fin_p2(t):
        ost = ost_[t % 2]
        norm_p2(h[:, t, :], [B_HT[t]], 25 + t, 128, ost, out_ap=ost.ap)
        P.op("sp", lambda e, t=t, ost=ost: e.dma_start(out=out_d[t * 128:(t + 1) * 128, :], in_=ost.ap),
             reads=ost.bufs, dma=ost.bufs[0])

    def fin_push(t):
        if fq2:
            fin_p2(fq2.pop(0))
        if fq1:
            t1 = fq1.pop(0)
            norm_p1(h[:, t1, :], [B_HT[t1]], 25 + t1, 128, ost_[t1 % 2], out_ap=ost_[t1 % 2].ap, junk=fjunk)
            fq2.append(t1)
        fq1.append(t)

    def fin_flush():
        while fq1 or fq2:
            if fq2:
                fin_p2(fq2.pop(0))
            if fq1:
                t1 = fq1.pop(0)
                norm_p1(h[:, t1, :], [B_HT[t1]], 25 + t1, 128, ost_[t1 % 2], out_ap=ost_[t1 % 2].ap, junk=fjunk)
                fq2.append(t1)

    it = 0

    def ple_group(n, t, spp, s_):
        nonlocal it
        wpp = ring[spp][:, 0:1024].rearrange("p (k c) -> p k c", k=2)
        wv = ring[s_].rearrange("p (k c) -> p k c", k=16)
        sgp, tp = sgp_[it % 2], tp_[it % 2]
        it += 1
        bgt = next_bank()
        mm_group(banks[bgt][:, :], [(nT2[:, kc, t * 128:(t + 1) * 128], wv[:, kc, :]) for kc in range(16)],
                 reads=[RS[s_], B_TT[t]], writes=[PB[bgt]])
        P.op("act", lambda e, b=bgt, sgp=sgp: e.activation(out=sgp.ap, in_=banks[b][:, :], func=AF.Sigmoid),
             reads=[PB[bgt]], writes=sgp.bufs)
        bp = next_bank()
        mm_group(banks[bp][:, :], [(pT[:, ec, t * 128:(t + 1) * 128], wpp[:, ec, :]) for ec in range(2)],
                 reads=[RS[spp], B_PT], writes=[PB[bp]])
        P.op("dve", lambda e, b=bp, sgp=sgp, tp=tp: e.tensor_tensor(out=tp.ap, in0=banks[b][:, :], in1=sgp.ap, op=ALU.mult),
             reads=[PB[bp]] + sgp.bufs, writes=tp.bufs)
        P.op("dve" if n >= 2 else "pool",
             lambda e, tp=tp, t=t, n=n: e.tensor_tensor(out=h[:, t, n * 512:(n + 1) * 512], in0=h[:, t, n * 512:(n + 1) * 512],
                                                       in1=tp.ap, op=ALU.add),
             reads=tp.bufs + [B_HT[t]], writes=[B_HT[t]])

    for n in range(2):
        spp, s_ = use(f"PP{n}", f"PG{n}")
        for t in range(NT_TILES):
            ple_group(n, t, spp, s_)
    spp2, s2, spp3, s3 = use("PP2", "PG2", "PP3", "PG3")
    SKEW = 3
    for i in range(NT_TILES + SKEW):
        if i < NT_TILES:
            ple_group(2, i, spp2, s2)
        if i - SKEW >= 0:
            ple_group(3, i - SKEW, spp3, s3)
            fin_push(i - SKEW)
    fin_flush()
    P.op("sp", lambda e: e.nop(), writes=ost_[0].bufs + ost_[1].bufs)
    P.emit()
    return nc


_NC_CACHE = {}


def _get_nc(stop=None):
    if stop not in _NC_CACHE:
        _NC_CACHE[stop] = build(stop)
    return _NC_CACHE[stop]


def _prep_inputs(x, p, g_mix, w_in, conv_a_w, w_out_a, b_glu, conf_dw_w, conf_dw_b, conf_ln_g, conf_ln_b,
                 w_pw_b, b_pw_b, w_o, g_ffn, w_gate, w_up, w_down, g_ple, w_ple_gate, w_ple_proj, g_final):
    f = lambda a: np.asarray(a, np.float32)
    x2 = f(x).reshape(SEQ, D)
    p2 = f(p).reshape(SEQ, PLE)
    wflat = _prep_weights(f(w_in)[0], f(w_out_a)[0], f(w_pw_b)[0], f(w_o)[0], f(w_gate)[0], f(w_up)[0], f(w_down)[0],
                          f(w_ple_gate)[0], f(w_ple_proj)[0])
    prm = _prep_prm(f(conv_a_w)[0], f(b_glu)[0], f(conf_dw_w)[0], f(conf_dw_b)[0], f(conf_ln_g)[0], f(conf_ln_b)[0], f(b_pw_b)[0])
    gains = np.ascontiguousarray(np.stack([f(g_mix)[0], f(g_ffn)[0], f(g_ple)[0], f(g_final)]))
    xpad = np.concatenate([np.zeros((HALO, D), np.float32), x2], axis=0)
    in_maps = []
    for c in range(NCORES):
        in_maps.append({
            "x_ext": np.ascontiguousarray(xpad[c * T:c * T + TE]),
            "p_c": np.ascontiguousarray(p2[c * T:(c + 1) * T].reshape(NT_TILES, 128, PLE).transpose(1, 0, 2)).reshape(128, NT_TILES * PLE),
            "hmask": np.full((128, HALO), 0.0 if c == 0 else 1.0, np.float32),
            "wflat": wflat,
            "prm": prm,
            "gains": gains,
        })
    return in_maps


def kernel(**inputs):
    in_maps = _prep_inputs(**inputs)
    nc = _get_nc(None)
    res = run_bass_kernel_spmd(nc, in_maps, core_ids=list(range(NCORES)))
    out = np.concatenate([np.asarray(r["out"], np.float32) for r in res.results], axis=0)
    return out.reshape(1, SEQ, D)
```

```python
import numpy as np
import concourse.bass as bass
import concourse.mybir as mybir
from concourse.bass_utils import run_bass_kernel_spmd

F32 = mybir.dt.float32
BF16 = mybir.dt.bfloat16
AF = mybir.ActivationFunctionType
ALU = mybir.AluOpType

NCORES = 8
D = 2048
SEQ = 8192
T = SEQ // NCORES
HALO = 32
TE = T + HALO
NT_TILES = T // 128
KC = D // 128
CW = 1024
NJ = CW // 128
DFF = 5632
NFB = DFF // 512
PLE = 256
EPS = 1e-6
LN_EPS = 1e-5
SLOT = 8192
NSLOT = 4

COMPUTE = ("pe", "act", "dve", "pool")
ENGS = ("pe", "act", "dve", "pool", "sp")


class Buf:
    __slots__ = ("name", "writer", "readers", "sem", "cnt")

    def __init__(self, name):
        self.name = name
        self.writer = None
        self.readers = []
        self.sem = None
        self.cnt = 0


class Op:
    __slots__ = ("eng", "fn", "deps", "idx", "needed", "val", "dma_buf", "dma_val", "is_dma")


class Prog:
    def __init__(self, nc):
        self.nc = nc
        self.streams = {e: [] for e in ENGS}
        self.dma_bufs = []

    def op(self, eng, fn, reads=(), writes=(), dma=None):
        o = Op()
        o.eng = eng
        o.fn = fn
        o.needed = False
        o.val = None
        o.is_dma = dma is not None
        o.dma_buf = dma
        o.dma_val = None
        deps = []
        for b in reads:
            if b.writer is not None:
                deps.append(b.writer)
            b.readers.append(o)
        for b in writes:
            if b.writer is not None:
                deps.append(b.writer)
            deps.extend(r for r in b.readers if r is not o)
            b.writer = o
            b.readers = []
        if dma is not None:
            if dma.sem is None:
                self.dma_bufs.append(dma)
                dma.sem = True
            dma.cnt += 16
            o.dma_val = dma.cnt
        o.deps = [d for d in deps if d is not o and not (eng == "pe" and d.eng == "pe" and not d.is_dma)]
        o.idx = len(self.streams[eng])
        self.streams[eng].append(o)
        return o

    def emit(self):
        nc = self.nc
        for e in ENGS:
            for o in self.streams[e]:
                for d in o.deps:
                    if not d.is_dma:
                        d.needed = True
        for e in COMPUTE:
            c = 0
            for o in self.streams[e]:
                if o.needed and not o.is_dma:
                    c += 1
                    o.val = c
        sems = {e: nc.alloc_semaphore("S_" + e) for e in COMPUTE}
        for b in self.dma_bufs:
            b.sem = nc.alloc_semaphore("D_" + b.name)
        streams = self.streams

        def run(e, eng):
            waited = {}
            for o in streams[e]:
                w = {}
                for d in o.deps:
                    if d.is_dma:
                        s, v = d.dma_buf.sem, d.dma_val
                    else:
                        s, v = sems[d.eng], d.val
                    k = id(s)
                    if k not in w or w[k][1] < v:
                        w[k] = (s, v)
                for k, (s, v) in w.items():
                    if waited.get(k, 0) >= v:
                        continue
                    waited[k] = v
                    eng.wait_ge(s, v)
                ins = o.fn(eng)
                if o.is_dma:
                    ins.then_inc(o.dma_buf.sem, 16)
                elif o.needed:
                    ins.then_inc(sems[e], 1)

        with nc.Block() as block:
            @block.tensor
            def _(eng):
                run("pe", eng)

            @block.scalar
            def _(eng):
                run("act", eng)

            @block.vector
            def _(eng):
                run("dve", eng)

            @block.gpsimd
            def _(eng):
                run("pool", eng)

            @block.sync
            def _(eng):
                run("sp", eng)


def _kc(w):
    k, c = w.shape
    return np.ascontiguousarray(w.reshape(k // 128, 128, c).transpose(1, 0, 2)).reshape(128, -1)


def _ch(group, j):
    base = {"ah": 0, "ab": 8, "ac": 16, "gv": 24, "gg": 32, "ga": 40, "gb": 56}[group]
    return base + j


B_ORDER = []
for _i in range(0, NJ, 2):
    B_ORDER += [("gg", _i), ("gv", _i), ("gg", _i + 1), ("gv", _i + 1)]
A_ORDER = []
for _j in range(NJ):
    A_ORDER += [("ah", _j), ("ac", _j), ("ab", _j)]


def _load_plan():
    plan = []
    for i in range(4):
        plan.append((f"B{i}", 16 * 512))
    for i in range(6):
        plan.append((f"A{i}", 16 * 512))
    for c in range(16):
        plan.append((f"G{c}", 6144))
    for n in range(4):
        plan.append((f"O{n}", 16 * 512))
    for b in range(NFB):
        plan.append((f"FG{b}", 16 * 512))
        plan.append((f"FU{b}", 16 * 512))
        plan.append((f"FD{b}", 4 * 2048))
    for n in range(4):
        plan.append((f"PP{n}", 2 * 512))
        plan.append((f"PG{n}", 16 * 512))
    return plan


PLAN = _load_plan()
WTOT = sum(n for _, n in PLAN)


def _prep_weights(w_in, w_out_a, w_pw_b, w_o, w_gate, w_up, w_down, w_ple_gate, w_ple_proj):
    wflat = np.empty((128, WTOT), np.float32)
    pos = 0

    def put(a):
        nonlocal pos
        n = a.shape[1]
        wflat[:, pos:pos + n] = a
        pos += n

    def cols(chunks):
        idx = np.concatenate([np.arange(_ch(g, j) * 128, _ch(g, j) * 128 + 128) for g, j in chunks])
        return _kc(w_in[:, idx])

    for i in range(4):
        put(cols(B_ORDER[4 * i:4 * i + 4]))
    for i in range(6):
        put(cols(A_ORDER[4 * i:4 * i + 4]))
    for c in range(16):
        put(_kc(w_out_a[:, c * 128:(c + 1) * 128]))
        put(_kc(w_pw_b[:, c * 128:(c + 1) * 128]))
        put(cols([("ga", c)]))
        put(cols([("gb", c)]))
    for n in range(4):
        put(_kc(w_o[:, n * 512:(n + 1) * 512]))
    for b in range(NFB):
        put(_kc(w_gate[:, b * 512:(b + 1) * 512]))
        put(_kc(w_up[:, b * 512:(b + 1) * 512]))
        put(_kc(w_down[b * 512:(b + 1) * 512, :]))
    for n in range(4):
        put(_kc(w_ple_proj[:, n * 512:(n + 1) * 512]))
        put(_kc(w_ple_gate[:, n * 512:(n + 1) * 512]))
    assert pos == WTOT
    return wflat


P_BV, P_BG, P_CA, P_CW, P_DWB, P_LNG, P_LNB, P_BPW = 0, 8, 16, 40, 288, 296, 304, 312
NPRM = 328


def _fm(v, n):
    return np.ascontiguousarray(np.asarray(v, np.float32).reshape(n, 128).T)


def _prep_prm(conv_a_w, b_glu, conf_dw_w, conf_dw_b, conf_ln_g, conf_ln_b, b_pw_b):
    prm = np.zeros((128, NPRM), np.float32)
    prm[:, P_BV:P_BV + 8] = _fm(b_glu[:CW], 8)
    prm[:, P_BG:P_BG + 8] = _fm(b_glu[CW:], 8)
    prm[:, P_CA:P_CA + 24] = np.ascontiguousarray(conv_a_w.reshape(3, 8, 128).transpose(2, 1, 0)).reshape(128, 24)
    prm[:, P_CW:P_CW + 248] = np.ascontiguousarray(conf_dw_w.reshape(31, 8, 128).transpose(2, 1, 0)).reshape(128, 248)
    prm[:, P_DWB:P_DWB + 8] = _fm(conf_dw_b, 8)
    prm[:, P_LNG:P_LNG + 8] = _fm(conf_ln_g, 8)
    prm[:, P_LNB:P_LNB + 8] = _fm(conf_ln_b, 8)
    prm[:, P_BPW:P_BPW + 16] = _fm(b_pw_b, 16)
    return prm


def build(stop=None):
    nc = bass.Bass("TRN2", target_bir_lowering=False)
    x_d = nc.dram_tensor("x_ext", [TE, D], F32, kind="ExternalInput").ap()
    p_d = nc.dram_tensor("p_c", [128, NT_TILES * PLE], F32, kind="ExternalInput").ap()
    hm_d = nc.dram_tensor("hmask", [128, HALO], F32, kind="ExternalInput").ap()
    w_d = nc.dram_tensor("wflat", [128, WTOT], F32, kind="ExternalInput").ap()
    prm_d = nc.dram_tensor("prm", [128, NPRM], F32, kind="ExternalInput").ap()
    g_d = nc.dram_tensor("gains", [4, D], F32, kind="ExternalInput").ap()
    out_d = nc.dram_tensor("out", [T, D], F32, kind="ExternalOutput").ap()
    dbg_d = None
    if stop is not None:
        dbg_d = nc.dram_tensor("dbg", [128, 16384], F32, kind="ExternalOutput").ap()

    NB4 = 52992
    big = nc.alloc_sbuf_tensor("big", [128, NB4], F32)

    def reg(off, nbytes, dt=F32):
        assert off % 4 == 0 and nbytes % 4 == 0 and (off + nbytes) <= NB4 * 4, (off, nbytes)
        v = big[:, off // 4:(off + nbytes) // 4]
        return v.bitcast(BF16) if dt == BF16 else v

    ring = [reg(i * 16384, 16384, BF16) for i in range(NSLOT)]
    O_NT, O_YA, O_VT, O_H, O_VPRE, O_GB, O_SM, O_SC = 65536, 99328, 115712, 65536, 132096, 164864, 173056, 181248
    SC_SIZE = NB4 * 4 - O_SC
    nT1 = reg(O_NT, 16 * TE * 2, BF16).rearrange("p (k t) -> p k t", k=16)
    yaT = reg(O_YA, 16384, BF16).rearrange("p (k t) -> p k t", k=8)
    vT = reg(O_VT, 16384, BF16).rearrange("p (k t) -> p k t", k=8)
    xs = reg(O_YA, 65536).rearrange("p (t d) -> p t d", t=8)
    h = reg(O_H, 65536).rearrange("p (t d) -> p t d", t=8)
    vpre = reg(O_VPRE, 32768).rearrange("p (k t) -> p k t", k=8)
    mT = reg(O_VPRE, 32768, BF16).rearrange("p (k t) -> p k t", k=16)
    nT2 = mT
    gb = reg(O_GB, 8192)
    prm = reg(O_SM, 1312)
    hmask = reg(O_SM + 1344, 128)
    ident = reg(O_SM + 1472, 256, BF16)
    ones32 = reg(O_SM + 1728, 512)
    ssq = reg(O_SM + 2240, 160)
    rstd = reg(O_SM + 2400, 160)
    stmp = reg(O_SM + 2560, 160)
    epsr = reg(O_SM + 2720, 4)
    epsl = reg(O_SM + 2724, 4)
    identf = reg(O_SM + 2752, 512)
    pT = reg(O_SM + 3328, 4096, BF16).rearrange("p (k t) -> p k t", k=2)
    mhalf1 = reg(O_SM + 7424, 4)

    banks = [nc.alloc_psum_tensor(f"bank{i}", [128, 512], F32) for i in range(8)]
    banks_bf = [b[:].bitcast(BF16) for b in banks]

    P = Prog(nc)
    RS = [Buf(f"ring{i}") for i in range(NSLOT)]
    PB = [Buf(f"pb{i}") for i in range(8)]
    PAGE = 512
    SCP = [Buf(f"sc{i}") for i in range(SC_SIZE // PAGE + 1)]
    B_NT, B_YA, B_VT, B_VPRE, B_GB = Buf("nT"), Buf("ya"), Buf("vT"), Buf("vpre"), Buf("gb")
    B_TT = [Buf(f"tt{t}") for t in range(8)]
    B_HT = [Buf(f"h{t}") for t in range(8)]
    B_XS = [Buf(f"xs{t}") for t in range(8)]
    B_SS = [Buf(f"ss{i}") for i in range(40)]
    B_PRM, B_CONST, B_PT = Buf("prm"), Buf("const"), Buf("pT")
    B_OUT = Buf("outst")
    B_ID, B_IDF, B_ONES, B_EPS = Buf("ident"), Buf("identf"), Buf("ones"), Buf("eps")

    class Scr:
        def __init__(self, off, nbytes, dt=F32):
            assert off + nbytes <= SC_SIZE, (off, nbytes, SC_SIZE)
            self.ap = reg(O_SC + off, nbytes, dt)
            self.bufs = SCP[off // PAGE:(off + nbytes - 1) // PAGE + 1]

    wstate = {"next": 0, "off": 0}
    loaded = {}

    def issue_load(after=()):
        i = wstate["next"]
        if i >= len(PLAN):
            return
        name, n = PLAN[i]
        s = i % NSLOT
        off = wstate["off"]
        P.op("pool", lambda e, s=s, off=off, n=n: e.dma_start(out=ring[s][:, 0:n], in_=w_d[:, off:off + n]),
             reads=list(after), writes=[RS[s]], dma=RS[s])
        loaded[name] = s
        wstate["next"] = i + 1
        wstate["off"] = off + n

    PIDX = {nm: i for i, (nm, _) in enumerate(PLAN)}

    def use(*names, after=()):
        k0 = min(PIDX[n] for n in names)
        while wstate["next"] < len(PLAN) and wstate["next"] <= k0 + NSLOT - 1:
            issue_load(after)
        return [loaded[n] for n in names]

    rr = {"i": 0, "t": 0}

    def next_bank():
        b = rr["i"] % 6
        rr["i"] += 1
        return b

    def next_tbank():
        b = 6 + rr["t"] % 2
        rr["t"] += 1
        return b

    def mm_group(out_ap, pairs, reads, writes):
        def fn(e):
            n = len(pairs)
            r = None
            for i, (l, rh) in enumerate(pairs):
                r = e.matmul(out_ap, lhsT=l, rhs=rh, start=(i == 0), stop=(i == n - 1))
            return r
        return P.op("pe", fn, reads=reads, writes=writes)

    def dump_bf(view3, nq, kq, reads):
        cv = Scr(0, 16384)
        for q in range(nq):
            P.op("act", lambda e, q=q: e.activation(out=cv.ap.rearrange("p (k t) -> p k t", k=4),
                                                    in_=view3[:, 4 * q:4 * q + 4, :], func=AF.Copy),
                 reads=reads, writes=cv.bufs)
            P.op("sp", lambda e, q=q: e.dma_start(out=dbg_d[:, q * 4096:(q + 1) * 4096], in_=cv.ap), reads=cv.bufs, dma=B_OUT)
        P.op("sp", lambda e: e.nop(), writes=[B_OUT] + cv.bufs)
        P.emit()
        return nc

    def dump_h():
        for t in range(8):
            P.op("sp", lambda e, t=t: e.dma_start(out=dbg_d[:, t * 2048:(t + 1) * 2048], in_=h[:, t, :]), reads=[B_HT[t]], dma=B_OUT)
        P.op("sp", lambda e: e.nop(), writes=[B_OUT])
        P.emit()
        return nc

    pst = Scr(8192, 8192)
    P.op("sp", lambda e: e.dma_start(out=pst.ap, in_=p_d), writes=pst.bufs, dma=pst.bufs[0])
    xh = Scr(0, 8192)
    P.op("sp", lambda e: e.dma_start(out=xh.ap[0:HALO, :], in_=x_d[0:HALO, :]), writes=xh.bufs, dma=xh.bufs[0])
    P.op("sp", lambda e: e.dma_start(out=gb, in_=g_d[0:1, :].partition_broadcast(128)), writes=[B_GB], dma=B_GB)
    for t in range(NT_TILES):
        P.op("sp", lambda e, t=t: e.dma_start(out=xs[:, t, :], in_=x_d[HALO + t * 128:HALO + (t + 1) * 128, :]),
             writes=[B_XS[t]], dma=B_XS[t])
    P.op("sp", lambda e: e.dma_start(out=prm, in_=prm_d), writes=[B_PRM], dma=B_PRM)
    P.op("sp", lambda e: e.dma_start(out=hmask, in_=hm_d), writes=[B_CONST], dma=B_CONST)

    P.op("pool", lambda e: e.memset(identf, 0.0), writes=[B_IDF])
    P.op("pool", lambda e: e.affine_select(out=identf, in_=identf, pattern=[[-1, 128]], compare_op=ALU.not_equal,
                                           fill=1.0, base=0, channel_multiplier=1), reads=[B_IDF], writes=[B_IDF])
    P.op("pool", lambda e: e.memset(ones32, 1.0 / CW), writes=[B_ONES])

    def mk_eps(e):
        e.memset(epsr, EPS)
        e.memset(mhalf1, -0.5)
        return e.memset(epsl, LN_EPS)
    P.op("pool", mk_eps, writes=[B_EPS])
    P.op("pool", lambda e: e.memset(ssq, 0.0), writes=B_SS)
    P.op("dve", lambda e: e.tensor_copy(out=ident, in_=identf), reads=[B_IDF], writes=[B_ID])


    ntb = [Scr(16384, 4096, BF16), Scr(20480, 4096, BF16)]

    def norm_p1(src_ap, src_bufs, col, npart, ntile, out_ap=None, junk=None):
        dst = ntile.ap[0:npart, :] if out_ap is None else out_ap
        jb = ntile.bufs
        if junk is not None:
            dst, jb = junk.ap, junk.bufs
        sb = B_SS[col]
        P.op("act", lambda e: e.activation(out=dst, in_=src_ap, func=AF.Square, accum_out=ssq[0:npart, col:col + 1]),
             reads=src_bufs, writes=jb + [sb])
        P.op("pool", lambda e: e.tensor_scalar(out=stmp[0:npart, col:col + 1], in0=ssq[0:npart, col:col + 1],
                                              scalar1=1.0 / D, scalar2=EPS, op0=ALU.mult, op1=ALU.add),
             reads=[sb], writes=[sb])
        P.op("pool", lambda e: e.tensor_tensor(out=rstd[0:npart, col:col + 1], in0=stmp[0:npart, col:col + 1],
                                               in1=mhalf1[0:npart, :], op=ALU.pow),
             reads=[sb, B_EPS], writes=[sb])

    def norm_p2(src_ap, src_bufs, col, npart, ntile, out_ap=None):
        dst = ntile.ap[0:npart, :] if out_ap is None else out_ap
        sb = B_SS[col]
        P.op("dve", lambda e: e.scalar_tensor_tensor(out=dst, in0=src_ap, scalar=rstd[0:npart, col:col + 1],
                                                     in1=gb[0:npart, :], op0=ALU.mult, op1=ALU.mult),
             reads=src_bufs + [sb, B_GB], writes=ntile.bufs)

    def norm_ops(src_ap, src_bufs, col, npart, ntile, out_ap=None):
        norm_p1(src_ap, src_bufs, col, npart, ntile, out_ap)
        norm_p2(src_ap, src_bufs, col, npart, ntile, out_ap)

    def transposes(ntile, dst_views, dst_bufs, engs=("act", "dve")):
        for half in range(2):
            tb = next_tbank()

            def fn(e, half=half, tb=tb):
                r = None
                for i in range(8):
                    kc = half * 8 + i
                    r = e.transpose(out=banks_bf[tb][:, i * 128:(i + 1) * 128],
                                    in_=ntile.ap[:, kc * 128:(kc + 1) * 128], identity=ident)
                return r
            P.op("pe", fn, reads=ntile.bufs + [B_ID], writes=[PB[tb]])
            src = banks_bf[tb].rearrange("p (k t) -> p k t", k=8)
            dst = dst_views[half]
            if engs[half] == "act":
                P.op("act", lambda e, dst=dst, src=src: e.activation(out=dst, in_=src, func=AF.Copy),
                     reads=[PB[tb]], writes=dst_bufs)
            else:
                P.op("dve", lambda e, dst=dst, src=src: e.tensor_copy(out=dst, in_=src),
                     reads=[PB[tb]], writes=dst_bufs)

    nth = Scr(24576, 4096, BF16)
    norm_ops(xh.ap[0:HALO, :], xh.bufs, 8, HALO, nth)
    tb = next_tbank()

    def fn_h(e, tb=tb):
        r = None
        for kc in range(16):
            r = e.transpose(out=banks_bf[tb][:, kc * HALO:(kc + 1) * HALO],
                            in_=nth.ap[0:HALO, kc * 128:(kc + 1) * 128], identity=ident[0:HALO, 0:HALO])
        return r
    P.op("pe", fn_h, reads=nth.bufs + [B_ID], writes=[PB[tb]])
    src_h = banks_bf[tb][:, 0:16 * HALO].rearrange("p (k t) -> p k t", k=16)
    P.op("act", lambda e: e.activation(out=nT1[:, :, 0:HALO], in_=src_h, func=AF.Copy), reads=[PB[tb]], writes=[B_NT])
    ntb3 = ntb + [Scr(24576, 4096, BF16)]

    def tr0(t):
        c0 = HALO + t * 128
        transposes(ntb3[t % 3], [nT1[:, 0:8, c0:c0 + 128], nT1[:, 8:16, c0:c0 + 128]], [B_NT])
    for t in range(NT_TILES + 2):
        if t < NT_TILES:
            norm_p1(xs[:, t, :], [B_XS[t]], t, 128, ntb3[t % 3])
        if 3 <= t < 3 + NSLOT:
            issue_load()
        if 0 <= t - 1 < NT_TILES:
            norm_p2(xs[:, t - 1, :], [B_XS[t - 1]], t - 1, 128, ntb3[(t - 1) % 3])
        if 0 <= t - 2 < NT_TILES:
            tr0(t - 2)

    pbf = Scr(0, 4096, BF16)
    P.op("dve", lambda e: e.tensor_copy(out=pbf.ap, in_=pst.ap), reads=pst.bufs, writes=pbf.bufs)
    for half in range(2):
        tb = next_tbank()

        def fn(e, half=half, tb=tb):
            r = None
            for i in range(8):
                q = half * 8 + i
                r = e.transpose(out=banks_bf[tb][:, i * 128:(i + 1) * 128],
                                in_=pbf.ap[:, q * 128:(q + 1) * 128], identity=ident)
            return r
        P.op("pe", fn, reads=pbf.bufs + [B_ID], writes=[PB[tb]])
        src = banks_bf[tb].rearrange("p (t c k) -> p c t k", t=4, c=2)
        dst = pT[:, :, half * 512:(half + 1) * 512].rearrange("p c (t k) -> p c t k", t=4)
        P.op("act", lambda e, dst=dst, src=src: e.activation(out=dst, in_=src, func=AF.Copy),
             reads=[PB[tb]], writes=[B_PT])

    P.op("sp", lambda e: e.dma_start(out=gb, in_=g_d[1:2, :].partition_broadcast(128)), writes=[B_GB], dma=B_GB)

    if stop == "n1T":
        return dump_bf(nT1[:, :, HALO:TE], 4, 4, [B_NT])

    TT3 = [(i * 352, 352) for i in range(3)]
    KD = 10
    NPE = 31 - KD
    sg1 = Scr(0, 4224)
    ub_ = [Scr(4608, 2112, BF16), Scr(7168, 2112, BF16)]
    DGS = -(-NPE * 256 // 512) * 512
    dg_ = [Scr(9728, NPE * 256, BF16), Scr(9728 + DGS, NPE * 256, BF16)]
    acc1 = Scr(9728 + 2 * DGS, 4096)

    def w_in_chunk(slot_i, q):
        v = ring[slot_i].rearrange("p (k c) -> p k c", k=16)
        return [v[:, kc, q * 128:(q + 1) * 128] for kc in range(16)]

    pos_in = {}
    for i, ch in enumerate(B_ORDER):
        pos_in[ch] = (f"B{i // 4}", i % 4)
    for i, ch in enumerate(A_ORDER):
        pos_in[ch] = (f"A{i // 4}", i % 4)

    def inproj_ext(group, j, consume):
        name, q = pos_in[(group, j)]
        s, = use(name)
        lhs = w_in_chunk(s, q)
        for (c0, n) in TT3:
            b = next_bank()
            mm_group(banks[b][:, 0:n], [(lhs[kc], nT1[:, kc, c0:c0 + n]) for kc in range(16)],
                     reads=[RS[s], B_NT], writes=[PB[b]])
            consume(b, c0, n)

    first = {"vpre": True, "ya": True, "vT": True}

    def xs_guard(key):
        if first[key]:
            first[key] = False
            return list(B_XS)
        return []

    for j in range(NJ):
        sg, ub, dg = sg1, ub_[j % 2], dg_[j % 2]
        d3 = dg.ap.rearrange("p (k c) -> p k c", k=NPE)
        id3 = ident.unsqueeze(1).to_broadcast([128, NPE, 128])
        cw3 = prm[:, P_CW + j * 31 + KD:P_CW + (j + 1) * 31].unsqueeze(2).to_broadcast([128, NPE, 128])
        P.op("pool", lambda e, d3=d3, cw3=cw3, id3=id3: e.tensor_tensor(out=d3, in0=id3, in1=cw3, op=ALU.mult),
             reads=[B_ID, B_PRM], writes=dg.bufs)

        def cons_g(b, c0, n, sg=sg, j=j):
            P.op("act", lambda e: e.activation(out=sg.ap[:, c0:c0 + n], in_=banks[b][:, 0:n], func=AF.Sigmoid,
                                               bias=prm[:, P_BG + j:P_BG + j + 1]),
                 reads=[PB[b], B_PRM], writes=sg.bufs)
        inproj_ext("gg", j, cons_g)

        def cons_v(b, c0, n, sg=sg, ub=ub, j=j):
            P.op("dve", lambda e: e.scalar_tensor_tensor(out=ub.ap[:, c0:c0 + n], in0=banks[b][:, 0:n],
                                                         scalar=prm[:, P_BV + j:P_BV + j + 1], in1=sg.ap[:, c0:c0 + n],
                                                         op0=ALU.add, op1=ALU.mult),
                 reads=[PB[b], B_PRM] + sg.bufs, writes=ub.bufs)
            if c0 == 0:
                P.op("dve", lambda e: e.tensor_tensor(out=ub.ap[:, 0:HALO], in0=ub.ap[:, 0:HALO], in1=hmask, op=ALU.mult),
                     reads=ub.bufs + [B_CONST], writes=ub.bufs)
        inproj_ext("gv", j, cons_v)

        cbanks = []
        for tt in range(2):
            b = next_bank()
            cbanks.append(b)
            base = HALO + tt * 512 - 30
            mm_group(banks[b][:, :], [(d3[:, k - KD, :], ub.ap[:, base + k:base + k + 512]) for k in range(KD, 31)],
                     reads=dg.bufs + ub.bufs, writes=[PB[b]])
        for k in range(KD):
            src = ub.ap[:, HALO - 30 + k:HALO - 30 + k + T]
            wk = prm[:, P_CW + j * 31 + k:P_CW + j * 31 + k + 1]
            if k == 0:
                P.op("dve", lambda e, src=src, wk=wk: e.tensor_scalar(out=acc1.ap, in0=src, scalar1=wk, scalar2=None, op0=ALU.mult),
                     reads=ub.bufs + [B_PRM], writes=acc1.bufs)
            else:
                P.op("dve", lambda e, src=src, wk=wk: e.scalar_tensor_tensor(out=acc1.ap, in0=src, scalar=wk, in1=acc1.ap,
                                                                           op0=ALU.mult, op1=ALU.add),
                     reads=ub.bufs + [B_PRM] + acc1.bufs, writes=acc1.bufs)
        for tt in range(2):
            b = cbanks[tt]
            P.op("act", lambda e, b=b, j=j, tt=tt: e.activation(out=vpre[:, j, tt * 512:(tt + 1) * 512], in_=banks[b][:, :],
                                                               func=AF.Identity, bias=prm[:, P_DWB + j:P_DWB + j + 1]),
                 reads=[PB[b], B_PRM], writes=[B_VPRE] + xs_guard("vpre"))
        P.op("pool", lambda e, j=j: e.tensor_tensor(out=vpre[:, j, :], in0=vpre[:, j, :], in1=acc1.ap, op=ALU.add),
             reads=acc1.bufs + [B_VPRE], writes=[B_VPRE])

    if stop == "vpre":
        P.op("sp", lambda e: e.dma_start(out=dbg_d[:, 0:8192], in_=vpre.rearrange("p k t -> p (k t)")), reads=[B_VPRE], dma=B_OUT)
        P.op("sp", lambda e: e.nop(), writes=[B_OUT])
        P.emit()
        return nc

    z_sb = Scr(0, 4096)
    mean_sb = Scr(4096, 4096)
    var_sb = Scr(8192, 4096)
    def ln_stats():
        vsq_ = [Scr(12288, 4096), Scr(16384, 4096)]
        for j in range(NJ):
            vs = vsq_[j % 2]
            P.op("act", lambda e, vs=vs, j=j: e.activation(out=vs.ap, in_=vpre[:, j, :], func=AF.Square),
                 reads=[B_VPRE], writes=vs.bufs)

            def fn(e, vs=vs, j=j):
                r = None
                for tt in range(2):
                    e.matmul(banks[tt][:, :], lhsT=ones32, rhs=vpre[:, j, tt * 512:(tt + 1) * 512], start=(j == 0), stop=(j == NJ - 1))
                    r = e.matmul(banks[2 + tt][:, :], lhsT=ones32, rhs=vs.ap[:, tt * 512:(tt + 1) * 512], start=(j == 0), stop=(j == NJ - 1))
                return r
            P.op("pe", fn, reads=[B_VPRE, B_ONES] + vs.bufs, writes=[PB[0], PB[1], PB[2], PB[3]])
        rr["i"] = 4
        for tt in range(2):
            sl_ = slice(tt * 512, (tt + 1) * 512)
            P.op("act", lambda e, tt=tt, sl_=sl_: e.activation(out=mean_sb.ap[:, sl_], in_=banks[tt][:, :], func=AF.Copy),
                 reads=[PB[tt]], writes=mean_sb.bufs)
            P.op("act", lambda e, tt=tt, sl_=sl_: e.activation(out=var_sb.ap[:, sl_], in_=banks[tt][:, :], func=AF.Square),
                 reads=[PB[tt]], writes=var_sb.bufs)
            P.op("dve", lambda e, tt=tt, sl_=sl_: e.tensor_tensor(out=var_sb.ap[:, sl_], in0=banks[2 + tt][:, :], in1=var_sb.ap[:, sl_],
                                                                  op=ALU.subtract),
                 reads=[PB[2 + tt]] + var_sb.bufs, writes=var_sb.bufs)
        P.op("act", lambda e: e.activation(out=var_sb.ap, in_=var_sb.ap, func=AF.Sqrt, bias=epsl, scale=1.0),
             reads=var_sb.bufs + [B_EPS], writes=var_sb.bufs)
        P.op("dve", lambda e: e.reciprocal(out=var_sb.ap, in_=var_sb.ap), reads=var_sb.bufs, writes=var_sb.bufs)
        P.op("dve", lambda e: e.scalar_tensor_tensor(out=mean_sb.ap, in0=mean_sb.ap, scalar=-1.0, in1=var_sb.ap,
                                                     op0=ALU.mult, op1=ALU.mult),
             reads=mean_sb.bufs + var_sb.bufs, writes=mean_sb.bufs)


    def ln_chunk(j):
        z = z_sb
        P.op("pool", lambda e: e.tensor_tensor(out=z.ap, in0=vpre[:, j, :], in1=var_sb.ap, op=ALU.mult),
             reads=[B_VPRE] + var_sb.bufs, writes=z.bufs)
        P.op("dve", lambda e: e.tensor_tensor(out=z.ap, in0=z.ap, in1=mean_sb.ap, op=ALU.add),
             reads=z.bufs + mean_sb.bufs, writes=z.bufs)
        P.op("act", lambda e: e.activation(out=vT[:, j, :], in_=z.ap, func=AF.Silu,
                                           bias=prm[:, P_LNB + j:P_LNB + j + 1], scale=prm[:, P_LNG + j:P_LNG + j + 1]),
             reads=z.bufs + [B_PRM], writes=[B_VT] + xs_guard("vT"))

    ah = Scr(12288, 4224)
    ab = Scr(16896, 4096)
    ca = Scr(20992, 2112, BF16)
    dd = Scr(23552, 768, BF16)
    d3a = dd.ap.rearrange("p (k c) -> p k c", k=3)
    id3a = ident.unsqueeze(1).to_broadcast([128, 3, 128])
    for j in range(NJ):
        cw3 = prm[:, P_CA + j * 3:P_CA + (j + 1) * 3].unsqueeze(2).to_broadcast([128, 3, 128])
        P.op("pool", lambda e, cw3=cw3: e.tensor_tensor(out=d3a, in0=id3a, in1=cw3, op=ALU.mult),
             reads=[B_ID, B_PRM], writes=dd.bufs)

        def cons_h(b, c0, n):
            P.op("act", lambda e: e.activation(out=ah.ap[:, c0:c0 + n], in_=banks[b][:, 0:n], func=AF.Copy),
                 reads=[PB[b]], writes=ah.bufs)
        inproj_ext("ah", j, cons_h)

        def cons_c(b, c0, n):
            P.op("dve", lambda e: e.tensor_tensor(out=ca.ap[:, c0:c0 + n], in0=banks[b][:, 0:n], in1=ah.ap[:, c0:c0 + n], op=ALU.mult),
                 reads=[PB[b]] + ah.bufs, writes=ca.bufs)
        inproj_ext("ac", j, cons_c)

        name, q = pos_in[("ab", j)]
        s, = use(name)
        lhs = w_in_chunk(s, q)
        for tt in range(2):
            b = next_bank()
            mm_group(banks[b][:, :], [(lhs[kc], nT1[:, kc, HALO + tt * 512:HALO + (tt + 1) * 512]) for kc in range(16)],
                     reads=[RS[s], B_NT], writes=[PB[b]])
            P.op("act", lambda e, b=b, tt=tt: e.activation(out=ab.ap[:, tt * 512:(tt + 1) * 512], in_=banks[b][:, :], func=AF.Copy),
                 reads=[PB[b]], writes=ab.bufs)

        for tt in range(2):
            b = next_bank()
            base = HALO + tt * 512 - 2
            mm_group(banks[b][:, :], [(d3a[:, k, :], ca.ap[:, base + k:base + k + 512]) for k in range(3)],
                     reads=dd.bufs + ca.bufs, writes=[PB[b]])
            P.op("dve", lambda e, b=b, j=j, tt=tt: e.tensor_tensor(out=yaT[:, j, tt * 512:(tt + 1) * 512], in0=banks[b][:, :],
                                                                 in1=ab.ap[:, tt * 512:(tt + 1) * 512], op=ALU.mult),
                 reads=[PB[b]] + ab.bufs, writes=[B_YA] + xs_guard("ya"))
        if j == 0:
            ln_stats()
        else:
            ln_chunk(j - 1)
    ln_chunk(NJ - 1)

    if stop == "vT":
        return dump_bf(vT, 2, 4, [B_VT])
    if stop == "ya":
        return dump_bf(yaT, 2, 4, [B_YA])

    sga_ = [Scr(0, 2048), Scr(2048, 2048)]
    sgb2_ = [Scr(4096, 2048), Scr(6144, 2048)]
    t1_ = [Scr(8192, 2048), Scr(10240, 2048)]
    t2_ = [Scr(12288, 2048), Scr(14336, 2048)]
    it = 0
    first_mt = True
    x0s = Scr(16384, 8192)
    P.op("sp", lambda e: e.dma_start(out=x0s.ap, in_=x_d[HALO:HALO + 128, :]), writes=x0s.bufs, dma=x0s.bufs[0])
    for c in range(16):
        s, = use(f"G{c}")
        woa = ring[s][:, 0:1024].rearrange("p (k c) -> p k c", k=8)
        wpw = ring[s][:, 1024:2048].rearrange("p (k c) -> p k c", k=8)
        wga = ring[s][:, 2048:4096].rearrange("p (k c) -> p k c", k=16)
        wgb = ring[s][:, 4096:6144].rearrange("p (k c) -> p k c", k=16)
        for tt in range(2):
            sga, sgb2, t1, t2 = sga_[it % 2], sgb2_[it % 2], t1_[it % 2], t2_[it % 2]
            it += 1
            ts_ = slice(tt * 512, (tt + 1) * 512)
            te_ = slice(HALO + tt * 512, HALO + (tt + 1) * 512)
            b_ga = next_bank()
            mm_group(banks[b_ga][:, :], [(wga[:, kc, :], nT1[:, kc, te_]) for kc in range(16)], reads=[RS[s], B_NT], writes=[PB[b_ga]])
            P.op("act", lambda e, b=b_ga, sga=sga: e.activation(out=sga.ap, in_=banks[b][:, :], func=AF.Sigmoid),
                 reads=[PB[b_ga]], writes=sga.bufs)
            b_ya = next_bank()
            mm_group(banks[b_ya][:, :], [(woa[:, kc, :], yaT[:, kc, ts_]) for kc in range(8)], reads=[RS[s], B_YA], writes=[PB[b_ya]])
            P.op("dve", lambda e, b=b_ya, sga=sga, t1=t1: e.tensor_tensor(out=t1.ap, in0=banks[b][:, :], in1=sga.ap, op=ALU.mult),
                 reads=[PB[b_ya]] + sga.bufs, writes=t1.bufs)
            b_gb = next_bank()
            mm_group(banks[b_gb][:, :], [(wgb[:, kc, :], nT1[:, kc, te_]) for kc in range(16)], reads=[RS[s], B_NT], writes=[PB[b_gb]])
            P.op("act", lambda e, b=b_gb, sgb2=sgb2: e.activation(out=sgb2.ap, in_=banks[b][:, :], func=AF.Sigmoid),
                 reads=[PB[b_gb]], writes=sgb2.bufs)
            b_yb = next_bank()
            mm_group(banks[b_yb][:, :], [(wpw[:, kc, :], vT[:, kc, ts_]) for kc in range(8)], reads=[RS[s], B_VT], writes=[PB[b_yb]])
            P.op("dve", lambda e, b=b_yb, sgb2=sgb2, t2=t2, c=c: e.scalar_tensor_tensor(
                out=t2.ap, in0=banks[b][:, :], scalar=prm[:, P_BPW + c:P_BPW + c + 1], in1=sgb2.ap, op0=ALU.add, op1=ALU.mult),
                reads=[PB[b_yb], B_PRM] + sgb2.bufs, writes=t2.bufs)
            extra = [B_VPRE] if first_mt else []
            first_mt = False
            P.op("pool", lambda e, t1=t1, t2=t2, c=c, ts_=ts_: e.tensor_tensor(out=mT[:, c, ts_], in0=t1.ap, in1=t2.ap, op=ALU.add),
                 reads=t1.bufs + t2.bufs, writes=B_TT[4 * tt:4 * tt + 4] + extra)

    if stop == "mT":
        return dump_bf(mT, 4, 4, B_TT)

    P.op("sp", lambda e: e.nop(), writes=[B_NT, B_YA, B_VT])
    for t in range(1, NT_TILES):
        P.op("sp", lambda e, t=t: e.dma_start(out=h[:, t, :], in_=x_d[HALO + t * 128:HALO + (t + 1) * 128, :]),
             writes=[B_HT[t]], dma=B_HT[t])

    ntc = ntb + [Scr(24576, 4096, BF16)]

    class NormPipe:
        def __init__(self, col0):
            self.col0 = col0
            self.q1 = []
            self.q2 = []

        def push(self, t):
            norm_p1(h[:, t, :], [B_HT[t]], self.col0 + t, 128, ntc[t % 3])
            if self.q2:
                self._tr(self.q2.pop(0))
            if self.q1:
                t1 = self.q1.pop(0)
                norm_p2(h[:, t1, :], [B_HT[t1]], self.col0 + t1, 128, ntc[t1 % 3])
                self.q2.append(t1)
            self.q1.append(t)

        def _tr(self, t):
            transposes(ntc[t % 3], [nT2[:, 0:8, t * 128:(t + 1) * 128], nT2[:, 8:16, t * 128:(t + 1) * 128]], [B_TT[t]],
                       engs=("act", "act"))

        def flush(self):
            while self.q1 or self.q2:
                if self.q2:
                    self._tr(self.q2.pop(0))
                if self.q1:
                    t1 = self.q1.pop(0)
                    norm_p2(h[:, t1, :], [B_HT[t1]], self.col0 + t1, 128, ntc[t1 % 3])
                    self.q2.append(t1)

    np2 = NormPipe(9)
    for n in range(4):
        s, = use(f"O{n}", after=([B_HT[7]] if n == 0 else ()))
        wv = ring[s].rearrange("p (k c) -> p k c", k=16)
        for t in range(NT_TILES):
            b = next_bank()
            mm_group(banks[b][:, :], [(mT[:, kc, t * 128:(t + 1) * 128], wv[:, kc, :]) for kc in range(16)],
                     reads=[RS[s], B_TT[t]], writes=[PB[b]])
            if t == 0:
                P.op("dve", lambda e, b=b, n=n: e.tensor_tensor(out=h[:, 0, n * 512:(n + 1) * 512], in0=x0s.ap[:, n * 512:(n + 1) * 512],
                                                              in1=banks[b][:, :], op=ALU.add),
                     reads=[PB[b]] + x0s.bufs, writes=[B_HT[0]])
            else:
                P.op("dve", lambda e, b=b, t=t, n=n: e.tensor_tensor(out=h[:, t, n * 512:(n + 1) * 512], in0=h[:, t, n * 512:(n + 1) * 512],
                                                                   in1=banks[b][:, :], op=ALU.add),
                     reads=[PB[b], B_HT[t]], writes=[B_HT[t]])
            if n == 3 and stop != "h1":
                np2.push(t)
    if stop == "h1":
        return dump_h()
    np2.flush()
    P.op("sp", lambda e: e.dma_start(out=gb, in_=g_d[2:3, :].partition_broadcast(128)), writes=[B_GB], dma=B_GB)

    fT_ = [Scr(0, 8192, BF16), Scr(8192, 8192, BF16)]
    sl_ = [Scr(24576, 2048), Scr(26624, 2048)]
    it = 0
    np3 = NormPipe(17)
    for blk in range(NFB):
        fT = fT_[blk % 2]
        f3 = fT.ap.rearrange("p (k t) -> p k t", k=4)
        sg_, su_ = use(f"FG{blk}", f"FU{blk}")
        wg = ring[sg_].rearrange("p (k c) -> p k c", k=16)
        wu = ring[su_].rearrange("p (k c) -> p k c", k=16)
        for cc in range(4):
            for tt in range(2):
                sl = sl_[it % 2]
                it += 1
                ts_ = slice(tt * 512, (tt + 1) * 512)
                bg = next_bank()
                mm_group(banks[bg][:, :], [(wg[:, kc, cc * 128:(cc + 1) * 128], nT2[:, kc, ts_]) for kc in range(16)],
                         reads=[RS[sg_]] + B_TT[4 * tt:4 * tt + 4], writes=[PB[bg]])
                P.op("act", lambda e, b=bg, sl=sl: e.activation(out=sl.ap, in_=banks[b][:, :], func=AF.Silu),
                     reads=[PB[bg]], writes=sl.bufs)
                bu = next_bank()
                mm_group(banks[bu][:, :], [(wu[:, kc, cc * 128:(cc + 1) * 128], nT2[:, kc, ts_]) for kc in range(16)],
                         reads=[RS[su_]] + B_TT[4 * tt:4 * tt + 4], writes=[PB[bu]])
                P.op("dve", lambda e, b=bu, sl=sl, f3=f3, cc=cc, ts_=ts_: e.tensor_tensor(out=f3[:, cc, ts_], in0=banks[b][:, :], in1=sl.ap, op=ALU.mult),
                     reads=[PB[bu]] + sl.bufs, writes=fT.bufs)
        sd_, = use(f"FD{blk}")
        wd = ring[sd_].rearrange("p (k c) -> p k c", k=4)
        for t in range(NT_TILES):
            for n in range(4):
                b = next_bank()
                mm_group(banks[b][:, :], [(f3[:, kc, t * 128:(t + 1) * 128], wd[:, kc, n * 512:(n + 1) * 512]) for kc in range(4)],
                         reads=[RS[sd_]] + fT.bufs, writes=[PB[b]])
                P.op("dve", lambda e, b=b, t=t, n=n: e.tensor_tensor(out=h[:, t, n * 512:(n + 1) * 512], in0=h[:, t, n * 512:(n + 1) * 512],
                                                                   in1=banks[b][:, :], op=ALU.add),
                     reads=[PB[b], B_HT[t]], writes=[B_HT[t]])
            if blk == NFB - 1 and stop != "h2":
                np3.push(t)
    if stop == "h2":
        return dump_h()
    np3.flush()
    P.op("sp", lambda e: e.dma_start(out=gb, in_=g_d[3:4, :].partition_broadcast(128)), writes=[B_GB], dma=B_GB)

    sgp_ = [Scr(0, 2048), Scr(2048, 2048)]
    tp_ = [Scr(4096, 2048), Scr(6144, 2048)]
    ost_ = [Scr(8192, 8192), Scr(16384, 8192)]
    fq1, fq2 = [], []
    fjunk = Scr(24576, 4096, BF16)

    def fin_p2(t):
        ost = ost_[t % 2]
        norm_p2(h[:, t, :], [B_HT[t]], 25 + t, 128, ost, out_ap=ost.ap)
        P.op("sp", lambda e, t=t, ost=ost: e.dma_start(out=out_d[t * 128:(t + 1) * 128, :], in_=ost.ap),
             reads=ost.bufs, dma=ost.bufs[0])

    def fin_push(t):
        if fq2:
            fin_p2(fq2.pop(0))
        if fq1:
            t1 = fq1.pop(0)
            norm_p1(h[:, t1, :], [B_HT[t1]], 25 + t1, 128, ost_[t1 % 2], out_ap=ost_[t1 % 2].ap, junk=fjunk)
            fq2.append(t1)
        fq1.append(t)

    def fin_flush():
        while fq1 or fq2:
            if fq2:
                fin_p2(fq2.pop(0))
            if fq1:
                t1 = fq1.pop(0)
                norm_p1(h[:, t1, :], [B_HT[t1]], 25 + t1, 128, ost_[t1 % 2], out_ap=ost_[t1 % 2].ap, junk=fjunk)
                fq2.append(t1)

    it = 0

    def ple_group(n, t, spp, s_):
        nonlocal it
        wpp = ring[spp][:, 0:1024].rearrange("p (k c) -> p k c", k=2)
        wv = ring[s_].rearrange("p (k c) -> p k c", k=16)
        sgp, tp = sgp_[it % 2], tp_[it % 2]
        it += 1
        bgt = next_bank()
        mm_group(banks[bgt][:, :], [(nT2[:, kc, t * 128:(t + 1) * 128], wv[:, kc, :]) for kc in range(16)],
                 reads=[RS[s_], B_TT[t]], writes=[PB[bgt]])
        P.op("act", lambda e, b=bgt, sgp=sgp: e.activation(out=sgp.ap, in_=banks[b][:, :], func=AF.Sigmoid),
             reads=[PB[bgt]], writes=sgp.bufs)
        bp = next_bank()
        mm_group(banks[bp][:, :], [(pT[:, ec, t * 128:(t + 1) * 128], wpp[:, ec, :]) for ec in range(2)],
                 reads=[RS[spp], B_PT], writes=[PB[bp]])
        P.op("dve", lambda e, b=bp, sgp=sgp, tp=tp: e.tensor_tensor(out=tp.ap, in0=banks[b][:, :], in1=sgp.ap, op=ALU.mult),
             reads=[PB[bp]] + sgp.bufs, writes=tp.bufs)
        P.op("dve" if n >= 2 else "pool",
             lambda e, tp=tp, t=t, n=n: e.tensor_tensor(out=h[:, t, n * 512:(n + 1) * 512], in0=h[:, t, n * 512:(n + 1) * 512],
                                                       in1=tp.ap, op=ALU.add),
             reads=tp.bufs + [B_HT[t]], writes=[B_HT[t]])

    for n in range(2):
        spp, s_ = use(f"PP{n}", f"PG{n}")
        for t in range(NT_TILES):
            ple_group(n, t, spp, s_)
    spp2, s2, spp3, s3 = use("PP2", "PG2", "PP3", "PG3")
    SKEW = 3
    for i in range(NT_TILES + SKEW):
        if i < NT_TILES:
            ple_group(2, i, spp2, s2)
        if i - SKEW >= 0:
            ple_group(3, i - SKEW, spp3, s3)
            fin_push(i - SKEW)
    fin_flush()
    P.op("sp", lambda e: e.nop(), writes=ost_[0].bufs + ost_[1].bufs)
    P.emit()
    return nc


_NC_CACHE = {}


def _get_nc(stop=None):
    if stop not in _NC_CACHE:
        _NC_CACHE[stop] = build(stop)
    return _NC_CACHE[stop]


def _prep_inputs(x, p, g_mix, w_in, conv_a_w, w_out_a, b_glu, conf_dw_w, conf_dw_b, conf_ln_g, conf_ln_b,
                 w_pw_b, b_pw_b, w_o, g_ffn, w_gate, w_up, w_down, g_ple, w_ple_gate, w_ple_proj, g_final):
    f = lambda a: np.asarray(a, np.float32)
    x2 = f(x).reshape(SEQ, D)
    p2 = f(p).reshape(SEQ, PLE)
    wflat = _prep_weights(f(w_in)[0], f(w_out_a)[0], f(w_pw_b)[0], f(w_o)[0], f(w_gate)[0], f(w_up)[0], f(w_down)[0],
                          f(w_ple_gate)[0], f(w_ple_proj)[0])
    prm = _prep_prm(f(conv_a_w)[0], f(b_glu)[0], f(conf_dw_w)[0], f(conf_dw_b)[0], f(conf_ln_g)[0], f(conf_ln_b)[0], f(b_pw_b)[0])
    gains = np.ascontiguousarray(np.stack([f(g_mix)[0], f(g_ffn)[0], f(g_ple)[0], f(g_final)]))
    xpad = np.concatenate([np.zeros((HALO, D), np.float32), x2], axis=0)
    in_maps = []
    for c in range(NCORES):
        in_maps.append({
            "x_ext": np.ascontiguousarray(xpad[c * T:c * T + TE]),
            "p_c": np.ascontiguousarray(p2[c * T:(c + 1) * T].reshape(NT_TILES, 128, PLE).transpose(1, 0, 2)).reshape(128, NT_TILES * PLE),
            "hmask": np.full((128, HALO), 0.0 if c == 0 else 1.0, np.float32),
            "wflat": wflat,
            "prm": prm,
            "gains": gains,
        })
    return in_maps


def kernel(**inputs):
    in_maps = _prep_inputs(**inputs)
    nc = _get_nc(None)
    res = run_bass_kernel_spmd(nc, in_maps, core_ids=list(range(NCORES)))
    out = np.concatenate([np.asarray(r["out"], np.float32) for r in res.results], axis=0)
    return out.reshape(1, SEQ, D)
```

```python
import numpy as np
import concourse.bass as bass
import concourse.mybir as mybir
from concourse.bass_utils import run_bass_kernel_spmd

F32 = mybir.dt.float32
BF16 = mybir.dt.bfloat16
AF = mybir.ActivationFunctionType
ALU = mybir.AluOpType

NCORES = 8
D = 2048
SEQ = 8192
T = SEQ // NCORES
HALO = 32
TE = T + HALO
NT_TILES = T // 128
KC = D // 128
CW = 1024
NJ = CW // 128
DFF = 5632
NFB = DFF // 512
PLE = 256
EPS = 1e-6
LN_EPS = 1e-5
SLOT = 8192
NSLOT = 4

COMPUTE = ("pe", "act", "dve", "pool")
ENGS = ("pe", "act", "dve", "pool", "sp")


class Buf:
    __slots__ = ("name", "writer", "readers", "sem", "cnt")

    def __init__(self, name):
        self.name = name
        self.writer = None
        self.readers = []
        self.sem = None
        self.cnt = 0


class Op:
    __slots__ = ("eng", "fn", "deps", "idx", "needed", "val", "dma_buf", "dma_val", "is_dma")


class Prog:
    def __init__(self, nc):
        self.nc = nc
        self.streams = {e: [] for e in ENGS}
        self.dma_bufs = []

    def op(self, eng, fn, reads=(), writes=(), dma=None):
        o = Op()
        o.eng = eng
        o.fn = fn
        o.needed = False
        o.val = None
        o.is_dma = dma is not None
        o.dma_buf = dma
        o.dma_val = None
        deps = []
        for b in reads:
            if b.writer is not None:
                deps.append(b.writer)
            b.readers.append(o)
        for b in writes:
            if b.writer is not None:
                deps.append(b.writer)
            deps.extend(r for r in b.readers if r is not o)
            b.writer = o
            b.readers = []
        if dma is not None:
            if dma.sem is None:
                self.dma_bufs.append(dma)
                dma.sem = True
            dma.cnt += 16
            o.dma_val = dma.cnt
        o.deps = [d for d in deps if d is not o and not (eng == "pe" and d.eng == "pe" and not d.is_dma)]
        o.idx = len(self.streams[eng])
        self.streams[eng].append(o)
        return o

    def emit(self):
        nc = self.nc
        for e in ENGS:
            for o in self.streams[e]:
                for d in o.deps:
                    if not d.is_dma:
                        d.needed = True
        for e in COMPUTE:
            c = 0
            for o in self.streams[e]:
                if o.needed and not o.is_dma:
                    c += 1
                    o.val = c
        sems = {e: nc.alloc_semaphore("S_" + e) for e in COMPUTE}
        for b in self.dma_bufs:
            b.sem = nc.alloc_semaphore("D_" + b.name)
        streams = self.streams

        def run(e, eng):
            waited = {}
            for o in streams[e]:
                w = {}
                for d in o.deps:
                    if d.is_dma:
                        s, v = d.dma_buf.sem, d.dma_val
                    else:
                        s, v = sems[d.eng], d.val
                    k = id(s)
                    if k not in w or w[k][1] < v:
                        w[k] = (s, v)
                for k, (s, v) in w.items():
                    if waited.get(k, 0) >= v:
                        continue
                    waited[k] = v
                    eng.wait_ge(s, v)
                ins = o.fn(eng)
                if o.is_dma:
                    ins.then_inc(o.dma_buf.sem, 16)
                elif o.needed:
                    ins.then_inc(sems[e], 1)

        with nc.Block() as block:
            @block.tensor
            def _(eng):
                run("pe", eng)

            @block.scalar
            def _(eng):
                run("act", eng)

            @block.vector
            def _(eng):
                run("dve", eng)

            @block.gpsimd
            def _(eng):
                run("pool", eng)

            @block.sync
            def _(eng):
                run("sp", eng)


def _kc(w):
    k, c = w.shape
    return np.ascontiguousarray(w.reshape(k // 128, 128, c).transpose(1, 0, 2)).reshape(128, -1)


def _ch(group, j):
    base = {"ah": 0, "ab": 8, "ac": 16, "gv": 24, "gg": 32, "ga": 40, "gb": 56}[group]
    return base + j


B_ORDER = []
for _i in range(0, NJ, 2):
    B_ORDER += [("gg", _i), ("gv", _i), ("gg", _i + 1), ("gv", _i + 1)]
A_ORDER = []
for _j in range(NJ):
    A_ORDER += [("ah", _j), ("ac", _j), ("ab", _j)]


def _load_plan():
    plan = []
    for i in range(4):
        plan.append((f"B{i}", 16 * 512))
    for i in range(6):
        plan.append((f"A{i}", 16 * 512))
    for c in range(16):
        plan.append((f"G{c}", 6144))
    for n in range(4):
        plan.append((f"O{n}", 16 * 512))
    for b in range(NFB):
        plan.append((f"FG{b}", 16 * 512))
        plan.append((f"FU{b}", 16 * 512))
        plan.append((f"FD{b}", 4 * 2048))
    for n in range(4):
        plan.append((f"PP{n}", 2 * 512))
        plan.append((f"PG{n}", 16 * 512))
    return plan


PLAN = _load_plan()
WTOT = sum(n for _, n in PLAN)


def _prep_weights(w_in, w_out_a, w_pw_b, w_o, w_gate, w_up, w_down, w_ple_gate, w_ple_proj):
    wflat = np.empty((128, WTOT), np.float32)
    pos = 0

    def put(a):
        nonlocal pos
        n = a.shape[1]
        wflat[:, pos:pos + n] = a
        pos += n

    def cols(chunks):
        idx = np.concatenate([np.arange(_ch(g, j) * 128, _ch(g, j) * 128 + 128) for g, j in chunks])
        return _kc(w_in[:, idx])

    for i in range(4):
        put(cols(B_ORDER[4 * i:4 * i + 4]))
    for i in range(6):
        put(cols(A_ORDER[4 * i:4 * i + 4]))
    for c in range(16):
        put(_kc(w_out_a[:, c * 128:(c + 1) * 128]))
        put(_kc(w_pw_b[:, c * 128:(c + 1) * 128]))
        put(cols([("ga", c)]))
        put(cols([("gb", c)]))
    for n in range(4):
        put(_kc(w_o[:, n * 512:(n + 1) * 512]))
    for b in range(NFB):
        put(_kc(w_gate[:, b * 512:(b + 1) * 512]))
        put(_kc(w_up[:, b * 512:(b + 1) * 512]))
        put(_kc(w_down[b * 512:(b + 1) * 512, :]))
    for n in range(4):
        put(_kc(w_ple_proj[:, n * 512:(n + 1) * 512]))
        put(_kc(w_ple_gate[:, n * 512:(n + 1) * 512]))
    assert pos == WTOT
    return wflat


P_BV, P_BG, P_CA, P_CW, P_DWB, P_LNG, P_LNB, P_BPW = 0, 8, 16, 40, 288, 296, 304, 312
NPRM = 328


def _fm(v, n):
    return np.ascontiguousarray(np.asarray(v, np.float32).reshape(n, 128).T)


def _prep_prm(conv_a_w, b_glu, conf_dw_w, conf_dw_b, conf_ln_g, conf_ln_b, b_pw_b):
    prm = np.zeros((128, NPRM), np.float32)
    prm[:, P_BV:P_BV + 8] = _fm(b_glu[:CW], 8)
    prm[:, P_BG:P_BG + 8] = _fm(b_glu[CW:], 8)
    prm[:, P_CA:P_CA + 24] = np.ascontiguousarray(conv_a_w.reshape(3, 8, 128).transpose(2, 1, 0)).reshape(128, 24)
    prm[:, P_CW:P_CW + 248] = np.ascontiguousarray(conf_dw_w.reshape(31, 8, 128).transpose(2, 1, 0)).reshape(128, 248)
    prm[:, P_DWB:P_DWB + 8] = _fm(conf_dw_b, 8)
    prm[:, P_LNG:P_LNG + 8] = _fm(conf_ln_g, 8)
    prm[:, P_LNB:P_LNB + 8] = _fm(conf_ln_b, 8)
    prm[:, P_BPW:P_BPW + 16] = _fm(b_pw_b, 16)
    return prm


def build(stop=None):
    nc = bass.Bass("TRN2", target_bir_lowering=False)
    x_d = nc.dram_tensor("x_ext", [TE, D], F32, kind="ExternalInput").ap()
    p_d = nc.dram_tensor("p_c", [128, NT_TILES * PLE], F32, kind="ExternalInput").ap()
    hm_d = nc.dram_tensor("hmask", [128, HALO], F32, kind="ExternalInput").ap()
    w_d = nc.dram_tensor("wflat", [128, WTOT], F32, kind="ExternalInput").ap()
    prm_d = nc.dram_tensor("prm", [128, NPRM], F32, kind="ExternalInput").ap()
    g_d = nc.dram_tensor("gains", [4, D], F32, kind="ExternalInput").ap()
    out_d = nc.dram_tensor("out", [T, D], F32, kind="ExternalOutput").ap()
    dbg_d = None
    if stop is not None:
        dbg_d = nc.dram_tensor("dbg", [128, 16384], F32, kind="ExternalOutput").ap()

    NB4 = 52992
    big = nc.alloc_sbuf_tensor("big", [128, NB4], F32)

    def reg(off, nbytes, dt=F32):
        assert off % 4 == 0 and nbytes % 4 == 0 and (off + nbytes) <= NB4 * 4, (off, nbytes)
        v = big[:, off // 4:(off + nbytes) // 4]
        return v.bitcast(BF16) if dt == BF16 else v

    ring = [reg(i * 16384, 16384, BF16) for i in range(NSLOT)]
    O_NT, O_YA, O_VT, O_H, O_VPRE, O_GB, O_SM, O_SC = 65536, 99328, 115712, 65536, 132096, 164864, 173056, 181248
    SC_SIZE = NB4 * 4 - O_SC
    nT1 = reg(O_NT, 16 * TE * 2, BF16).rearrange("p (k t) -> p k t", k=16)
    yaT = reg(O_YA, 16384, BF16).rearrange("p (k t) -> p k t", k=8)
    vT = reg(O_VT, 16384, BF16).rearrange("p (k t) -> p k t", k=8)
    xs = reg(O_YA, 65536).rearrange("p (t d) -> p t d", t=8)
    h = reg(O_H, 65536).rearrange("p (t d) -> p t d", t=8)
    vpre = reg(O_VPRE, 32768).rearrange("p (k t) -> p k t", k=8)
    mT = reg(O_VPRE, 32768, BF16).rearrange("p (k t) -> p k t", k=16)
    nT2 = mT
    gb = reg(O_GB, 8192)
    prm = reg(O_SM, 1312)
    hmask = reg(O_SM + 1344, 128)
    ident = reg(O_SM + 1472, 256, BF16)
    ones32 = reg(O_SM + 1728, 512)
    ssq = reg(O_SM + 2240, 160)
    rstd = reg(O_SM + 2400, 160)
    stmp = reg(O_SM + 2560, 160)
    epsr = reg(O_SM + 2720, 4)
    epsl = reg(O_SM + 2724, 4)
    identf = reg(O_SM + 2752, 512)
    pT = reg(O_SM + 3328, 4096, BF16).rearrange("p (k t) -> p k t", k=2)
    mhalf1 = reg(O_SM + 7424, 4)

    banks = [nc.alloc_psum_tensor(f"bank{i}", [128, 512], F32) for i in range(8)]
    banks_bf = [b[:].bitcast(BF16) for b in banks]

    P = Prog(nc)
    RS = [Buf(f"ring{i}") for i in range(NSLOT)]
    PB = [Buf(f"pb{i}") for i in range(8)]
    PAGE = 512
    SCP = [Buf(f"sc{i}") for i in range(SC_SIZE // PAGE + 1)]
    B_NT, B_YA, B_VT, B_VPRE, B_GB = Buf("nT"), Buf("ya"), Buf("vT"), Buf("vpre"), Buf("gb")
    B_TT = [Buf(f"tt{t}") for t in range(8)]
    B_HT = [Buf(f"h{t}") for t in range(8)]
    B_XS = [Buf(f"xs{t}") for t in range(8)]
    B_SS = [Buf(f"ss{i}") for i in range(40)]
    B_PRM, B_CONST, B_PT = Buf("prm"), Buf("const"), Buf("pT")
    B_OUT = Buf("outst")
    B_ID, B_IDF, B_ONES, B_EPS = Buf("ident"), Buf("identf"), Buf("ones"), Buf("eps")

    class Scr:
        def __init__(self, off, nbytes, dt=F32):
            assert off + nbytes <= SC_SIZE, (off, nbytes, SC_SIZE)
            self.ap = reg(O_SC + off, nbytes, dt)
            self.bufs = SCP[off // PAGE:(off + nbytes - 1) // PAGE + 1]

    wstate = {"next": 0, "off": 0}
    loaded = {}

    def issue_load(after=()):
        i = wstate["next"]
        if i >= len(PLAN):
            return
        name, n = PLAN[i]
        s = i % NSLOT
        off = wstate["off"]
        P.op("pool", lambda e, s=s, off=off, n=n: e.dma_start(out=ring[s][:, 0:n], in_=w_d[:, off:off + n]),
             reads=list(after), writes=[RS[s]], dma=RS[s])
        loaded[name] = s
        wstate["next"] = i + 1
        wstate["off"] = off + n

    PIDX = {nm: i for i, (nm, _) in enumerate(PLAN)}

    def use(*names, after=()):
        k0 = min(PIDX[n] for n in names)
        while wstate["next"] < len(PLAN) and wstate["next"] <= k0 + NSLOT - 1:
            issue_load(after)
        return [loaded[n] for n in names]

    rr = {"i": 0, "t": 0}

    def next_bank():
        b = rr["i"] % 6
        rr["i"] += 1
        return b

    def next_tbank():
        b = 6 + rr["t"] % 2
        rr["t"] += 1
        return b

    def mm_group(out_ap, pairs, reads, writes):
        def fn(e):
            n = len(pairs)
            r = None
            for i, (l, rh) in enumerate(pairs):
                r = e.matmul(out_ap, lhsT=l, rhs=rh, start=(i == 0), stop=(i == n - 1))
            return r
        return P.op("pe", fn, reads=reads, writes=writes)

    def dump_bf(view3, nq, kq, reads):
        cv = Scr(0, 16384)
        for q in range(nq):
            P.op("act", lambda e, q=q: e.activation(out=cv.ap.rearrange("p (k t) -> p k t", k=4),
                                                    in_=view3[:, 4 * q:4 * q + 4, :], func=AF.Copy),
                 reads=reads, writes=cv.bufs)
            P.op("sp", lambda e, q=q: e.dma_start(out=dbg_d[:, q * 4096:(q + 1) * 4096], in_=cv.ap), reads=cv.bufs, dma=B_OUT)
        P.op("sp", lambda e: e.nop(), writes=[B_OUT] + cv.bufs)
        P.emit()
        return nc

    def dump_h():
        for t in range(8):
            P.op("sp", lambda e, t=t: e.dma_start(out=dbg_d[:, t * 2048:(t + 1) * 2048], in_=h[:, t, :]), reads=[B_HT[t]], dma=B_OUT)
        P.op("sp", lambda e: e.nop(), writes=[B_OUT])
        P.emit()
        return nc

    pst = Scr(8192, 8192)
    P.op("sp", lambda e: e.dma_start(out=pst.ap, in_=p_d), writes=pst.bufs, dma=pst.bufs[0])
    xh = Scr(0, 8192)
    P.op("sp", lambda e: e.dma_start(out=xh.ap[0:HALO, :], in_=x_d[0:HALO, :]), writes=xh.bufs, dma=xh.bufs[0])
    P.op("sp", lambda e: e.dma_start(out=gb, in_=g_d[0:1, :].partition_broadcast(128)), writes=[B_GB], dma=B_GB)
    for t in range(NT_TILES):
        P.op("sp", lambda e, t=t: e.dma_start(out=xs[:, t, :], in_=x_d[HALO + t * 128:HALO + (t + 1) * 128, :]),
             writes=[B_XS[t]], dma=B_XS[t])
    P.op("sp", lambda e: e.dma_start(out=prm, in_=prm_d), writes=[B_PRM], dma=B_PRM)
    P.op("sp", lambda e: e.dma_start(out=hmask, in_=hm_d), writes=[B_CONST], dma=B_CONST)

    P.op("pool", lambda e: e.memset(identf, 0.0), writes=[B_IDF])
    P.op("pool", lambda e: e.affine_select(out=identf, in_=identf, pattern=[[-1, 128]], compare_op=ALU.not_equal,
                                           fill=1.0, base=0, channel_multiplier=1), reads=[B_IDF], writes=[B_IDF])
    P.op("pool", lambda e: e.memset(ones32, 1.0 / CW), writes=[B_ONES])

    def mk_eps(e):
        e.memset(epsr, EPS)
        e.memset(mhalf1, -0.5)
        return e.memset(epsl, LN_EPS)
    P.op("pool", mk_eps, writes=[B_EPS])
    P.op("pool", lambda e: e.memset(ssq, 0.0), writes=B_SS)
    P.op("dve", lambda e: e.tensor_copy(out=ident, in_=identf), reads=[B_IDF], writes=[B_ID])


    ntb = [Scr(16384, 4096, BF16), Scr(20480, 4096, BF16)]

    def norm_p1(src_ap, src_bufs, col, npart, ntile, out_ap=None, junk=None):
        dst = ntile.ap[0:npart, :] if out_ap is None else out_ap
        jb = ntile.bufs
        if junk is not None:
            dst, jb = junk.ap, junk.bufs
        sb = B_SS[col]
        P.op("act", lambda e: e.activation(out=dst, in_=src_ap, func=AF.Square, accum_out=ssq[0:npart, col:col + 1]),
             reads=src_bufs, writes=jb + [sb])
        P.op("pool", lambda e: e.tensor_scalar(out=stmp[0:npart, col:col + 1], in0=ssq[0:npart, col:col + 1],
                                              scalar1=1.0 / D, scalar2=EPS, op0=ALU.mult, op1=ALU.add),
             reads=[sb], writes=[sb])
        P.op("pool", lambda e: e.tensor_tensor(out=rstd[0:npart, col:col + 1], in0=stmp[0:npart, col:col + 1],
                                               in1=mhalf1[0:npart, :], op=ALU.pow),
             reads=[sb, B_EPS], writes=[sb])

    def norm_p2(src_ap, src_bufs, col, npart, ntile, out_ap=None):
        dst = ntile.ap[0:npart, :] if out_ap is None else out_ap
        sb = B_SS[col]
        P.op("dve", lambda e: e.scalar_tensor_tensor(out=dst, in0=src_ap, scalar=rstd[0:npart, col:col + 1],
                                                     in1=gb[0:npart, :], op0=ALU.mult, op1=ALU.mult),
             reads=src_bufs + [sb, B_GB], writes=ntile.bufs)

    def norm_ops(src_ap, src_bufs, col, npart, ntile, out_ap=None):
        norm_p1(src_ap, src_bufs, col, npart, ntile, out_ap)
        norm_p2(src_ap, src_bufs, col, npart, ntile, out_ap)

    def transposes(ntile, dst_views, dst_bufs, engs=("act", "dve")):
        for half in range(2):
            tb = next_tbank()

            def fn(e, half=half, tb=tb):
                r = None
                for i in range(8):
                    kc = half * 8 + i
                    r = e.transpose(out=banks_bf[tb][:, i * 128:(i + 1) * 128],
                                    in_=ntile.ap[:, kc * 128:(kc + 1) * 128], identity=ident)
                return r
            P.op("pe", fn, reads=ntile.bufs + [B_ID], writes=[PB[tb]])
            src = banks_bf[tb].rearrange("p (k t) -> p k t", k=8)
            dst = dst_views[half]
            if engs[half] == "act":
                P.op("act", lambda e, dst=dst, src=src: e.activation(out=dst, in_=src, func=AF.Copy),
                     reads=[PB[tb]], writes=dst_bufs)
            else:
                P.op("dve", lambda e, dst=dst, src=src: e.tensor_copy(out=dst, in_=src),
                     reads=[PB[tb]], writes=dst_bufs)

    nth = Scr(24576, 4096, BF16)
    norm_ops(xh.ap[0:HALO, :], xh.bufs, 8, HALO, nth)
    tb = next_tbank()

    def fn_h(e, tb=tb):
        r = None
        for kc in range(16):
            r = e.transpose(out=banks_bf[tb][:, kc * HALO:(kc + 1) * HALO],
                            in_=nth.ap[0:HALO, kc * 128:(kc + 1) * 128], identity=ident[0:HALO, 0:HALO])
        return r
    P.op("pe", fn_h, reads=nth.bufs + [B_ID], writes=[PB[tb]])
    src_h = banks_bf[tb][:, 0:16 * HALO].rearrange("p (k t) -> p k t", k=16)
    P.op("act", lambda e: e.activation(out=nT1[:, :, 0:HALO], in_=src_h, func=AF.Copy), reads=[PB[tb]], writes=[B_NT])
    ntb3 = ntb + [Scr(24576, 4096, BF16)]

    def tr0(t):
        c0 = HALO + t * 128
        transposes(ntb3[t % 3], [nT1[:, 0:8, c0:c0 + 128], nT1[:, 8:16, c0:c0 + 128]], [B_NT])
    for t in range(NT_TILES + 2):
        if t < NT_TILES:
            norm_p1(xs[:, t, :], [B_XS[t]], t, 128, ntb3[t % 3])
        if 3 <= t < 3 + NSLOT:
            issue_load()
        if 0 <= t - 1 < NT_TILES:
            norm_p2(xs[:, t - 1, :], [B_XS[t - 1]], t - 1, 128, ntb3[(t - 1) % 3])
        if 0 <= t - 2 < NT_TILES:
            tr0(t - 2)

    pbf = Scr(0, 4096, BF16)
    P.op("dve", lambda e: e.tensor_copy(out=pbf.ap, in_=pst.ap), reads=pst.bufs, writes=pbf.bufs)
    for half in range(2):
        tb = next_tbank()

        def fn(e, half=half, tb=tb):
            r = None
            for i in range(8):
                q = half * 8 + i
                r = e.transpose(out=banks_bf[tb][:, i * 128:(i + 1) * 128],
                                in_=pbf.ap[:, q * 128:(q + 1) * 128], identity=ident)
            return r
        P.op("pe", fn, reads=pbf.bufs + [B_ID], writes=[PB[tb]])
        src = banks_bf[tb].rearrange("p (t c k) -> p c t k", t=4, c=2)
        dst = pT[:, :, half * 512:(half + 1) * 512].rearrange("p c (t k) -> p c t k", t=4)
        P.op("act", lambda e, dst=dst, src=src: e.activation(out=dst, in_=src, func=AF.Copy),
             reads=[PB[tb]], writes=[B_PT])

    P.op("sp", lambda e: e.dma_start(out=gb, in_=g_d[1:2, :].partition_broadcast(128)), writes=[B_GB], dma=B_GB)

    if stop == "n1T":
        return dump_bf(nT1[:, :, HALO:TE], 4, 4, [B_NT])

    TT3 = [(i * 352, 352) for i in range(3)]
    KD = 12
    NPE = 31 - KD
    sg1 = Scr(0, 4224)
    ub_ = [Scr(4608, 2112, BF16), Scr(7168, 2112, BF16)]
    DGS = -(-NPE * 256 // 512) * 512
    dg_ = [Scr(9728, NPE * 256, BF16), Scr(9728 + DGS, NPE * 256, BF16)]
    acc1 = Scr(9728 + 2 * DGS, 4096)

    def w_in_chunk(slot_i, q):
        v = ring[slot_i].rearrange("p (k c) -> p k c", k=16)
        return [v[:, kc, q * 128:(q + 1) * 128] for kc in range(16)]

    pos_in = {}
    for i, ch in enumerate(B_ORDER):
        pos_in[ch] = (f"B{i // 4}", i % 4)
    for i, ch in enumerate(A_ORDER):
        pos_in[ch] = (f"A{i // 4}", i % 4)

    def inproj_ext(group, j, consume):
        name, q = pos_in[(group, j)]
        s, = use(name)
        lhs = w_in_chunk(s, q)
        for (c0, n) in TT3:
            b = next_bank()
            mm_group(banks[b][:, 0:n], [(lhs[kc], nT1[:, kc, c0:c0 + n]) for kc in range(16)],
                     reads=[RS[s], B_NT], writes=[PB[b]])
            consume(b, c0, n)

    first = {"vpre": True, "ya": True, "vT": True}

    def xs_guard(key):
        if first[key]:
            first[key] = False
            return list(B_XS)
        return []

    for j in range(NJ):
        sg, ub, dg = sg1, ub_[j % 2], dg_[j % 2]
        d3 = dg.ap.rearrange("p (k c) -> p k c", k=NPE)
        id3 = ident.unsqueeze(1).to_broadcast([128, NPE, 128])
        cw3 = prm[:, P_CW + j * 31 + KD:P_CW + (j + 1) * 31].unsqueeze(2).to_broadcast([128, NPE, 128])
        P.op("pool", lambda e, d3=d3, cw3=cw3, id3=id3: e.tensor_tensor(out=d3, in0=id3, in1=cw3, op=ALU.mult),
             reads=[B_ID, B_PRM], writes=dg.bufs)

        def cons_g(b, c0, n, sg=sg, j=j):
            P.op("act", lambda e: e.activation(out=sg.ap[:, c0:c0 + n], in_=banks[b][:, 0:n], func=AF.Sigmoid,
                                               bias=prm[:, P_BG + j:P_BG + j + 1]),
                 reads=[PB[b], B_PRM], writes=sg.bufs)
        inproj_ext("gg", j, cons_g)

        def cons_v(b, c0, n, sg=sg, ub=ub, j=j):
            P.op("dve", lambda e: e.scalar_tensor_tensor(out=ub.ap[:, c0:c0 + n], in0=banks[b][:, 0:n],
                                                         scalar=prm[:, P_BV + j:P_BV + j + 1], in1=sg.ap[:, c0:c0 + n],
                                                         op0=ALU.add, op1=ALU.mult),
                 reads=[PB[b], B_PRM] + sg.bufs, writes=ub.bufs)
            if c0 == 0:
                P.op("dve", lambda e: e.tensor_tensor(out=ub.ap[:, 0:HALO], in0=ub.ap[:, 0:HALO], in1=hmask, op=ALU.mult),
                     reads=ub.bufs + [B_CONST], writes=ub.bufs)
        inproj_ext("gv", j, cons_v)

        cbanks = []
        for tt in range(2):
            b = next_bank()
            cbanks.append(b)
            base = HALO + tt * 512 - 30
            mm_group(banks[b][:, :], [(d3[:, k - KD, :], ub.ap[:, base + k:base + k + 512]) for k in range(KD, 31)],
                     reads=dg.bufs + ub.bufs, writes=[PB[b]])
        for k in range(KD):
            src = ub.ap[:, HALO - 30 + k:HALO - 30 + k + T]
            wk = prm[:, P_CW + j * 31 + k:P_CW + j * 31 + k + 1]
            if k == 0:
                P.op("dve", lambda e, src=src, wk=wk: e.tensor_scalar(out=acc1.ap, in0=src, scalar1=wk, scalar2=None, op0=ALU.mult),
                     reads=ub.bufs + [B_PRM], writes=acc1.bufs)
            else:
                P.op("dve", lambda e, src=src, wk=wk: e.scalar_tensor_tensor(out=acc1.ap, in0=src, scalar=wk, in1=acc1.ap,
                                                                           op0=ALU.mult, op1=ALU.add),
                     reads=ub.bufs + [B_PRM] + acc1.bufs, writes=acc1.bufs)
        for tt in range(2):
            b = cbanks[tt]
            P.op("act", lambda e, b=b, j=j, tt=tt: e.activation(out=vpre[:, j, tt * 512:(tt + 1) * 512], in_=banks[b][:, :],
                                                               func=AF.Identity, bias=prm[:, P_DWB + j:P_DWB + j + 1]),
                 reads=[PB[b], B_PRM], writes=[B_VPRE] + xs_guard("vpre"))
        P.op("pool", lambda e, j=j: e.tensor_tensor(out=vpre[:, j, :], in0=vpre[:, j, :], in1=acc1.ap, op=ALU.add),
             reads=acc1.bufs + [B_VPRE], writes=[B_VPRE])

    if stop == "vpre":
        P.op("sp", lambda e: e.dma_start(out=dbg_d[:, 0:8192], in_=vpre.rearrange("p k t -> p (k t)")), reads=[B_VPRE], dma=B_OUT)
        P.op("sp", lambda e: e.nop(), writes=[B_OUT])
        P.emit()
        return nc

    z_sb = Scr(0, 4096)
    mean_sb = Scr(4096, 4096)
    var_sb = Scr(8192, 4096)
    def ln_stats():
        vsq_ = [Scr(12288, 4096), Scr(16384, 4096)]
        for j in range(NJ):
            vs = vsq_[j % 2]
            P.op("act", lambda e, vs=vs, j=j: e.activation(out=vs.ap, in_=vpre[:, j, :], func=AF.Square),
                 reads=[B_VPRE], writes=vs.bufs)

            def fn(e, vs=vs, j=j):
                r = None
                for tt in range(2):
                    e.matmul(banks[tt][:, :], lhsT=ones32, rhs=vpre[:, j, tt * 512:(tt + 1) * 512], start=(j == 0), stop=(j == NJ - 1))
                    r = e.matmul(banks[2 + tt][:, :], lhsT=ones32, rhs=vs.ap[:, tt * 512:(tt + 1) * 512], start=(j == 0), stop=(j == NJ - 1))
                return r
            P.op("pe", fn, reads=[B_VPRE, B_ONES] + vs.bufs, writes=[PB[0], PB[1], PB[2], PB[3]])
        rr["i"] = 4
        for tt in range(2):
            sl_ = slice(tt * 512, (tt + 1) * 512)
            P.op("act", lambda e, tt=tt, sl_=sl_: e.activation(out=mean_sb.ap[:, sl_], in_=banks[tt][:, :], func=AF.Copy),
                 reads=[PB[tt]], writes=mean_sb.bufs)
            P.op("act", lambda e, tt=tt, sl_=sl_: e.activation(out=var_sb.ap[:, sl_], in_=banks[tt][:, :], func=AF.Square),
                 reads=[PB[tt]], writes=var_sb.bufs)
            P.op("dve", lambda e, tt=tt, sl_=sl_: e.tensor_tensor(out=var_sb.ap[:, sl_], in0=banks[2 + tt][:, :], in1=var_sb.ap[:, sl_],
                                                                  op=ALU.subtract),
                 reads=[PB[2 + tt]] + var_sb.bufs, writes=var_sb.bufs)
        P.op("act", lambda e: e.activation(out=var_sb.ap, in_=var_sb.ap, func=AF.Sqrt, bias=epsl, scale=1.0),
             reads=var_sb.bufs + [B_EPS], writes=var_sb.bufs)
        P.op("dve", lambda e: e.reciprocal(out=var_sb.ap, in_=var_sb.ap), reads=var_sb.bufs, writes=var_sb.bufs)
        P.op("dve", lambda e: e.scalar_tensor_tensor(out=mean_sb.ap, in0=mean_sb.ap, scalar=-1.0, in1=var_sb.ap,
                                                     op0=ALU.mult, op1=ALU.mult),
             reads=mean_sb.bufs + var_sb.bufs, writes=mean_sb.bufs)


    def ln_chunk(j):
        z = z_sb
        P.op("pool", lambda e: e.tensor_tensor(out=z.ap, in0=vpre[:, j, :], in1=var_sb.ap, op=ALU.mult),
             reads=[B_VPRE] + var_sb.bufs, writes=z.bufs)
        P.op("dve", lambda e: e.tensor_tensor(out=z.ap, in0=z.ap, in1=mean_sb.ap, op=ALU.add),
             reads=z.bufs + mean_sb.bufs, writes=z.bufs)
        P.op("act", lambda e: e.activation(out=vT[:, j, :], in_=z.ap, func=AF.Silu,
                                           bias=prm[:, P_LNB + j:P_LNB + j + 1], scale=prm[:, P_LNG + j:P_LNG + j + 1]),
             reads=z.bufs + [B_PRM], writes=[B_VT] + xs_guard("vT"))

    ah = Scr(12288, 4224)
    ca = Scr(16896, 4224)
    acc3 = Scr(21504, 4096)
    for j in range(NJ):
        def cons_h(b, c0, n):
            P.op("act", lambda e: e.activation(out=ah.ap[:, c0:c0 + n], in_=banks[b][:, 0:n], func=AF.Copy),
                 reads=[PB[b]], writes=ah.bufs)
        inproj_ext("ah", j, cons_h)

        def cons_c(b, c0, n):
            P.op("dve", lambda e: e.tensor_tensor(out=ca.ap[:, c0:c0 + n], in0=banks[b][:, 0:n], in1=ah.ap[:, c0:c0 + n], op=ALU.mult),
                 reads=[PB[b]] + ah.bufs, writes=ca.bufs)
        inproj_ext("ac", j, cons_c)

        for k in range(3):
            src = ca.ap[:, HALO - 2 + k:HALO - 2 + k + T]
            wk = prm[:, P_CA + j * 3 + k:P_CA + j * 3 + k + 1]
            if k == 0:
                P.op("dve", lambda e, src=src, wk=wk: e.tensor_scalar(out=acc3.ap, in0=src, scalar1=wk, scalar2=None, op0=ALU.mult),
                     reads=ca.bufs + [B_PRM], writes=acc3.bufs)
            else:
                P.op("dve", lambda e, src=src, wk=wk: e.scalar_tensor_tensor(out=acc3.ap, in0=src, scalar=wk, in1=acc3.ap,
                                                                           op0=ALU.mult, op1=ALU.add),
                     reads=ca.bufs + [B_PRM] + acc3.bufs, writes=acc3.bufs)

        name, q = pos_in[("ab", j)]
        s, = use(name)
        lhs = w_in_chunk(s, q)
        for tt in range(2):
            b = next_bank()
            mm_group(banks[b][:, :], [(lhs[kc], nT1[:, kc, HALO + tt * 512:HALO + (tt + 1) * 512]) for kc in range(16)],
                     reads=[RS[s], B_NT], writes=[PB[b]])
            P.op("dve", lambda e, b=b, j=j, tt=tt: e.tensor_tensor(out=yaT[:, j, tt * 512:(tt + 1) * 512], in0=banks[b][:, :],
                                                                 in1=acc3.ap[:, tt * 512:(tt + 1) * 512], op=ALU.mult),
                 reads=[PB[b]] + acc3.bufs, writes=[B_YA] + xs_guard("ya"))
        if j == 0:
            ln_stats()
        else:
            ln_chunk(j - 1)
    ln_chunk(NJ - 1)

    if stop == "vT":
        return dump_bf(vT, 2, 4, [B_VT])
    if stop == "ya":
        return dump_bf(yaT, 2, 4, [B_YA])

    sga_ = [Scr(0, 2048), Scr(2048, 2048)]
    sgb2_ = [Scr(4096, 2048), Scr(6144, 2048)]
    t1_ = [Scr(8192, 2048), Scr(10240, 2048)]
    t2_ = [Scr(12288, 2048), Scr(14336, 2048)]
    it = 0
    first_mt = True
    x0s = Scr(16384, 8192)
    P.op("sp", lambda e: e.dma_start(out=x0s.ap, in_=x_d[HALO:HALO + 128, :]), writes=x0s.bufs, dma=x0s.bufs[0])
    for c in range(16):
        s, = use(f"G{c}")
        woa = ring[s][:, 0:1024].rearrange("p (k c) -> p k c", k=8)
        wpw = ring[s][:, 1024:2048].rearrange("p (k c) -> p k c", k=8)
        wga = ring[s][:, 2048:4096].rearrange("p (k c) -> p k c", k=16)
        wgb = ring[s][:, 4096:6144].rearrange("p (k c) -> p k c", k=16)
        for tt in range(2):
            sga, sgb2, t1, t2 = sga_[it % 2], sgb2_[it % 2], t1_[it % 2], t2_[it % 2]
            it += 1
            ts_ = slice(tt * 512, (tt + 1) * 512)
            te_ = slice(HALO + tt * 512, HALO + (tt + 1) * 512)
            b_ga = next_bank()
            mm_group(banks[b_ga][:, :], [(wga[:, kc, :], nT1[:, kc, te_]) for kc in range(16)], reads=[RS[s], B_NT], writes=[PB[b_ga]])
            P.op("act", lambda e, b=b_ga, sga=sga: e.activation(out=sga.ap, in_=banks[b][:, :], func=AF.Sigmoid),
                 reads=[PB[b_ga]], writes=sga.bufs)
            b_ya = next_bank()
            mm_group(banks[b_ya][:, :], [(woa[:, kc, :], yaT[:, kc, ts_]) for kc in range(8)], reads=[RS[s], B_YA], writes=[PB[b_ya]])
            P.op("dve", lambda e, b=b_ya, sga=sga, t1=t1: e.tensor_tensor(out=t1.ap, in0=banks[b][:, :], in1=sga.ap, op=ALU.mult),
                 reads=[PB[b_ya]] + sga.bufs, writes=t1.bufs)
            b_gb = next_bank()
            mm_group(banks[b_gb][:, :], [(wgb[:, kc, :], nT1[:, kc, te_]) for kc in range(16)], reads=[RS[s], B_NT], writes=[PB[b_gb]])
            P.op("act", lambda e, b=b_gb, sgb2=sgb2: e.activation(out=sgb2.ap, in_=banks[b][:, :], func=AF.Sigmoid),
                 reads=[PB[b_gb]], writes=sgb2.bufs)
            b_yb = next_bank()
            mm_group(banks[b_yb][:, :], [(wpw[:, kc, :], vT[:, kc, ts_]) for kc in range(8)], reads=[RS[s], B_VT], writes=[PB[b_yb]])
            P.op("dve", lambda e, b=b_yb, sgb2=sgb2, t2=t2, c=c: e.scalar_tensor_tensor(
                out=t2.ap, in0=banks[b][:, :], scalar=prm[:, P_BPW + c:P_BPW + c + 1], in1=sgb2.ap, op0=ALU.add, op1=ALU.mult),
                reads=[PB[b_yb], B_PRM] + sgb2.bufs, writes=t2.bufs)
            extra = [B_VPRE] if first_mt else []
            first_mt = False
            P.op("pool", lambda e, t1=t1, t2=t2, c=c, ts_=ts_: e.tensor_tensor(out=mT[:, c, ts_], in0=t1.ap, in1=t2.ap, op=ALU.add),
                 reads=t1.bufs + t2.bufs, writes=B_TT[4 * tt:4 * tt + 4] + extra)

    if stop == "mT":
        return dump_bf(mT, 4, 4, B_TT)

    P.op("sp", lambda e: e.nop(), writes=[B_NT, B_YA, B_VT])
    for t in range(1, NT_TILES):
        P.op("sp", lambda e, t=t: e.dma_start(out=h[:, t, :], in_=x_d[HALO + t * 128:HALO + (t + 1) * 128, :]),
             writes=[B_HT[t]], dma=B_HT[t])

    ntc = ntb + [Scr(24576, 4096, BF16)]

    class NormPipe:
        def __init__(self, col0):
            self.col0 = col0
            self.q1 = []
            self.q2 = []

        def push(self, t):
            norm_p1(h[:, t, :], [B_HT[t]], self.col0 + t, 128, ntc[t % 3])
            if self.q2:
                self._tr(self.q2.pop(0))
            if self.q1:
                t1 = self.q1.pop(0)
                norm_p2(h[:, t1, :], [B_HT[t1]], self.col0 + t1, 128, ntc[t1 % 3])
                self.q2.append(t1)
            self.q1.append(t)

        def _tr(self, t):
            transposes(ntc[t % 3], [nT2[:, 0:8, t * 128:(t + 1) * 128], nT2[:, 8:16, t * 128:(t + 1) * 128]], [B_TT[t]],
                       engs=("act", "act"))

        def flush(self):
            while self.q1 or self.q2:
                if self.q2:
                    self._tr(self.q2.pop(0))
                if self.q1:
                    t1 = self.q1.pop(0)
                    norm_p2(h[:, t1, :], [B_HT[t1]], self.col0 + t1, 128, ntc[t1 % 3])
                    self.q2.append(t1)

    np2 = NormPipe(9)
    for n in range(4):
        s, = use(f"O{n}", after=([B_HT[7]] if n == 0 else ()))
        wv = ring[s].rearrange("p (k c) -> p k c", k=16)
        for t in range(NT_TILES):
            b = next_bank()
            mm_group(banks[b][:, :], [(mT[:, kc, t * 128:(t + 1) * 128], wv[:, kc, :]) for kc in range(16)],
                     reads=[RS[s], B_TT[t]], writes=[PB[b]])
            if t == 0:
                P.op("dve", lambda e, b=b, n=n: e.tensor_tensor(out=h[:, 0, n * 512:(n + 1) * 512], in0=x0s.ap[:, n * 512:(n + 1) * 512],
                                                              in1=banks[b][:, :], op=ALU.add),
                     reads=[PB[b]] + x0s.bufs, writes=[B_HT[0]])
            else:
                P.op("dve", lambda e, b=b, t=t, n=n: e.tensor_tensor(out=h[:, t, n * 512:(n + 1) * 512], in0=h[:, t, n * 512:(n + 1) * 512],
                                                                   in1=banks[b][:, :], op=ALU.add),
                     reads=[PB[b], B_HT[t]], writes=[B_HT[t]])
            if n == 3 and stop != "h1":
                np2.push(t)
    if stop == "h1":
        return dump_h()
    np2.flush()
    P.op("sp", lambda e: e.dma_start(out=gb, in_=g_d[2:3, :].partition_broadcast(128)), writes=[B_GB], dma=B_GB)

    fT_ = [Scr(0, 8192, BF16), Scr(8192, 8192, BF16)]
    sl_ = [Scr(24576, 2048), Scr(26624, 2048)]
    it = 0
    np3 = NormPipe(17)
    for blk in range(NFB):
        fT = fT_[blk % 2]
        f3 = fT.ap.rearrange("p (k t) -> p k t", k=4)
        sg_, su_ = use(f"FG{blk}", f"FU{blk}")
        wg = ring[sg_].rearrange("p (k c) -> p k c", k=16)
        wu = ring[su_].rearrange("p (k c) -> p k c", k=16)
        for cc in range(4):
            for tt in range(2):
                sl = sl_[it % 2]
                it += 1
                ts_ = slice(tt * 512, (tt + 1) * 512)
                bg = next_bank()
                mm_group(banks[bg][:, :], [(wg[:, kc, cc * 128:(cc + 1) * 128], nT2[:, kc, ts_]) for kc in range(16)],
                         reads=[RS[sg_]] + B_TT[4 * tt:4 * tt + 4], writes=[PB[bg]])
                P.op("act", lambda e, b=bg, sl=sl: e.activation(out=sl.ap, in_=banks[b][:, :], func=AF.Silu),
                     reads=[PB[bg]], writes=sl.bufs)
                bu = next_bank()
                mm_group(banks[bu][:, :], [(wu[:, kc, cc * 128:(cc + 1) * 128], nT2[:, kc, ts_]) for kc in range(16)],
                         reads=[RS[su_]] + B_TT[4 * tt:4 * tt + 4], writes=[PB[bu]])
                P.op("dve", lambda e, b=bu, sl=sl, f3=f3, cc=cc, ts_=ts_: e.tensor_tensor(out=f3[:, cc, ts_], in0=banks[b][:, :], in1=sl.ap, op=ALU.mult),
                     reads=[PB[bu]] + sl.bufs, writes=fT.bufs)
        sd_, = use(f"FD{blk}")
        wd = ring[sd_].rearrange("p (k c) -> p k c", k=4)
        for t in range(NT_TILES):
            for n in range(4):
                b = next_bank()
                mm_group(banks[b][:, :], [(f3[:, kc, t * 128:(t + 1) * 128], wd[:, kc, n * 512:(n + 1) * 512]) for kc in range(4)],
                         reads=[RS[sd_]] + fT.bufs, writes=[PB[b]])
                P.op("dve", lambda e, b=b, t=t, n=n: e.tensor_tensor(out=h[:, t, n * 512:(n + 1) * 512], in0=h[:, t, n * 512:(n + 1) * 512],
                                                                   in1=banks[b][:, :], op=ALU.add),
                     reads=[PB[b], B_HT[t]], writes=[B_HT[t]])
            if blk == NFB - 1 and stop != "h2":
                np3.push(t)
    if stop == "h2":
        return dump_h()
    np3.flush()
    P.op("sp", lambda e: e.dma_start(out=gb, in_=g_d[3:4, :].partition_broadcast(128)), writes=[B_GB], dma=B_GB)

    sgp_ = [Scr(0, 2048), Scr(2048, 2048)]
    tp_ = [Scr(4096, 2048), Scr(6144, 2048)]
    ost_ = [Scr(8192, 8192), Scr(16384, 8192)]
    fq1, fq2 = [], []
    fjunk = Scr(24576, 4096, BF16)

    def fin_p2(t):
        ost = ost_[t % 2]
        norm_p2(h[:, t, :], [B_HT[t]], 25 + t, 128, ost, out_ap=ost.ap)
        P.op("sp", lambda e, t=t, ost=ost: e.dma_start(out=out_d[t * 128:(t + 1) * 128, :], in_=ost.ap),
             reads=ost.bufs, dma=ost.bufs[0])

    def fin_push(t):
        if fq2:
            fin_p2(fq2.pop(0))
        if fq1:
            t1 = fq1.pop(0)
            norm_p1(h[:, t1, :], [B_HT[t1]], 25 + t1, 128, ost_[t1 % 2], out_ap=ost_[t1 % 2].ap, junk=fjunk)
            fq2.append(t1)
        fq1.append(t)

    def fin_flush():
        while fq1 or fq2:
            if fq2:
                fin_p2(fq2.pop(0))
            if fq1:
                t1 = fq1.pop(0)
                norm_p1(h[:, t1, :], [B_HT[t1]], 25 + t1, 128, ost_[t1 % 2], out_ap=ost_[t1 % 2].ap, junk=fjunk)
                fq2.append(t1)

    it = 0

    def ple_group(n, t, spp, s_):
        nonlocal it
        wpp = ring[spp][:, 0:1024].rearrange("p (k c) -> p k c", k=2)
        wv = ring[s_].rearrange("p (k c) -> p k c", k=16)
        sgp, tp = sgp_[it % 2], tp_[it % 2]
        it += 1
        bgt = next_bank()
        mm_group(banks[bgt][:, :], [(nT2[:, kc, t * 128:(t + 1) * 128], wv[:, kc, :]) for kc in range(16)],
                 reads=[RS[s_], B_TT[t]], writes=[PB[bgt]])
        P.op("act", lambda e, b=bgt, sgp=sgp: e.activation(out=sgp.ap, in_=banks[b][:, :], func=AF.Sigmoid),
             reads=[PB[bgt]], writes=sgp.bufs)
        bp = next_bank()
        mm_group(banks[bp][:, :], [(pT[:, ec, t * 128:(t + 1) * 128], wpp[:, ec, :]) for ec in range(2)],
                 reads=[RS[spp], B_PT], writes=[PB[bp]])
        P.op("dve", lambda e, b=bp, sgp=sgp, tp=tp: e.tensor_tensor(out=tp.ap, in0=banks[b][:, :], in1=sgp.ap, op=ALU.mult),
             reads=[PB[bp]] + sgp.bufs, writes=tp.bufs)
        P.op("dve" if n >= 2 else "pool",
             lambda e, tp=tp, t=t, n=n: e.tensor_tensor(out=h[:, t, n * 512:(n + 1) * 512], in0=h[:, t, n * 512:(n + 1) * 512],
                                                       in1=tp.ap, op=ALU.add),
             reads=tp.bufs + [B_HT[t]], writes=[B_HT[t]])

    for n in range(2):
        spp, s_ = use(f"PP{n}", f"PG{n}")
        for t in range(NT_TILES):
            ple_group(n, t, spp, s_)
    spp2, s2, spp3, s3 = use("PP2", "PG2", "PP3", "PG3")
    SKEW = 3
    for i in range(NT_TILES + SKEW):
        if i < NT_TILES:
            ple_group(2, i, spp2, s2)
        if i - SKEW >= 0:
            ple_group(3, i - SKEW, spp3, s3)
            fin_push(i - SKEW)
    fin_flush()
    P.op("sp", lambda e: e.nop(), writes=ost_[0].bufs + ost_[1].bufs)
    P.emit()
    return nc


_NC_CACHE = {}


def _get_nc(stop=None):
    if stop not in _NC_CACHE:
        _NC_CACHE[stop] = build(stop)
    return _NC_CACHE[stop]


def _prep_inputs(x, p, g_mix, w_in, conv_a_w, w_out_a, b_glu, conf_dw_w, conf_dw_b, conf_ln_g, conf_ln_b,
                 w_pw_b, b_pw_b, w_o, g_ffn, w_gate, w_up, w_down, g_ple, w_ple_gate, w_ple_proj, g_final):
    f = lambda a: np.asarray(a, np.float32)
    x2 = f(x).reshape(SEQ, D)
    p2 = f(p).reshape(SEQ, PLE)
    wflat = _prep_weights(f(w_in)[0], f(w_out_a)[0], f(w_pw_b)[0], f(w_o)[0], f(w_gate)[0], f(w_up)[0], f(w_down)[0],
                          f(w_ple_gate)[0], f(w_ple_proj)[0])
    prm = _prep_prm(f(conv_a_w)[0], f(b_glu)[0], f(conf_dw_w)[0], f(conf_dw_b)[0], f(conf_ln_g)[0], f(conf_ln_b)[0], f(b_pw_b)[0])
    gains = np.ascontiguousarray(np.stack([f(g_mix)[0], f(g_ffn)[0], f(g_ple)[0], f(g_final)]))
    xpad = np.concatenate([np.zeros((HALO, D), np.float32), x2], axis=0)
    in_maps = []
    for c in range(NCORES):
        in_maps.append({
            "x_ext": np.ascontiguousarray(xpad[c * T:c * T + TE]),
            "p_c": np.ascontiguousarray(p2[c * T:(c + 1) * T].reshape(NT_TILES, 128, PLE).transpose(1, 0, 2)).reshape(128, NT_TILES * PLE),
            "hmask": np.full((128, HALO), 0.0 if c == 0 else 1.0, np.float32),
            "wflat": wflat,
            "prm": prm,
            "gains": gains,
        })
    return in_maps


def kernel(**inputs):
    in_maps = _prep_inputs(**inputs)
    nc = _get_nc(None)
    res = run_bass_kernel_spmd(nc, in_maps, core_ids=list(range(NCORES)))
    out = np.concatenate([np.asarray(r["out"], np.float32) for r in res.results], axis=0)
    return out.reshape(1, SEQ, D)
```

```python
import numpy as np
import concourse.bass as bass
import concourse.mybir as mybir
from concourse.bass_utils import run_bass_kernel_spmd

F32 = mybir.dt.float32
BF16 = mybir.dt.bfloat16
AF = mybir.ActivationFunctionType
ALU = mybir.AluOpType

NCORES = 8
D = 2048
SEQ = 8192
T = SEQ // NCORES
HALO = 32
TE = T + HALO
NT_TILES = T // 128
KC = D // 128
CW = 1024
NJ = CW // 128
DFF = 5632
NFB = DFF // 512
PLE = 256
EPS = 1e-6
LN_EPS = 1e-5
SLOT = 8192
NSLOT = 4

COMPUTE = ("pe", "act", "dve", "pool")
ENGS = ("pe", "act", "dve", "pool", "sp")


class Buf:
    __slots__ = ("name", "writer", "readers", "sem", "cnt")

    def __init__(self, name):
        self.name = name
        self.writer = None
        self.readers = []
        self.sem = None
        self.cnt = 0


class Op:
    __slots__ = ("eng", "fn", "deps", "idx", "needed", "val", "dma_buf", "dma_val", "is_dma")


class Prog:
    def __init__(self, nc):
        self.nc = nc
        self.streams = {e: [] for e in ENGS}
        self.dma_bufs = []

    def op(self, eng, fn, reads=(), writes=(), dma=None):
        o = Op()
        o.eng = eng
        o.fn = fn
        o.needed = False
        o.val = None
        o.is_dma = dma is not None
        o.dma_buf = dma
        o.dma_val = None
        deps = []
        for b in reads:
            if b.writer is not None:
                deps.append(b.writer)
            b.readers.append(o)
        for b in writes:
            if b.writer is not None:
                deps.append(b.writer)
            deps.extend(r for r in b.readers if r is not o)
            b.writer = o
            b.readers = []
        if dma is not None:
            if dma.sem is None:
                self.dma_bufs.append(dma)
                dma.sem = True
            dma.cnt += 16
            o.dma_val = dma.cnt
        o.deps = [d for d in deps if d is not o and not (eng == "pe" and d.eng == "pe" and not d.is_dma)]
        o.idx = len(self.streams[eng])
        self.streams[eng].append(o)
        return o

    def emit(self):
        nc = self.nc
        for e in ENGS:
            for o in self.streams[e]:
                for d in o.deps:
                    if not d.is_dma:
                        d.needed = True
        for e in COMPUTE:
            c = 0
            for o in self.streams[e]:
                if o.needed and not o.is_dma:
                    c += 1
                    o.val = c
        sems = {e: nc.alloc_semaphore("S_" + e) for e in COMPUTE}
        for b in self.dma_bufs:
            b.sem = nc.alloc_semaphore("D_" + b.name)
        streams = self.streams

        def run(e, eng):
            waited = {}
            for o in streams[e]:
                w = {}
                for d in o.deps:
                    if d.is_dma:
                        s, v = d.dma_buf.sem, d.dma_val
                    else:
                        s, v = sems[d.eng], d.val
                    k = id(s)
                    if k not in w or w[k][1] < v:
                        w[k] = (s, v)
                for k, (s, v) in w.items():
                    if waited.get(k, 0) >= v:
                        continue
                    waited[k] = v
                    eng.wait_ge(s, v)
                ins = o.fn(eng)
                if o.is_dma:
                    ins.then_inc(o.dma_buf.sem, 16)
                elif o.needed:
                    ins.then_inc(sems[e], 1)

        with nc.Block() as block:
            @block.tensor
            def _(eng):
                run("pe", eng)

            @block.scalar
            def _(eng):
                run("act", eng)

            @block.vector
            def _(eng):
                run("dve", eng)

            @block.gpsimd
            def _(eng):
                run("pool", eng)

            @block.sync
            def _(eng):
                run("sp", eng)


def _kc(w):
    k, c = w.shape
    return np.ascontiguousarray(w.reshape(k // 128, 128, c).transpose(1, 0, 2)).reshape(128, -1)


def _ch(group, j):
    base = {"ah": 0, "ab": 8, "ac": 16, "gv": 24, "gg": 32, "ga": 40, "gb": 56}[group]
    return base + j


B_ORDER = []
for _i in range(0, NJ, 2):
    B_ORDER += [("gg", _i), ("gv", _i), ("gg", _i + 1), ("gv", _i + 1)]
A_ORDER = []
for _j in range(NJ):
    A_ORDER += [("ah", _j), ("ac", _j), ("ab", _j)]


def _load_plan():
    plan = []
    for i in range(4):
        plan.append((f"B{i}", 16 * 512))
    for i in range(6):
        plan.append((f"A{i}", 16 * 512))
    for c in range(16):
        plan.append((f"G{c}", 6144))
    for n in range(4):
        plan.append((f"O{n}", 16 * 512))
    for b in range(NFB):
        plan.append((f"FG{b}", 16 * 512))
        plan.append((f"FU{b}", 16 * 512))
        if b >= 1:
            plan.append((f"FD{b - 1}", 4 * 2048))
    plan.append((f"FD{NFB - 1}", 4 * 2048))
    for n in range(4):
        plan.append((f"PP{n}", 2 * 512))
        plan.append((f"PG{n}", 16 * 512))
    return plan


PLAN = _load_plan()
WTOT = sum(n for _, n in PLAN)


def _prep_weights(w_in, w_out_a, w_pw_b, w_o, w_gate, w_up, w_down, w_ple_gate, w_ple_proj):
    wflat = np.empty((128, WTOT), np.float32)
    pos = 0

    def put(a):
        nonlocal pos
        n = a.shape[1]
        wflat[:, pos:pos + n] = a
        pos += n

    def cols(chunks):
        idx = np.concatenate([np.arange(_ch(g, j) * 128, _ch(g, j) * 128 + 128) for g, j in chunks])
        return _kc(w_in[:, idx])

    for i in range(4):
        put(cols(B_ORDER[4 * i:4 * i + 4]))
    for i in range(6):
        put(cols(A_ORDER[4 * i:4 * i + 4]))
    for c in range(16):
        put(_kc(w_out_a[:, c * 128:(c + 1) * 128]))
        put(_kc(w_pw_b[:, c * 128:(c + 1) * 128]))
        put(cols([("ga", c)]))
        put(cols([("gb", c)]))
    for n in range(4):
        put(_kc(w_o[:, n * 512:(n + 1) * 512]))
    for b in range(NFB):
        put(_kc(w_gate[:, b * 512:(b + 1) * 512]))
        put(_kc(w_up[:, b * 512:(b + 1) * 512]))
        if b >= 1:
            put(_kc(w_down[(b - 1) * 512:b * 512, :]))
    put(_kc(w_down[(NFB - 1) * 512:NFB * 512, :]))
    for n in range(4):
        put(_kc(w_ple_proj[:, n * 512:(n + 1) * 512]))
        put(_kc(w_ple_gate[:, n * 512:(n + 1) * 512]))
    assert pos == WTOT
    return wflat


P_BV, P_BG, P_CA, P_CW, P_DWB, P_LNG, P_LNB, P_BPW = 0, 8, 16, 40, 288, 296, 304, 312
NPRM = 328


def _fm(v, n):
    return np.ascontiguousarray(np.asarray(v, np.float32).reshape(n, 128).T)


def _prep_prm(conv_a_w, b_glu, conf_dw_w, conf_dw_b, conf_ln_g, conf_ln_b, b_pw_b):
    prm = np.zeros((128, NPRM), np.float32)
    prm[:, P_BV:P_BV + 8] = _fm(b_glu[:CW], 8)
    prm[:, P_BG:P_BG + 8] = _fm(b_glu[CW:], 8)
    prm[:, P_CA:P_CA + 24] = np.ascontiguousarray(conv_a_w.reshape(3, 8, 128).transpose(2, 1, 0)).reshape(128, 24)
    prm[:, P_CW:P_CW + 248] = np.ascontiguousarray(conf_dw_w.reshape(31, 8, 128).transpose(2, 1, 0)).reshape(128, 248)
    prm[:, P_DWB:P_DWB + 8] = _fm(conf_dw_b, 8)
    prm[:, P_LNG:P_LNG + 8] = _fm(conf_ln_g, 8)
    prm[:, P_LNB:P_LNB + 8] = _fm(conf_ln_b, 8)
    prm[:, P_BPW:P_BPW + 16] = _fm(b_pw_b, 16)
    return prm


def build(stop=None):
    nc = bass.Bass("TRN2", target_bir_lowering=False)
    x_d = nc.dram_tensor("x_ext", [TE, D], F32, kind="ExternalInput").ap()
    p_d = nc.dram_tensor("p_c", [128, NT_TILES * PLE], F32, kind="ExternalInput").ap()
    hm_d = nc.dram_tensor("hmask", [128, HALO], F32, kind="ExternalInput").ap()
    w_d = nc.dram_tensor("wflat", [128, WTOT], F32, kind="ExternalInput").ap()
    prm_d = nc.dram_tensor("prm", [128, NPRM], F32, kind="ExternalInput").ap()
    g_d = nc.dram_tensor("gains", [4, D], F32, kind="ExternalInput").ap()
    out_d = nc.dram_tensor("out", [T, D], F32, kind="ExternalOutput").ap()
    dbg_d = None
    if stop is not None:
        dbg_d = nc.dram_tensor("dbg", [128, 16384], F32, kind="ExternalOutput").ap()

    NB4 = 52992
    big = nc.alloc_sbuf_tensor("big", [128, NB4], F32)

    def reg(off, nbytes, dt=F32):
        assert off % 4 == 0 and nbytes % 4 == 0 and (off + nbytes) <= NB4 * 4, (off, nbytes)
        v = big[:, off // 4:(off + nbytes) // 4]
        return v.bitcast(BF16) if dt == BF16 else v

    ring = [reg(i * 16384, 16384, BF16) for i in range(NSLOT)]
    O_NT, O_YA, O_VT, O_H, O_VPRE, O_GB, O_SM, O_SC = 65536, 99328, 115712, 65536, 132096, 164864, 173056, 181248
    SC_SIZE = NB4 * 4 - O_SC
    nT1 = reg(O_NT, 16 * TE * 2, BF16).rearrange("p (k t) -> p k t", k=16)
    yaT = reg(O_YA, 16384, BF16).rearrange("p (k t) -> p k t", k=8)
    vT = reg(O_VT, 16384, BF16).rearrange("p (k t) -> p k t", k=8)
    xs = reg(O_YA, 65536).rearrange("p (t d) -> p t d", t=8)
    h = reg(O_H, 65536).rearrange("p (t d) -> p t d", t=8)
    vpre = reg(O_VPRE, 32768).rearrange("p (k t) -> p k t", k=8)
    mT = reg(O_VPRE, 32768, BF16).rearrange("p (k t) -> p k t", k=16)
    nT2 = mT
    gb = reg(O_GB, 8192)
    prm = reg(O_SM, 1312)
    hmask = reg(O_SM + 1344, 128)
    ident = reg(O_SM + 1472, 256, BF16)
    ones32 = reg(O_SM + 1728, 512)
    ssq = reg(O_SM + 2240, 160)
    rstd = reg(O_SM + 2400, 160)
    stmp = reg(O_SM + 2560, 160)
    epsr = reg(O_SM + 2720, 4)
    epsl = reg(O_SM + 2724, 4)
    identf = reg(O_SM + 2752, 512)
    pT = reg(O_SM + 3328, 4096, BF16).rearrange("p (k t) -> p k t", k=2)
    mhalf1 = reg(O_SM + 7424, 4)

    banks = [nc.alloc_psum_tensor(f"bank{i}", [128, 512], F32) for i in range(8)]
    banks_bf = [b[:].bitcast(BF16) for b in banks]

    P = Prog(nc)
    RS = [Buf(f"ring{i}") for i in range(NSLOT)]
    PB = [Buf(f"pb{i}") for i in range(8)]
    PAGE = 512
    SCP = [Buf(f"sc{i}") for i in range(SC_SIZE // PAGE + 1)]
    B_NT, B_YA, B_VT, B_VPRE, B_GB = Buf("nT"), Buf("ya"), Buf("vT"), Buf("vpre"), Buf("gb")
    B_TT = [Buf(f"tt{t}") for t in range(8)]
    B_HT = [Buf(f"h{t}") for t in range(8)]
    B_XS = [Buf(f"xs{t}") for t in range(8)]
    B_SS = [Buf(f"ss{i}") for i in range(40)]
    B_PRM, B_CONST, B_PT = Buf("prm"), Buf("const"), Buf("pT")
    B_OUT = Buf("outst")
    B_ID, B_IDF, B_ONES, B_EPS = Buf("ident"), Buf("identf"), Buf("ones"), Buf("eps")

    class Scr:
        def __init__(self, off, nbytes, dt=F32):
            assert off + nbytes <= SC_SIZE, (off, nbytes, SC_SIZE)
            self.ap = reg(O_SC + off, nbytes, dt)
            self.bufs = SCP[off // PAGE:(off + nbytes - 1) // PAGE + 1]

    wstate = {"next": 0, "off": 0}
    loaded = {}

    def issue_load(after=()):
        i = wstate["next"]
        if i >= len(PLAN):
            return
        name, n = PLAN[i]
        s = i % NSLOT
        off = wstate["off"]
        P.op("pool", lambda e, s=s, off=off, n=n: e.dma_start(out=ring[s][:, 0:n], in_=w_d[:, off:off + n]),
             reads=list(after), writes=[RS[s]], dma=RS[s])
        loaded[name] = s
        wstate["next"] = i + 1
        wstate["off"] = off + n

    PIDX = {nm: i for i, (nm, _) in enumerate(PLAN)}

    def use(*names, after=()):
        k0 = min(PIDX[n] for n in names)
        while wstate["next"] < len(PLAN) and wstate["next"] <= k0 + NSLOT - 1:
            issue_load(after)
        return [loaded[n] for n in names]

    rr = {"i": 0, "t": 0}

    def next_bank():
        b = rr["i"] % 6
        rr["i"] += 1
        return b

    def next_tbank():
        b = 6 + rr["t"] % 2
        rr["t"] += 1
        return b

    def mm_group(out_ap, pairs, reads, writes):
        def fn(e):
            n = len(pairs)
            r = None
            for i, (l, rh) in enumerate(pairs):
                r = e.matmul(out_ap, lhsT=l, rhs=rh, start=(i == 0), stop=(i == n - 1))
            return r
        return P.op("pe", fn, reads=reads, writes=writes)

    def dump_bf(view3, nq, kq, reads):
        cv = Scr(0, 16384)
        for q in range(nq):
            P.op("act", lambda e, q=q: e.activation(out=cv.ap.rearrange("p (k t) -> p k t", k=4),
                                                    in_=view3[:, 4 * q:4 * q + 4, :], func=AF.Copy),
                 reads=reads, writes=cv.bufs)
            P.op("sp", lambda e, q=q: e.dma_start(out=dbg_d[:, q * 4096:(q + 1) * 4096], in_=cv.ap), reads=cv.bufs, dma=B_OUT)
        P.op("sp", lambda e: e.nop(), writes=[B_OUT] + cv.bufs)
        P.emit()
        return nc

    def dump_h():
        for t in range(8):
            P.op("sp", lambda e, t=t: e.dma_start(out=dbg_d[:, t * 2048:(t + 1) * 2048], in_=h[:, t, :]), reads=[B_HT[t]], dma=B_OUT)
        P.op("sp", lambda e: e.nop(), writes=[B_OUT])
        P.emit()
        return nc

    pst = Scr(8192, 8192)
    P.op("sp", lambda e: e.dma_start(out=pst.ap, in_=p_d), writes=pst.bufs, dma=pst.bufs[0])
    xh = Scr(0, 8192)
    P.op("sp", lambda e: e.dma_start(out=xh.ap[0:HALO, :], in_=x_d[0:HALO, :]), writes=xh.bufs, dma=xh.bufs[0])
    P.op("sp", lambda e: e.dma_start(out=gb, in_=g_d[0:1, :].partition_broadcast(128)), writes=[B_GB], dma=B_GB)
    for t in range(NT_TILES):
        P.op("sp", lambda e, t=t: e.dma_start(out=xs[:, t, :], in_=x_d[HALO + t * 128:HALO + (t + 1) * 128, :]),
             writes=[B_XS[t]], dma=B_XS[t])
    P.op("sp", lambda e: e.dma_start(out=prm, in_=prm_d), writes=[B_PRM], dma=B_PRM)
    P.op("sp", lambda e: e.dma_start(out=hmask, in_=hm_d), writes=[B_CONST], dma=B_CONST)

    P.op("pool", lambda e: e.memset(identf, 0.0), writes=[B_IDF])
    P.op("pool", lambda e: e.affine_select(out=identf, in_=identf, pattern=[[-1, 128]], compare_op=ALU.not_equal,
                                           fill=1.0, base=0, channel_multiplier=1), reads=[B_IDF], writes=[B_IDF])
    P.op("pool", lambda e: e.memset(ones32, 1.0 / CW), writes=[B_ONES])

    def mk_eps(e):
        e.memset(epsr, EPS)
        e.memset(mhalf1, -0.5)
        return e.memset(epsl, LN_EPS)
    P.op("pool", mk_eps, writes=[B_EPS])
    P.op("pool", lambda e: e.memset(ssq, 0.0), writes=B_SS)
    P.op("dve", lambda e: e.tensor_copy(out=ident, in_=identf), reads=[B_IDF], writes=[B_ID])


    ntb = [Scr(16384, 4096, BF16), Scr(20480, 4096, BF16)]

    def norm_p1(src_ap, src_bufs, col, npart, ntile, out_ap=None, junk=None):
        dst = ntile.ap[0:npart, :] if out_ap is None else out_ap
        jb = ntile.bufs
        if junk is not None:
            dst, jb = junk.ap, junk.bufs
        sb = B_SS[col]
        P.op("act", lambda e: e.activation(out=dst, in_=src_ap, func=AF.Square, accum_out=ssq[0:npart, col:col + 1]),
             reads=src_bufs, writes=jb + [sb])
        P.op("pool", lambda e: e.tensor_scalar(out=stmp[0:npart, col:col + 1], in0=ssq[0:npart, col:col + 1],
                                              scalar1=1.0 / D, scalar2=EPS, op0=ALU.mult, op1=ALU.add),
             reads=[sb], writes=[sb])
        P.op("pool", lambda e: e.tensor_tensor(out=rstd[0:npart, col:col + 1], in0=stmp[0:npart, col:col + 1],
                                               in1=mhalf1[0:npart, :], op=ALU.pow),
             reads=[sb, B_EPS], writes=[sb])

    def norm_p2(src_ap, src_bufs, col, npart, ntile, out_ap=None):
        dst = ntile.ap[0:npart, :] if out_ap is None else out_ap
        sb = B_SS[col]
        P.op("dve", lambda e: e.scalar_tensor_tensor(out=dst, in0=src_ap, scalar=rstd[0:npart, col:col + 1],
                                                     in1=gb[0:npart, :], op0=ALU.mult, op1=ALU.mult),
             reads=src_bufs + [sb, B_GB], writes=ntile.bufs)

    def norm_ops(src_ap, src_bufs, col, npart, ntile, out_ap=None):
        norm_p1(src_ap, src_bufs, col, npart, ntile, out_ap)
        norm_p2(src_ap, src_bufs, col, npart, ntile, out_ap)

    def transposes(ntile, dst_views, dst_bufs, engs=("act", "dve")):
        for half in range(2):
            tb = next_tbank()

            def fn(e, half=half, tb=tb):
                r = None
                for i in range(8):
                    kc = half * 8 + i
                    r = e.transpose(out=banks_bf[tb][:, i * 128:(i + 1) * 128],
                                    in_=ntile.ap[:, kc * 128:(kc + 1) * 128], identity=ident)
                return r
            P.op("pe", fn, reads=ntile.bufs + [B_ID], writes=[PB[tb]])
            src = banks_bf[tb].rearrange("p (k t) -> p k t", k=8)
            dst = dst_views[half]
            if engs[half] == "act":
                P.op("act", lambda e, dst=dst, src=src: e.activation(out=dst, in_=src, func=AF.Copy),
                     reads=[PB[tb]], writes=dst_bufs)
            else:
                P.op("dve", lambda e, dst=dst, src=src: e.tensor_copy(out=dst, in_=src),
                     reads=[PB[tb]], writes=dst_bufs)

    nth = Scr(24576, 4096, BF16)
    norm_ops(xh.ap[0:HALO, :], xh.bufs, 8, HALO, nth)
    tb = next_tbank()

    def fn_h(e, tb=tb):
        r = None
        for kc in range(16):
            r = e.transpose(out=banks_bf[tb][:, kc * HALO:(kc + 1) * HALO],
                            in_=nth.ap[0:HALO, kc * 128:(kc + 1) * 128], identity=ident[0:HALO, 0:HALO])
        return r
    P.op("pe", fn_h, reads=nth.bufs + [B_ID], writes=[PB[tb]])
    src_h = banks_bf[tb][:, 0:16 * HALO].rearrange("p (k t) -> p k t", k=16)
    P.op("act", lambda e: e.activation(out=nT1[:, :, 0:HALO], in_=src_h, func=AF.Copy), reads=[PB[tb]], writes=[B_NT])
    ntb3 = ntb + [Scr(24576, 4096, BF16)]

    def tr0(t):
        c0 = HALO + t * 128
        transposes(ntb3[t % 3], [nT1[:, 0:8, c0:c0 + 128], nT1[:, 8:16, c0:c0 + 128]], [B_NT])
    for t in range(NT_TILES + 2):
        if t < NT_TILES:
            norm_p1(xs[:, t, :], [B_XS[t]], t, 128, ntb3[t % 3])
        if 3 <= t < 3 + NSLOT:
            issue_load()
        if 0 <= t - 1 < NT_TILES:
            norm_p2(xs[:, t - 1, :], [B_XS[t - 1]], t - 1, 128, ntb3[(t - 1) % 3])
        if 0 <= t - 2 < NT_TILES:
            tr0(t - 2)

    pbf = Scr(0, 4096, BF16)
    P.op("dve", lambda e: e.tensor_copy(out=pbf.ap, in_=pst.ap), reads=pst.bufs, writes=pbf.bufs)
    for half in range(2):
        tb = next_tbank()

        def fn(e, half=half, tb=tb):
            r = None
            for i in range(8):
                q = half * 8 + i
                r = e.transpose(out=banks_bf[tb][:, i * 128:(i + 1) * 128],
                                in_=pbf.ap[:, q * 128:(q + 1) * 128], identity=ident)
            return r
        P.op("pe", fn, reads=pbf.bufs + [B_ID], writes=[PB[tb]])
        src = banks_bf[tb].rearrange("p (t c k) -> p c t k", t=4, c=2)
        dst = pT[:, :, half * 512:(half + 1) * 512].rearrange("p c (t k) -> p c t k", t=4)
        P.op("act", lambda e, dst=dst, src=src: e.activation(out=dst, in_=src, func=AF.Copy),
             reads=[PB[tb]], writes=[B_PT])

    P.op("sp", lambda e: e.dma_start(out=gb, in_=g_d[1:2, :].partition_broadcast(128)), writes=[B_GB], dma=B_GB)

    if stop == "n1T":
        return dump_bf(nT1[:, :, HALO:TE], 4, 4, [B_NT])

    TT3 = [(i * 352, 352) for i in range(3)]
    KD = 12
    NPE = 31 - KD
    sg1 = Scr(0, 4224)
    ub_ = [Scr(4608, 2112, BF16), Scr(7168, 2112, BF16)]
    DGS = -(-NPE * 256 // 512) * 512
    dg_ = [Scr(9728, NPE * 256, BF16), Scr(9728 + DGS, NPE * 256, BF16)]
    acc1 = Scr(9728 + 2 * DGS, 4096)

    def w_in_chunk(slot_i, q):
        v = ring[slot_i].rearrange("p (k c) -> p k c", k=16)
        return [v[:, kc, q * 128:(q + 1) * 128] for kc in range(16)]

    pos_in = {}
    for i, ch in enumerate(B_ORDER):
        pos_in[ch] = (f"B{i // 4}", i % 4)
    for i, ch in enumerate(A_ORDER):
        pos_in[ch] = (f"A{i // 4}", i % 4)

    def inproj_ext(group, j, consume):
        name, q = pos_in[(group, j)]
        s, = use(name)
        lhs = w_in_chunk(s, q)
        for (c0, n) in TT3:
            b = next_bank()
            mm_group(banks[b][:, 0:n], [(lhs[kc], nT1[:, kc, c0:c0 + n]) for kc in range(16)],
                     reads=[RS[s], B_NT], writes=[PB[b]])
            consume(b, c0, n)

    first = {"vpre": True, "ya": True, "vT": True}

    def xs_guard(key):
        if first[key]:
            first[key] = False
            return list(B_XS)
        return []

    for j in range(NJ):
        sg, ub, dg = sg1, ub_[j % 2], dg_[j % 2]
        d3 = dg.ap.rearrange("p (k c) -> p k c", k=NPE)
        id3 = ident.unsqueeze(1).to_broadcast([128, NPE, 128])
        cw3 = prm[:, P_CW + j * 31 + KD:P_CW + (j + 1) * 31].unsqueeze(2).to_broadcast([128, NPE, 128])
        P.op("pool", lambda e, d3=d3, cw3=cw3, id3=id3: e.tensor_tensor(out=d3, in0=id3, in1=cw3, op=ALU.mult),
             reads=[B_ID, B_PRM], writes=dg.bufs)

        def cons_g(b, c0, n, sg=sg, j=j):
            P.op("act", lambda e: e.activation(out=sg.ap[:, c0:c0 + n], in_=banks[b][:, 0:n], func=AF.Sigmoid,
                                               bias=prm[:, P_BG + j:P_BG + j + 1]),
                 reads=[PB[b], B_PRM], writes=sg.bufs)
        inproj_ext("gg", j, cons_g)

        def cons_v(b, c0, n, sg=sg, ub=ub, j=j):
            P.op("dve", lambda e: e.scalar_tensor_tensor(out=ub.ap[:, c0:c0 + n], in0=banks[b][:, 0:n],
                                                         scalar=prm[:, P_BV + j:P_BV + j + 1], in1=sg.ap[:, c0:c0 + n],
                                                         op0=ALU.add, op1=ALU.mult),
                 reads=[PB[b], B_PRM] + sg.bufs, writes=ub.bufs)
            if c0 == 0:
                P.op("dve", lambda e: e.tensor_tensor(out=ub.ap[:, 0:HALO], in0=ub.ap[:, 0:HALO], in1=hmask, op=ALU.mult),
                     reads=ub.bufs + [B_CONST], writes=ub.bufs)
        inproj_ext("gv", j, cons_v)

        cbanks = []
        for tt in range(2):
            b = next_bank()
            cbanks.append(b)
            base = HALO + tt * 512 - 30
            mm_group(banks[b][:, :], [(d3[:, k - KD, :], ub.ap[:, base + k:base + k + 512]) for k in range(KD, 31)],
                     reads=dg.bufs + ub.bufs, writes=[PB[b]])
        for k in range(KD):
            src = ub.ap[:, HALO - 30 + k:HALO - 30 + k + T]
            wk = prm[:, P_CW + j * 31 + k:P_CW + j * 31 + k + 1]
            if k == 0:
                P.op("dve", lambda e, src=src, wk=wk: e.tensor_scalar(out=acc1.ap, in0=src, scalar1=wk, scalar2=None, op0=ALU.mult),
                     reads=ub.bufs + [B_PRM], writes=acc1.bufs)
            else:
                P.op("dve", lambda e, src=src, wk=wk: e.scalar_tensor_tensor(out=acc1.ap, in0=src, scalar=wk, in1=acc1.ap,
                                                                           op0=ALU.mult, op1=ALU.add),
                     reads=ub.bufs + [B_PRM] + acc1.bufs, writes=acc1.bufs)
        for tt in range(2):
            b = cbanks[tt]
            P.op("act", lambda e, b=b, j=j, tt=tt: e.activation(out=vpre[:, j, tt * 512:(tt + 1) * 512], in_=banks[b][:, :],
                                                               func=AF.Identity, bias=prm[:, P_DWB + j:P_DWB + j + 1]),
                 reads=[PB[b], B_PRM], writes=[B_VPRE] + xs_guard("vpre"))
        P.op("pool", lambda e, j=j: e.tensor_tensor(out=vpre[:, j, :], in0=vpre[:, j, :], in1=acc1.ap, op=ALU.add),
             reads=acc1.bufs + [B_VPRE], writes=[B_VPRE])

    if stop == "vpre":
        P.op("sp", lambda e: e.dma_start(out=dbg_d[:, 0:8192], in_=vpre.rearrange("p k t -> p (k t)")), reads=[B_VPRE], dma=B_OUT)
        P.op("sp", lambda e: e.nop(), writes=[B_OUT])
        P.emit()
        return nc

    z_sb = Scr(0, 4096)
    mean_sb = Scr(4096, 4096)
    var_sb = Scr(8192, 4096)
    def ln_stats():
        vsq_ = [Scr(12288, 4096), Scr(16384, 4096)]
        for j in range(NJ):
            vs = vsq_[j % 2]
            P.op("act", lambda e, vs=vs, j=j: e.activation(out=vs.ap, in_=vpre[:, j, :], func=AF.Square),
                 reads=[B_VPRE], writes=vs.bufs)

            def fn(e, vs=vs, j=j):
                r = None
                for tt in range(2):
                    e.matmul(banks[tt][:, :], lhsT=ones32, rhs=vpre[:, j, tt * 512:(tt + 1) * 512], start=(j == 0), stop=(j == NJ - 1))
                    r = e.matmul(banks[2 + tt][:, :], lhsT=ones32, rhs=vs.ap[:, tt * 512:(tt + 1) * 512], start=(j == 0), stop=(j == NJ - 1))
                return r
            P.op("pe", fn, reads=[B_VPRE, B_ONES] + vs.bufs, writes=[PB[0], PB[1], PB[2], PB[3]])
        rr["i"] = 4
        for tt in range(2):
            sl_ = slice(tt * 512, (tt + 1) * 512)
            P.op("act", lambda e, tt=tt, sl_=sl_: e.activation(out=mean_sb.ap[:, sl_], in_=banks[tt][:, :], func=AF.Copy),
                 reads=[PB[tt]], writes=mean_sb.bufs)
            P.op("act", lambda e, tt=tt, sl_=sl_: e.activation(out=var_sb.ap[:, sl_], in_=banks[tt][:, :], func=AF.Square),
                 reads=[PB[tt]], writes=var_sb.bufs)
            P.op("dve", lambda e, tt=tt, sl_=sl_: e.tensor_tensor(out=var_sb.ap[:, sl_], in0=banks[2 + tt][:, :], in1=var_sb.ap[:, sl_],
                                                                  op=ALU.subtract),
                 reads=[PB[2 + tt]] + var_sb.bufs, writes=var_sb.bufs)
        P.op("act", lambda e: e.activation(out=var_sb.ap, in_=var_sb.ap, func=AF.Sqrt, bias=epsl, scale=1.0),
             reads=var_sb.bufs + [B_EPS], writes=var_sb.bufs)
        P.op("dve", lambda e: e.reciprocal(out=var_sb.ap, in_=var_sb.ap), reads=var_sb.bufs, writes=var_sb.bufs)
        P.op("dve", lambda e: e.scalar_tensor_tensor(out=mean_sb.ap, in0=mean_sb.ap, scalar=-1.0, in1=var_sb.ap,
                                                     op0=ALU.mult, op1=ALU.mult),
             reads=mean_sb.bufs + var_sb.bufs, writes=mean_sb.bufs)


    def ln_chunk(j):
        z = z_sb
        P.op("pool", lambda e: e.tensor_tensor(out=z.ap, in0=vpre[:, j, :], in1=var_sb.ap, op=ALU.mult),
             reads=[B_VPRE] + var_sb.bufs, writes=z.bufs)
        P.op("dve", lambda e: e.tensor_tensor(out=z.ap, in0=z.ap, in1=mean_sb.ap, op=ALU.add),
             reads=z.bufs + mean_sb.bufs, writes=z.bufs)
        P.op("act", lambda e: e.activation(out=vT[:, j, :], in_=z.ap, func=AF.Silu,
                                           bias=prm[:, P_LNB + j:P_LNB + j + 1], scale=prm[:, P_LNG + j:P_LNG + j + 1]),
             reads=z.bufs + [B_PRM], writes=[B_VT] + xs_guard("vT"))

    ah = Scr(12288, 4224)
    ca = Scr(16896, 4224)
    acc3 = Scr(21504, 4096)
    for j in range(NJ):
        def cons_h(b, c0, n):
            P.op("act", lambda e: e.activation(out=ah.ap[:, c0:c0 + n], in_=banks[b][:, 0:n], func=AF.Copy),
                 reads=[PB[b]], writes=ah.bufs)
        inproj_ext("ah", j, cons_h)

        def cons_c(b, c0, n):
            P.op("dve", lambda e: e.tensor_tensor(out=ca.ap[:, c0:c0 + n], in0=banks[b][:, 0:n], in1=ah.ap[:, c0:c0 + n], op=ALU.mult),
                 reads=[PB[b]] + ah.bufs, writes=ca.bufs)
        inproj_ext("ac", j, cons_c)

        for k in range(3):
            src = ca.ap[:, HALO - 2 + k:HALO - 2 + k + T]
            wk = prm[:, P_CA + j * 3 + k:P_CA + j * 3 + k + 1]
            if k == 0:
                P.op("dve", lambda e, src=src, wk=wk: e.tensor_scalar(out=acc3.ap, in0=src, scalar1=wk, scalar2=None, op0=ALU.mult),
                     reads=ca.bufs + [B_PRM], writes=acc3.bufs)
            else:
                P.op("dve", lambda e, src=src, wk=wk: e.scalar_tensor_tensor(out=acc3.ap, in0=src, scalar=wk, in1=acc3.ap,
                                                                           op0=ALU.mult, op1=ALU.add),
                     reads=ca.bufs + [B_PRM] + acc3.bufs, writes=acc3.bufs)

        name, q = pos_in[("ab", j)]
        s, = use(name)
        lhs = w_in_chunk(s, q)
        for tt in range(2):
            b = next_bank()
            mm_group(banks[b][:, :], [(lhs[kc], nT1[:, kc, HALO + tt * 512:HALO + (tt + 1) * 512]) for kc in range(16)],
                     reads=[RS[s], B_NT], writes=[PB[b]])
            P.op("dve", lambda e, b=b, j=j, tt=tt: e.tensor_tensor(out=yaT[:, j, tt * 512:(tt + 1) * 512], in0=banks[b][:, :],
                                                                 in1=acc3.ap[:, tt * 512:(tt + 1) * 512], op=ALU.mult),
                 reads=[PB[b]] + acc3.bufs, writes=[B_YA] + xs_guard("ya"))
        if j == 0:
            ln_stats()
        else:
            ln_chunk(j - 1)
    ln_chunk(NJ - 1)

    if stop == "vT":
        return dump_bf(vT, 2, 4, [B_VT])
    if stop == "ya":
        return dump_bf(yaT, 2, 4, [B_YA])

    sga_ = [Scr(0, 2048), Scr(2048, 2048)]
    sgb2_ = [Scr(4096, 2048), Scr(6144, 2048)]
    t1_ = [Scr(8192, 2048), Scr(10240, 2048)]
    t2_ = [Scr(12288, 2048), Scr(14336, 2048)]
    it = 0
    first_mt = True
    x0s = Scr(16384, 8192)
    P.op("sp", lambda e: e.dma_start(out=x0s.ap, in_=x_d[HALO:HALO + 128, :]), writes=x0s.bufs, dma=x0s.bufs[0])
    for c in range(16):
        s, = use(f"G{c}")
        woa = ring[s][:, 0:1024].rearrange("p (k c) -> p k c", k=8)
        wpw = ring[s][:, 1024:2048].rearrange("p (k c) -> p k c", k=8)
        wga = ring[s][:, 2048:4096].rearrange("p (k c) -> p k c", k=16)
        wgb = ring[s][:, 4096:6144].rearrange("p (k c) -> p k c", k=16)
        for tt in range(2):
            sga, sgb2, t1, t2 = sga_[it % 2], sgb2_[it % 2], t1_[it % 2], t2_[it % 2]
            it += 1
            ts_ = slice(tt * 512, (tt + 1) * 512)
            te_ = slice(HALO + tt * 512, HALO + (tt + 1) * 512)
            b_ga = next_bank()
            mm_group(banks[b_ga][:, :], [(wga[:, kc, :], nT1[:, kc, te_]) for kc in range(16)], reads=[RS[s], B_NT], writes=[PB[b_ga]])
            P.op("act", lambda e, b=b_ga, sga=sga: e.activation(out=sga.ap, in_=banks[b][:, :], func=AF.Sigmoid),
                 reads=[PB[b_ga]], writes=sga.bufs)
            b_ya = next_bank()
            mm_group(banks[b_ya][:, :], [(woa[:, kc, :], yaT[:, kc, ts_]) for kc in range(8)], reads=[RS[s], B_YA], writes=[PB[b_ya]])
            P.op("dve", lambda e, b=b_ya, sga=sga, t1=t1: e.tensor_tensor(out=t1.ap, in0=banks[b][:, :], in1=sga.ap, op=ALU.mult),
                 reads=[PB[b_ya]] + sga.bufs, writes=t1.bufs)
            b_gb = next_bank()
            mm_group(banks[b_gb][:, :], [(wgb[:, kc, :], nT1[:, kc, te_]) for kc in range(16)], reads=[RS[s], B_NT], writes=[PB[b_gb]])
            P.op("act", lambda e, b=b_gb, sgb2=sgb2: e.activation(out=sgb2.ap, in_=banks[b][:, :], func=AF.Sigmoid),
                 reads=[PB[b_gb]], writes=sgb2.bufs)
            b_yb = next_bank()
            mm_group(banks[b_yb][:, :], [(wpw[:, kc, :], vT[:, kc, ts_]) for kc in range(8)], reads=[RS[s], B_VT], writes=[PB[b_yb]])
            P.op("dve", lambda e, b=b_yb, sgb2=sgb2, t2=t2, c=c: e.scalar_tensor_tensor(
                out=t2.ap, in0=banks[b][:, :], scalar=prm[:, P_BPW + c:P_BPW + c + 1], in1=sgb2.ap, op0=ALU.add, op1=ALU.mult),
                reads=[PB[b_yb], B_PRM] + sgb2.bufs, writes=t2.bufs)
            extra = [B_VPRE] if first_mt else []
            first_mt = False
            P.op("pool", lambda e, t1=t1, t2=t2, c=c, ts_=ts_: e.tensor_tensor(out=mT[:, c, ts_], in0=t1.ap, in1=t2.ap, op=ALU.add),
                 reads=t1.bufs + t2.bufs, writes=B_TT[4 * tt:4 * tt + 4] + extra)

    if stop == "mT":
        return dump_bf(mT, 4, 4, B_TT)

    P.op("sp", lambda e: e.nop(), writes=[B_NT, B_YA, B_VT])
    for t in range(1, NT_TILES):
        P.op("sp", lambda e, t=t: e.dma_start(out=h[:, t, :], in_=x_d[HALO + t * 128:HALO + (t + 1) * 128, :]),
             writes=[B_HT[t]], dma=B_HT[t])

    ntc = ntb + [Scr(24576, 4096, BF16)]

    class NormPipe:
        def __init__(self, col0):
            self.col0 = col0
            self.q1 = []
            self.q2 = []

        def push(self, t):
            norm_p1(h[:, t, :], [B_HT[t]], self.col0 + t, 128, ntc[t % 3])
            if self.q2:
                self._tr(self.q2.pop(0))
            if self.q1:
                t1 = self.q1.pop(0)
                norm_p2(h[:, t1, :], [B_HT[t1]], self.col0 + t1, 128, ntc[t1 % 3])
                self.q2.append(t1)
            self.q1.append(t)

        def _tr(self, t):
            transposes(ntc[t % 3], [nT2[:, 0:8, t * 128:(t + 1) * 128], nT2[:, 8:16, t * 128:(t + 1) * 128]], [B_TT[t]],
                       engs=("act", "act"))

        def flush(self):
            while self.q1 or self.q2:
                if self.q2:
                    self._tr(self.q2.pop(0))
                if self.q1:
                    t1 = self.q1.pop(0)
                    norm_p2(h[:, t1, :], [B_HT[t1]], self.col0 + t1, 128, ntc[t1 % 3])
                    self.q2.append(t1)

    np2 = NormPipe(9)
    for n in range(4):
        s, = use(f"O{n}", after=([B_HT[7]] if n == 0 else ()))
        wv = ring[s].rearrange("p (k c) -> p k c", k=16)
        for t in range(NT_TILES):
            b = next_bank()
            mm_group(banks[b][:, :], [(mT[:, kc, t * 128:(t + 1) * 128], wv[:, kc, :]) for kc in range(16)],
                     reads=[RS[s], B_TT[t]], writes=[PB[b]])
            if t == 0:
                P.op("dve", lambda e, b=b, n=n: e.tensor_tensor(out=h[:, 0, n * 512:(n + 1) * 512], in0=x0s.ap[:, n * 512:(n + 1) * 512],
                                                              in1=banks[b][:, :], op=ALU.add),
                     reads=[PB[b]] + x0s.bufs, writes=[B_HT[0]])
            else:
                P.op("dve", lambda e, b=b, t=t, n=n: e.tensor_tensor(out=h[:, t, n * 512:(n + 1) * 512], in0=h[:, t, n * 512:(n + 1) * 512],
                                                                   in1=banks[b][:, :], op=ALU.add),
                     reads=[PB[b], B_HT[t]], writes=[B_HT[t]])
            if n == 3 and stop != "h1":
                np2.push(t)
    if stop == "h1":
        return dump_h()
    np2.flush()
    P.op("sp", lambda e: e.dma_start(out=gb, in_=g_d[2:3, :].partition_broadcast(128)), writes=[B_GB], dma=B_GB)

    fT_ = [Scr(0, 8192, BF16), Scr(8192, 8192, BF16)]
    sl_ = [Scr(24576, 2048), Scr(26624, 2048)]
    it = 0
    np3 = NormPipe(17)

    def ffn_gu(blk):
        nonlocal it
        fT = fT_[blk % 2]
        f3 = fT.ap.rearrange("p (k t) -> p k t", k=4)
        sg_, su_ = use(f"FG{blk}", f"FU{blk}")
        wg = ring[sg_].rearrange("p (k c) -> p k c", k=16)
        wu = ring[su_].rearrange("p (k c) -> p k c", k=16)
        for cc in range(4):
            for tt in range(2):
                sl = sl_[it % 2]
                it += 1
                ts_ = slice(tt * 512, (tt + 1) * 512)
                bg = next_bank()
                mm_group(banks[bg][:, :], [(wg[:, kc, cc * 128:(cc + 1) * 128], nT2[:, kc, ts_]) for kc in range(16)],
                         reads=[RS[sg_]] + B_TT[4 * tt:4 * tt + 4], writes=[PB[bg]])
                P.op("act", lambda e, b=bg, sl=sl: e.activation(out=sl.ap, in_=banks[b][:, :], func=AF.Silu),
                     reads=[PB[bg]], writes=sl.bufs)
                bu = next_bank()
                mm_group(banks[bu][:, :], [(wu[:, kc, cc * 128:(cc + 1) * 128], nT2[:, kc, ts_]) for kc in range(16)],
                         reads=[RS[su_]] + B_TT[4 * tt:4 * tt + 4], writes=[PB[bu]])
                P.op("dve", lambda e, b=bu, sl=sl, f3=f3, cc=cc, ts_=ts_: e.tensor_tensor(out=f3[:, cc, ts_], in0=banks[b][:, :], in1=sl.ap, op=ALU.mult),
                     reads=[PB[bu]] + sl.bufs, writes=fT.bufs)

    def ffn_down(blk):
        fT = fT_[blk % 2]
        f3 = fT.ap.rearrange("p (k t) -> p k t", k=4)
        sd_, = use(f"FD{blk}")
        wd = ring[sd_].rearrange("p (k c) -> p k c", k=4)
        for t in range(NT_TILES):
            for n in range(4):
                b = next_bank()
                mm_group(banks[b][:, :], [(f3[:, kc, t * 128:(t + 1) * 128], wd[:, kc, n * 512:(n + 1) * 512]) for kc in range(4)],
                         reads=[RS[sd_]] + fT.bufs, writes=[PB[b]])
                P.op("dve", lambda e, b=b, t=t, n=n: e.tensor_tensor(out=h[:, t, n * 512:(n + 1) * 512], in0=h[:, t, n * 512:(n + 1) * 512],
                                                                   in1=banks[b][:, :], op=ALU.add),
                     reads=[PB[b], B_HT[t]], writes=[B_HT[t]])
            if blk == NFB - 1 and stop != "h2":
                np3.push(t)

    ffn_gu(0)
    for blk in range(1, NFB):
        ffn_gu(blk)
        ffn_down(blk - 1)
    ffn_down(NFB - 1)
    if stop == "h2":
        return dump_h()
    np3.flush()
    P.op("sp", lambda e: e.dma_start(out=gb, in_=g_d[3:4, :].partition_broadcast(128)), writes=[B_GB], dma=B_GB)

    sgp_ = [Scr(0, 2048), Scr(2048, 2048)]
    tp_ = [Scr(4096, 2048), Scr(6144, 2048)]
    ost_ = [Scr(8192, 8192), Scr(16384, 8192)]
    fq1, fq2 = [], []
    fjunk = Scr(24576, 4096, BF16)

    def fin_p2(t):
        ost = ost_[t % 2]
        norm_p2(h[:, t, :], [B_HT[t]], 25 + t, 128, ost, out_ap=ost.ap)
        P.op("sp", lambda e, t=t, ost=ost: e.dma_start(out=out_d[t * 128:(t + 1) * 128, :], in_=ost.ap),
             reads=ost.bufs, dma=ost.bufs[0])

    def fin_push(t):
        if fq2:
            fin_p2(fq2.pop(0))
        if fq1:
            t1 = fq1.pop(0)
            norm_p1(h[:, t1, :], [B_HT[t1]], 25 + t1, 128, ost_[t1 % 2], out_ap=ost_[t1 % 2].ap, junk=fjunk)
            fq2.append(t1)
        fq1.append(t)

    def fin_flush():
        while fq1 or fq2:
            if fq2:
                fin_p2(fq2.pop(0))
            if fq1:
                t1 = fq1.pop(0)
                norm_p1(h[:, t1, :], [B_HT[t1]], 25 + t1, 128, ost_[t1 % 2], out_ap=ost_[t1 % 2].ap, junk=fjunk)
                fq2.append(t1)

    it = 0

    def ple_group(n, t, spp, s_):
        nonlocal it
        wpp = ring[spp][:, 0:1024].rearrange("p (k c) -> p k c", k=2)
        wv = ring[s_].rearrange("p (k c) -> p k c", k=16)
        sgp, tp = sgp_[it % 2], tp_[it % 2]
        it += 1
        bgt = next_bank()
        mm_group(banks[bgt][:, :], [(nT2[:, kc, t * 128:(t + 1) * 128], wv[:, kc, :]) for kc in range(16)],
                 reads=[RS[s_], B_TT[t]], writes=[PB[bgt]])
        P.op("act", lambda e, b=bgt, sgp=sgp: e.activation(out=sgp.ap, in_=banks[b][:, :], func=AF.Sigmoid),
             reads=[PB[bgt]], writes=sgp.bufs)
        bp = next_bank()
        mm_group(banks[bp][:, :], [(pT[:, ec, t * 128:(t + 1) * 128], wpp[:, ec, :]) for ec in range(2)],
                 reads=[RS[spp], B_PT], writes=[PB[bp]])
        P.op("dve", lambda e, b=bp, sgp=sgp, tp=tp: e.tensor_tensor(out=tp.ap, in0=banks[b][:, :], in1=sgp.ap, op=ALU.mult),
             reads=[PB[bp]] + sgp.bufs, writes=tp.bufs)
        P.op("dve" if n >= 2 else "pool",
             lambda e, tp=tp, t=t, n=n: e.tensor_tensor(out=h[:, t, n * 512:(n + 1) * 512], in0=h[:, t, n * 512:(n + 1) * 512],
                                                       in1=tp.ap, op=ALU.add),
             reads=tp.bufs + [B_HT[t]], writes=[B_HT[t]])

    for n in range(2):
        spp, s_ = use(f"PP{n}", f"PG{n}")
        for t in range(NT_TILES):
            ple_group(n, t, spp, s_)
    spp2, s2, spp3, s3 = use("PP2", "PG2", "PP3", "PG3")
    SKEW = 3
    for i in range(NT_TILES + SKEW):
        if i < NT_TILES:
            ple_group(2, i, spp2, s2)
        if i - SKEW >= 0:
            ple_group(3, i - SKEW, spp3, s3)
            fin_push(i - SKEW)
    fin_flush()
    P.op("sp", lambda e: e.nop(), writes=ost_[0].bufs + ost_[1].bufs)
    P.emit()
    return nc


_NC_CACHE = {}


def _get_nc(stop=None):
    if stop not in _NC_CACHE:
        _NC_CACHE[stop] = build(stop)
    return _NC_CACHE[stop]


def _prep_inputs(x, p, g_mix, w_in, conv_a_w, w_out_a, b_glu, conf_dw_w, conf_dw_b, conf_ln_g, conf_ln_b,
                 w_pw_b, b_pw_b, w_o, g_ffn, w_gate, w_up, w_down, g_ple, w_ple_gate, w_ple_proj, g_final):
    f = lambda a: np.asarray(a, np.float32)
    x2 = f(x).reshape(SEQ, D)
    p2 = f(p).reshape(SEQ, PLE)
    wflat = _prep_weights(f(w_in)[0], f(w_out_a)[0], f(w_pw_b)[0], f(w_o)[0], f(w_gate)[0], f(w_up)[0], f(w_down)[0],
                          f(w_ple_gate)[0], f(w_ple_proj)[0])
    prm = _prep_prm(f(conv_a_w)[0], f(b_glu)[0], f(conf_dw_w)[0], f(conf_dw_b)[0], f(conf_ln_g)[0], f(conf_ln_b)[0], f(b_pw_b)[0])
    gains = np.ascontiguousarray(np.stack([f(g_mix)[0], f(g_ffn)[0], f(g_ple)[0], f(g_final)]))
    xpad = np.concatenate([np.zeros((HALO, D), np.float32), x2], axis=0)
    in_maps = []
    for c in range(NCORES):
        in_maps.append({
            "x_ext": np.ascontiguousarray(xpad[c * T:c * T + TE]),
            "p_c": np.ascontiguousarray(p2[c * T:(c + 1) * T].reshape(NT_TILES, 128, PLE).transpose(1, 0, 2)).reshape(128, NT_TILES * PLE),
            "hmask": np.full((128, HALO), 0.0 if c == 0 else 1.0, np.float32),
            "wflat": wflat,
            "prm": prm,
            "gains": gains,
        })
    return in_maps


def kernel(**inputs):
    in_maps = _prep_inputs(**inputs)
    nc = _get_nc(None)
    res = run_bass_kernel_spmd(nc, in_maps, core_ids=list(range(NCORES)))
    out = np.concatenate([np.asarray(r["out"], np.float32) for r in res.results], axis=0)
    return out.reshape(1, SEQ, D)
```

```python
import numpy as np
import concourse.bass as bass
import concourse.mybir as mybir
from concourse.bass_utils import run_bass_kernel_spmd

F32 = mybir.dt.float32
BF16 = mybir.dt.bfloat16
AF = mybir.ActivationFunctionType
ALU = mybir.AluOpType

NCORES = 8
D = 2048
SEQ = 8192
T = SEQ // NCORES
HALO = 32
TE = T + HALO
NT_TILES = T // 128
KC = D // 128
CW = 1024
NJ = CW // 128
DFF = 5632
NFB = DFF // 512
PLE = 256
EPS = 1e-6
LN_EPS = 1e-5
SLOT = 8192
NSLOT = 4

COMPUTE = ("pe", "act", "dve", "pool")
ENGS = ("pe", "act", "dve", "pool", "sp")


class Buf:
    __slots__ = ("name", "writer", "readers", "sem", "cnt")

    def __init__(self, name):
        self.name = name
        self.writer = None
        self.readers = []
        self.sem = None
        self.cnt = 0


class Op:
    __slots__ = ("eng", "fn", "deps", "idx", "needed", "val", "dma_buf", "dma_val", "is_dma")


class Prog:
    def __init__(self, nc):
        self.nc = nc
        self.streams = {e: [] for e in ENGS}
        self.dma_bufs = []

    def op(self, eng, fn, reads=(), writes=(), dma=None):
        o = Op()
        o.eng = eng
        o.fn = fn
        o.needed = False
        o.val = None
        o.is_dma = dma is not None
        o.dma_buf = dma
        o.dma_val = None
        deps = []
        for b in reads:
            if b.writer is not None:
                deps.append(b.writer)
            b.readers.append(o)
        for b in writes:
            if b.writer is not None:
                deps.append(b.writer)
            deps.extend(r for r in b.readers if r is not o)
            b.writer = o
            b.readers = []
        if dma is not None:
            if dma.sem is None:
                self.dma_bufs.append(dma)
                dma.sem = True
            dma.cnt += 16
            o.dma_val = dma.cnt
        o.deps = [d for d in deps if d is not o and not (eng == "pe" and d.eng == "pe" and not d.is_dma)]
        o.idx = len(self.streams[eng])
        self.streams[eng].append(o)
        return o

    def emit(self):
        nc = self.nc
        for e in ENGS:
            for o in self.streams[e]:
                for d in o.deps:
                    if not d.is_dma:
                        d.needed = True
        for e in COMPUTE:
            c = 0
            for o in self.streams[e]:
                if o.needed and not o.is_dma:
                    c += 1
                    o.val = c
        sems = {e: nc.alloc_semaphore("S_" + e) for e in COMPUTE}
        for b in self.dma_bufs:
            b.sem = nc.alloc_semaphore("D_" + b.name)
        streams = self.streams

        def run(e, eng):
            waited = {}
            for o in streams[e]:
                w = {}
                for d in o.deps:
                    if d.is_dma:
                        s, v = d.dma_buf.sem, d.dma_val
                    else:
                        s, v = sems[d.eng], d.val
                    k = id(s)
                    if k not in w or w[k][1] < v:
                        w[k] = (s, v)
                for k, (s, v) in w.items():
                    if waited.get(k, 0) >= v:
                        continue
                    waited[k] = v
                    eng.wait_ge(s, v)
                ins = o.fn(eng)
                if o.is_dma:
                    ins.then_inc(o.dma_buf.sem, 16)
                elif o.needed:
                    ins.then_inc(sems[e], 1)

        with nc.Block() as block:
            @block.tensor
            def _(eng):
                run("pe", eng)

            @block.scalar
            def _(eng):
                run("act", eng)

            @block.vector
            def _(eng):
                run("dve", eng)

            @block.gpsimd
            def _(eng):
                run("pool", eng)

            @block.sync
            def _(eng):
                run("sp", eng)


def _kc(w):
    k, c = w.shape
    return np.ascontiguousarray(w.reshape(k // 128, 128, c).transpose(1, 0, 2)).reshape(128, -1)


def _ch(group, j):
    base = {"ah": 0, "ab": 8, "ac": 16, "gv": 24, "gg": 32, "ga": 40, "gb": 56}[group]
    return base + j


B_ORDER = []
for _i in range(0, NJ, 2):
    B_ORDER += [("gg", _i), ("gv", _i), ("gg", _i + 1), ("gv", _i + 1)]
A_ORDER = []
for _j in range(NJ):
    A_ORDER += [("ah", _j), ("ac", _j), ("ab", _j)]


def _load_plan():
    plan = []
    for i in range(4):
        plan.append((f"B{i}", 16 * 512))
    for i in range(6):
        plan.append((f"A{i}", 16 * 512))
    for c in range(16):
        plan.append((f"G{c}", 6144))
    for n in range(4):
        plan.append((f"O{n}", 16 * 512))
    for b in range(NFB):
        plan.append((f"FG{b}", 16 * 512))
        plan.append((f"FU{b}", 16 * 512))
        if b >= 1:
            plan.append((f"FD{b - 1}", 4 * 2048))
    plan.append((f"FD{NFB - 1}", 4 * 2048))
    for n in range(4):
        plan.append((f"PP{n}", 2 * 512))
        plan.append((f"PG{n}", 16 * 512))
    return plan


PLAN = _load_plan()
WTOT = sum(n for _, n in PLAN)


def _prep_weights(w_in, w_out_a, w_pw_b, w_o, w_gate, w_up, w_down, w_ple_gate, w_ple_proj):
    wflat = np.empty((128, WTOT), np.float32)
    pos = 0

    def put(a):
        nonlocal pos
        n = a.shape[1]
        wflat[:, pos:pos + n] = a
        pos += n

    def cols(chunks):
        idx = np.concatenate([np.arange(_ch(g, j) * 128, _ch(g, j) * 128 + 128) for g, j in chunks])
        return _kc(w_in[:, idx])

    for i in range(4):
        put(cols(B_ORDER[4 * i:4 * i + 4]))
    for i in range(6):
        put(cols(A_ORDER[4 * i:4 * i + 4]))
    for c in range(16):
        put(_kc(w_out_a[:, c * 128:(c + 1) * 128]))
        put(_kc(w_pw_b[:, c * 128:(c + 1) * 128]))
        put(cols([("ga", c)]))
        put(cols([("gb", c)]))
    for n in range(4):
        put(_kc(w_o[:, n * 512:(n + 1) * 512]))
    for b in range(NFB):
        put(_kc(w_gate[:, b * 512:(b + 1) * 512]))
        put(_kc(w_up[:, b * 512:(b + 1) * 512]))
        if b >= 1:
            put(_kc(w_down[(b - 1) * 512:b * 512, :]))
    put(_kc(w_down[(NFB - 1) * 512:NFB * 512, :]))
    for n in range(4):
        put(_kc(w_ple_proj[:, n * 512:(n + 1) * 512]))
        put(_kc(w_ple_gate[:, n * 512:(n + 1) * 512]))
    assert pos == WTOT
    return wflat


P_BV, P_BG, P_CA, P_CW, P_DWB, P_LNG, P_LNB, P_BPW = 0, 8, 16, 40, 288, 296, 304, 312
NPRM = 328


def _fm(v, n):
    return np.ascontiguousarray(np.asarray(v, np.float32).reshape(n, 128).T)


def _prep_prm(conv_a_w, b_glu, conf_dw_w, conf_dw_b, conf_ln_g, conf_ln_b, b_pw_b):
    prm = np.zeros((128, NPRM), np.float32)
    prm[:, P_BV:P_BV + 8] = _fm(b_glu[:CW], 8)
    prm[:, P_BG:P_BG + 8] = _fm(b_glu[CW:], 8)
    prm[:, P_CA:P_CA + 24] = np.ascontiguousarray(conv_a_w.reshape(3, 8, 128).transpose(2, 1, 0)).reshape(128, 24)
    prm[:, P_CW:P_CW + 248] = np.ascontiguousarray(conf_dw_w.reshape(31, 8, 128).transpose(2, 1, 0)).reshape(128, 248)
    prm[:, P_DWB:P_DWB + 8] = _fm(conf_dw_b, 8)
    prm[:, P_LNG:P_LNG + 8] = _fm(conf_ln_g, 8)
    prm[:, P_LNB:P_LNB + 8] = _fm(conf_ln_b, 8)
    prm[:, P_BPW:P_BPW + 16] = _fm(b_pw_b, 16)
    return prm


def build(stop=None):
    nc = bass.Bass("TRN2", target_bir_lowering=False)
    x_d = nc.dram_tensor("x_ext", [TE, D], F32, kind="ExternalInput").ap()
    p_d = nc.dram_tensor("p_c", [128, NT_TILES * PLE], F32, kind="ExternalInput").ap()
    hm_d = nc.dram_tensor("hmask", [128, HALO], F32, kind="ExternalInput").ap()
    w_d = nc.dram_tensor("wflat", [128, WTOT], F32, kind="ExternalInput").ap()
    prm_d = nc.dram_tensor("prm", [128, NPRM], F32, kind="ExternalInput").ap()
    g_d = nc.dram_tensor("gains", [4, D], F32, kind="ExternalInput").ap()
    out_d = nc.dram_tensor("out", [T, D], F32, kind="ExternalOutput").ap()
    dbg_d = None
    if stop is not None:
        dbg_d = nc.dram_tensor("dbg", [128, 16384], F32, kind="ExternalOutput").ap()

    NB4 = 52992
    big = nc.alloc_sbuf_tensor("big", [128, NB4], F32)

    def reg(off, nbytes, dt=F32):
        assert off % 4 == 0 and nbytes % 4 == 0 and (off + nbytes) <= NB4 * 4, (off, nbytes)
        v = big[:, off // 4:(off + nbytes) // 4]
        return v.bitcast(BF16) if dt == BF16 else v

    ring = [reg(i * 16384, 16384, BF16) for i in range(NSLOT)]
    O_NT, O_YA, O_VT, O_H, O_VPRE, O_GB, O_SM, O_SC = 65536, 99328, 115712, 65536, 132096, 164864, 173056, 181248
    SC_SIZE = NB4 * 4 - O_SC
    nT1 = reg(O_NT, 16 * TE * 2, BF16).rearrange("p (k t) -> p k t", k=16)
    yaT = reg(O_YA, 16384, BF16).rearrange("p (k t) -> p k t", k=8)
    vT = reg(O_VT, 16384, BF16).rearrange("p (k t) -> p k t", k=8)
    xs = reg(O_YA, 65536).rearrange("p (t d) -> p t d", t=8)
    h = reg(O_H, 65536).rearrange("p (t d) -> p t d", t=8)
    vpre = reg(O_VPRE, 32768).rearrange("p (k t) -> p k t", k=8)
    mT = reg(O_VPRE, 32768, BF16).rearrange("p (k t) -> p k t", k=16)
    nT2 = mT
    gb = reg(O_GB, 8192)
    prm = reg(O_SM, 1312)
    hmask = reg(O_SM + 1344, 128)
    ident = reg(O_SM + 1472, 256, BF16)
    ones32 = reg(O_SM + 1728, 512)
    ssq = reg(O_SM + 2240, 160)
    rstd = reg(O_SM + 2400, 160)
    stmp = reg(O_SM + 2560, 160)
    epsr = reg(O_SM + 2720, 4)
    epsl = reg(O_SM + 2724, 4)
    identf = reg(O_SM + 2752, 512)
    pT = reg(O_SM + 3328, 4096, BF16).rearrange("p (k t) -> p k t", k=2)
    mhalf1 = reg(O_SM + 7424, 4)

    banks = [nc.alloc_psum_tensor(f"bank{i}", [128, 512], F32) for i in range(8)]
    banks_bf = [b[:].bitcast(BF16) for b in banks]

    P = Prog(nc)
    RS = [Buf(f"ring{i}") for i in range(NSLOT)]
    PB = [Buf(f"pb{i}") for i in range(8)]
    PAGE = 512
    SCP = [Buf(f"sc{i}") for i in range(SC_SIZE // PAGE + 1)]
    B_NT, B_YA, B_VT, B_VPRE, B_GB = Buf("nT"), Buf("ya"), Buf("vT"), Buf("vpre"), Buf("gb")
    B_TT = [Buf(f"tt{t}") for t in range(8)]
    B_HT = [Buf(f"h{t}") for t in range(8)]
    B_XS = [Buf(f"xs{t}") for t in range(8)]
    B_SS = [Buf(f"ss{i}") for i in range(40)]
    B_PRM, B_CONST, B_PT = Buf("prm"), Buf("const"), Buf("pT")
    B_OUT = Buf("outst")
    B_ID, B_IDF, B_ONES, B_EPS = Buf("ident"), Buf("identf"), Buf("ones"), Buf("eps")

    class Scr:
        def __init__(self, off, nbytes, dt=F32):
            assert off + nbytes <= SC_SIZE, (off, nbytes, SC_SIZE)
            self.ap = reg(O_SC + off, nbytes, dt)
            self.bufs = SCP[off // PAGE:(off + nbytes - 1) // PAGE + 1]

    wstate = {"next": 0, "off": 0}
    loaded = {}

    def issue_load(after=()):
        i = wstate["next"]
        if i >= len(PLAN):
            return
        name, n = PLAN[i]
        s = i % NSLOT
        off = wstate["off"]
        P.op("pool", lambda e, s=s, off=off, n=n: e.dma_start(out=ring[s][:, 0:n], in_=w_d[:, off:off + n]),
             reads=list(after), writes=[RS[s]], dma=RS[s])
        loaded[name] = s
        wstate["next"] = i + 1
        wstate["off"] = off + n

    PIDX = {nm: i for i, (nm, _) in enumerate(PLAN)}

    def use(*names, after=()):
        k0 = min(PIDX[n] for n in names)
        while wstate["next"] < len(PLAN) and wstate["next"] <= k0 + NSLOT - 1:
            issue_load(after)
        return [loaded[n] for n in names]

    rr = {"i": 0, "t": 0}

    def next_bank():
        b = rr["i"] % 6
        rr["i"] += 1
        return b

    def next_tbank():
        b = 6 + rr["t"] % 2
        rr["t"] += 1
        return b

    def mm_group(out_ap, pairs, reads, writes):
        def fn(e):
            n = len(pairs)
            r = None
            for i, (l, rh) in enumerate(pairs):
                r = e.matmul(out_ap, lhsT=l, rhs=rh, start=(i == 0), stop=(i == n - 1))
            return r
        return P.op("pe", fn, reads=reads, writes=writes)

    def dump_bf(view3, nq, kq, reads):
        cv = Scr(0, 16384)
        for q in range(nq):
            P.op("act", lambda e, q=q: e.activation(out=cv.ap.rearrange("p (k t) -> p k t", k=4),
                                                    in_=view3[:, 4 * q:4 * q + 4, :], func=AF.Copy),
                 reads=reads, writes=cv.bufs)
            P.op("sp", lambda e, q=q: e.dma_start(out=dbg_d[:, q * 4096:(q + 1) * 4096], in_=cv.ap), reads=cv.bufs, dma=B_OUT)
        P.op("sp", lambda e: e.nop(), writes=[B_OUT] + cv.bufs)
        P.emit()
        return nc

    def dump_h():
        for t in range(8):
            P.op("sp", lambda e, t=t: e.dma_start(out=dbg_d[:, t * 2048:(t + 1) * 2048], in_=h[:, t, :]), reads=[B_HT[t]], dma=B_OUT)
        P.op("sp", lambda e: e.nop(), writes=[B_OUT])
        P.emit()
        return nc

    pst = Scr(8192, 8192)
    P.op("sp", lambda e: e.dma_start(out=pst.ap, in_=p_d), writes=pst.bufs, dma=pst.bufs[0])
    xh = Scr(0, 8192)
    P.op("sp", lambda e: e.dma_start(out=xh.ap[0:HALO, :], in_=x_d[0:HALO, :]), writes=xh.bufs, dma=xh.bufs[0])
    P.op("sp", lambda e: e.dma_start(out=gb, in_=g_d[0:1, :].partition_broadcast(128)), writes=[B_GB], dma=B_GB)
    for t in range(NT_TILES):
        P.op("sp", lambda e, t=t: e.dma_start(out=xs[:, t, :], in_=x_d[HALO + t * 128:HALO + (t + 1) * 128, :]),
             writes=[B_XS[t]], dma=B_XS[t])
    P.op("sp", lambda e: e.dma_start(out=prm, in_=prm_d), writes=[B_PRM], dma=B_PRM)
    P.op("sp", lambda e: e.dma_start(out=hmask, in_=hm_d), writes=[B_CONST], dma=B_CONST)

    P.op("pool", lambda e: e.memset(identf, 0.0), writes=[B_IDF])
    P.op("pool", lambda e: e.affine_select(out=identf, in_=identf, pattern=[[-1, 128]], compare_op=ALU.not_equal,
                                           fill=1.0, base=0, channel_multiplier=1), reads=[B_IDF], writes=[B_IDF])
    P.op("pool", lambda e: e.memset(ones32, 1.0 / CW), writes=[B_ONES])

    def mk_eps(e):
        e.memset(epsr, EPS)
        e.memset(mhalf1, -0.5)
        return e.memset(epsl, LN_EPS)
    P.op("pool", mk_eps, writes=[B_EPS])
    P.op("pool", lambda e: e.memset(ssq, 0.0), writes=B_SS)
    P.op("dve", lambda e: e.tensor_copy(out=ident, in_=identf), reads=[B_IDF], writes=[B_ID])


    ntb = [Scr(16384, 4096, BF16), Scr(20480, 4096, BF16)]

    def norm_p1(src_ap, src_bufs, col, npart, ntile, out_ap=None, junk=None):
        dst = ntile.ap[0:npart, :] if out_ap is None else out_ap
        jb = ntile.bufs
        if junk is not None:
            dst, jb = junk.ap, junk.bufs
        sb = B_SS[col]
        P.op("act", lambda e: e.activation(out=dst, in_=src_ap, func=AF.Square, accum_out=ssq[0:npart, col:col + 1]),
             reads=src_bufs, writes=jb + [sb])
        P.op("pool", lambda e: e.tensor_scalar(out=stmp[0:npart, col:col + 1], in0=ssq[0:npart, col:col + 1],
                                              scalar1=1.0 / D, scalar2=EPS, op0=ALU.mult, op1=ALU.add),
             reads=[sb], writes=[sb])
        P.op("pool", lambda e: e.tensor_tensor(out=rstd[0:npart, col:col + 1], in0=stmp[0:npart, col:col + 1],
                                               in1=mhalf1[0:npart, :], op=ALU.pow),
             reads=[sb, B_EPS], writes=[sb])

    def norm_p2(src_ap, src_bufs, col, npart, ntile, out_ap=None):
        dst = ntile.ap[0:npart, :] if out_ap is None else out_ap
        sb = B_SS[col]
        P.op("dve", lambda e: e.scalar_tensor_tensor(out=dst, in0=src_ap, scalar=rstd[0:npart, col:col + 1],
                                                     in1=gb[0:npart, :], op0=ALU.mult, op1=ALU.mult),
             reads=src_bufs + [sb, B_GB], writes=ntile.bufs)

    def norm_ops(src_ap, src_bufs, col, npart, ntile, out_ap=None):
        norm_p1(src_ap, src_bufs, col, npart, ntile, out_ap)
        norm_p2(src_ap, src_bufs, col, npart, ntile, out_ap)

    def transposes(ntile, dst_views, dst_bufs, engs=("act", "dve")):
        for half in range(2):
            tb = next_tbank()

            def fn(e, half=half, tb=tb):
                r = None
                for i in range(8):
                    kc = half * 8 + i
                    r = e.transpose(out=banks_bf[tb][:, i * 128:(i + 1) * 128],
                                    in_=ntile.ap[:, kc * 128:(kc + 1) * 128], identity=ident)
                return r
            P.op("pe", fn, reads=ntile.bufs + [B_ID], writes=[PB[tb]])
            src = banks_bf[tb].rearrange("p (k t) -> p k t", k=8)
            dst = dst_views[half]
            if engs[half] == "act":
                P.op("act", lambda e, dst=dst, src=src: e.activation(out=dst, in_=src, func=AF.Copy),
                     reads=[PB[tb]], writes=dst_bufs)
            else:
                P.op("dve", lambda e, dst=dst, src=src: e.tensor_copy(out=dst, in_=src),
                     reads=[PB[tb]], writes=dst_bufs)

    nth = Scr(24576, 4096, BF16)
    norm_ops(xh.ap[0:HALO, :], xh.bufs, 8, HALO, nth)
    tb = next_tbank()

    def fn_h(e, tb=tb):
        r = None
        for kc in range(16):
            r = e.transpose(out=banks_bf[tb][:, kc * HALO:(kc + 1) * HALO],
                            in_=nth.ap[0:HALO, kc * 128:(kc + 1) * 128], identity=ident[0:HALO, 0:HALO])
        return r
    P.op("pe", fn_h, reads=nth.bufs + [B_ID], writes=[PB[tb]])
    src_h = banks_bf[tb][:, 0:16 * HALO].rearrange("p (k t) -> p k t", k=16)
    P.op("act", lambda e: e.activation(out=nT1[:, :, 0:HALO], in_=src_h, func=AF.Copy), reads=[PB[tb]], writes=[B_NT])
    ntb3 = ntb + [Scr(24576, 4096, BF16)]

    def tr0(t):
        c0 = HALO + t * 128
        transposes(ntb3[t % 3], [nT1[:, 0:8, c0:c0 + 128], nT1[:, 8:16, c0:c0 + 128]], [B_NT])
    for t in range(NT_TILES + 2):
        if t < NT_TILES:
            norm_p1(xs[:, t, :], [B_XS[t]], t, 128, ntb3[t % 3])
        if 3 <= t < 3 + NSLOT:
            issue_load()
        if 0 <= t - 1 < NT_TILES:
            norm_p2(xs[:, t - 1, :], [B_XS[t - 1]], t - 1, 128, ntb3[(t - 1) % 3])
        if 0 <= t - 2 < NT_TILES:
            tr0(t - 2)

    pbf = Scr(0, 4096, BF16)
    P.op("dve", lambda e: e.tensor_copy(out=pbf.ap, in_=pst.ap), reads=pst.bufs, writes=pbf.bufs)
    for half in range(2):
        tb = next_tbank()

        def fn(e, half=half, tb=tb):
            r = None
            for i in range(8):
                q = half * 8 + i
                r = e.transpose(out=banks_bf[tb][:, i * 128:(i + 1) * 128],
                                in_=pbf.ap[:, q * 128:(q + 1) * 128], identity=ident)
            return r
        P.op("pe", fn, reads=pbf.bufs + [B_ID], writes=[PB[tb]])
        src = banks_bf[tb].rearrange("p (t c k) -> p c t k", t=4, c=2)
        dst = pT[:, :, half * 512:(half + 1) * 512].rearrange("p c (t k) -> p c t k", t=4)
        P.op("act", lambda e, dst=dst, src=src: e.activation(out=dst, in_=src, func=AF.Copy),
             reads=[PB[tb]], writes=[B_PT])

    P.op("sp", lambda e: e.dma_start(out=gb, in_=g_d[1:2, :].partition_broadcast(128)), writes=[B_GB], dma=B_GB)

    if stop == "n1T":
        return dump_bf(nT1[:, :, HALO:TE], 4, 4, [B_NT])

    TT3 = [(i * 352, 352) for i in range(3)]
    KD = 12
    NPE = 31 - KD
    sg1 = Scr(0, 4224)
    ub_ = [Scr(4608, 2112, BF16), Scr(7168, 2112, BF16)]
    DGS = -(-NPE * 256 // 512) * 512
    dg_ = [Scr(9728, NPE * 256, BF16), Scr(9728 + DGS, NPE * 256, BF16)]
    acc1 = Scr(9728 + 2 * DGS, 4096)

    def w_in_chunk(slot_i, q):
        v = ring[slot_i].rearrange("p (k c) -> p k c", k=16)
        return [v[:, kc, q * 128:(q + 1) * 128] for kc in range(16)]

    pos_in = {}
    for i, ch in enumerate(B_ORDER):
        pos_in[ch] = (f"B{i // 4}", i % 4)
    for i, ch in enumerate(A_ORDER):
        pos_in[ch] = (f"A{i // 4}", i % 4)

    def inproj_ext(group, j, consume):
        name, q = pos_in[(group, j)]
        s, = use(name)
        lhs = w_in_chunk(s, q)
        for (c0, n) in TT3:
            b = next_bank()
            mm_group(banks[b][:, 0:n], [(lhs[kc], nT1[:, kc, c0:c0 + n]) for kc in range(16)],
                     reads=[RS[s], B_NT], writes=[PB[b]])
            consume(b, c0, n)

    first = {"vpre": True, "ya": True, "vT": True}

    def xs_guard(key):
        if first[key]:
            first[key] = False
            return list(B_XS)
        return []

    def b_pre(j):
        dg = dg_[j % 2]
        d3 = dg.ap.rearrange("p (k c) -> p k c", k=NPE)
        id3 = ident.unsqueeze(1).to_broadcast([128, NPE, 128])
        cw3 = prm[:, P_CW + j * 31 + KD:P_CW + (j + 1) * 31].unsqueeze(2).to_broadcast([128, NPE, 128])
        P.op("pool", lambda e: e.tensor_tensor(out=d3, in0=id3, in1=cw3, op=ALU.mult),
             reads=[B_ID, B_PRM], writes=dg.bufs)

    def b_gg(j):
        sg = sg1

        def cons_g(b, c0, n):
            P.op("act", lambda e: e.activation(out=sg.ap[:, c0:c0 + n], in_=banks[b][:, 0:n], func=AF.Sigmoid,
                                               bias=prm[:, P_BG + j:P_BG + j + 1]),
                 reads=[PB[b], B_PRM], writes=sg.bufs)
        inproj_ext("gg", j, cons_g)

    def b_gv(j):
        sg, ub = sg1, ub_[j % 2]

        def cons_v(b, c0, n):
            P.op("dve", lambda e: e.scalar_tensor_tensor(out=ub.ap[:, c0:c0 + n], in0=banks[b][:, 0:n],
                                                         scalar=prm[:, P_BV + j:P_BV + j + 1], in1=sg.ap[:, c0:c0 + n],
                                                         op0=ALU.add, op1=ALU.mult),
                 reads=[PB[b], B_PRM] + sg.bufs, writes=ub.bufs)
            if c0 == 0:
                P.op("dve", lambda e: e.tensor_tensor(out=ub.ap[:, 0:HALO], in0=ub.ap[:, 0:HALO], in1=hmask, op=ALU.mult),
                     reads=ub.bufs + [B_CONST], writes=ub.bufs)
        inproj_ext("gv", j, cons_v)

    def b_conv(j):
        ub, dg = ub_[j % 2], dg_[j % 2]
        d3 = dg.ap.rearrange("p (k c) -> p k c", k=NPE)
        cbanks = []
        for tt in range(2):
            b = next_bank()
            cbanks.append(b)
            base = HALO + tt * 512 - 30
            mm_group(banks[b][:, :], [(d3[:, k - KD, :], ub.ap[:, base + k:base + k + 512]) for k in range(KD, 31)],
                     reads=dg.bufs + ub.bufs, writes=[PB[b]])
        for k in range(KD):
            src = ub.ap[:, HALO - 30 + k:HALO - 30 + k + T]
            wk = prm[:, P_CW + j * 31 + k:P_CW + j * 31 + k + 1]
            if k == 0:
                P.op("dve", lambda e, src=src, wk=wk: e.tensor_scalar(out=acc1.ap, in0=src, scalar1=wk, scalar2=None, op0=ALU.mult),
                     reads=ub.bufs + [B_PRM], writes=acc1.bufs)
            else:
                P.op("dve", lambda e, src=src, wk=wk: e.scalar_tensor_tensor(out=acc1.ap, in0=src, scalar=wk, in1=acc1.ap,
                                                                           op0=ALU.mult, op1=ALU.add),
                     reads=ub.bufs + [B_PRM] + acc1.bufs, writes=acc1.bufs)
        for tt in range(2):
            b = cbanks[tt]
            P.op("act", lambda e, b=b, tt=tt: e.activation(out=vpre[:, j, tt * 512:(tt + 1) * 512], in_=banks[b][:, :],
                                                          func=AF.Identity, bias=prm[:, P_DWB + j:P_DWB + j + 1]),
                 reads=[PB[b], B_PRM], writes=[B_VPRE] + xs_guard("vpre"))
        P.op("pool", lambda e: e.tensor_tensor(out=vpre[:, j, :], in0=vpre[:, j, :], in1=acc1.ap, op=ALU.add),
             reads=acc1.bufs + [B_VPRE], writes=[B_VPRE])

    b_pre(0)
    b_gg(0)
    b_gv(0)
    for j in range(1, NJ):
        b_pre(j)
        b_gg(j)
        b_conv(j - 1)
        b_gv(j)
    b_conv(NJ - 1)

    if stop == "vpre":
        P.op("sp", lambda e: e.dma_start(out=dbg_d[:, 0:8192], in_=vpre.rearrange("p k t -> p (k t)")), reads=[B_VPRE], dma=B_OUT)
        P.op("sp", lambda e: e.nop(), writes=[B_OUT])
        P.emit()
        return nc

    z_sb = Scr(0, 4096)
    mean_sb = Scr(4096, 4096)
    var_sb = Scr(8192, 4096)
    def ln_stats():
        vsq_ = [Scr(12288, 4096), Scr(16384, 4096)]
        for j in range(NJ):
            vs = vsq_[j % 2]
            P.op("act", lambda e, vs=vs, j=j: e.activation(out=vs.ap, in_=vpre[:, j, :], func=AF.Square),
                 reads=[B_VPRE], writes=vs.bufs)

            def fn(e, vs=vs, j=j):
                r = None
                for tt in range(2):
                    e.matmul(banks[tt][:, :], lhsT=ones32, rhs=vpre[:, j, tt * 512:(tt + 1) * 512], start=(j == 0), stop=(j == NJ - 1))
                    r = e.matmul(banks[2 + tt][:, :], lhsT=ones32, rhs=vs.ap[:, tt * 512:(tt + 1) * 512], start=(j == 0), stop=(j == NJ - 1))
                return r
            P.op("pe", fn, reads=[B_VPRE, B_ONES] + vs.bufs, writes=[PB[0], PB[1], PB[2], PB[3]])
        rr["i"] = 4
        for tt in range(2):
            sl_ = slice(tt * 512, (tt + 1) * 512)
            P.op("act", lambda e, tt=tt, sl_=sl_: e.activation(out=mean_sb.ap[:, sl_], in_=banks[tt][:, :], func=AF.Copy),
                 reads=[PB[tt]], writes=mean_sb.bufs)
            P.op("act", lambda e, tt=tt, sl_=sl_: e.activation(out=var_sb.ap[:, sl_], in_=banks[tt][:, :], func=AF.Square),
                 reads=[PB[tt]], writes=var_sb.bufs)
            P.op("dve", lambda e, tt=tt, sl_=sl_: e.tensor_tensor(out=var_sb.ap[:, sl_], in0=banks[2 + tt][:, :], in1=var_sb.ap[:, sl_],
                                                                  op=ALU.subtract),
                 reads=[PB[2 + tt]] + var_sb.bufs, writes=var_sb.bufs)
        P.op("act", lambda e: e.activation(out=var_sb.ap, in_=var_sb.ap, func=AF.Sqrt, bias=epsl, scale=1.0),
             reads=var_sb.bufs + [B_EPS], writes=var_sb.bufs)
        P.op("dve", lambda e: e.reciprocal(out=var_sb.ap, in_=var_sb.ap), reads=var_sb.bufs, writes=var_sb.bufs)
        P.op("dve", lambda e: e.scalar_tensor_tensor(out=mean_sb.ap, in0=mean_sb.ap, scalar=-1.0, in1=var_sb.ap,
                                                     op0=ALU.mult, op1=ALU.mult),
             reads=mean_sb.bufs + var_sb.bufs, writes=mean_sb.bufs)


    def ln_chunk(j):
        z = z_sb
        P.op("pool", lambda e: e.tensor_tensor(out=z.ap, in0=vpre[:, j, :], in1=var_sb.ap, op=ALU.mult),
             reads=[B_VPRE] + var_sb.bufs, writes=z.bufs)
        P.op("dve", lambda e: e.tensor_tensor(out=z.ap, in0=z.ap, in1=mean_sb.ap, op=ALU.add),
             reads=z.bufs + mean_sb.bufs, writes=z.bufs)
        P.op("act", lambda e: e.activation(out=vT[:, j, :], in_=z.ap, func=AF.Silu,
                                           bias=prm[:, P_LNB + j:P_LNB + j + 1], scale=prm[:, P_LNG + j:P_LNG + j + 1]),
             reads=z.bufs + [B_PRM], writes=[B_VT] + xs_guard("vT"))

    ah = Scr(12288, 4224)
    ca = Scr(16896, 4224)
    acc3 = Scr(21504, 4096)
    for j in range(NJ):
        def cons_h(b, c0, n):
            P.op("act", lambda e: e.activation(out=ah.ap[:, c0:c0 + n], in_=banks[b][:, 0:n], func=AF.Copy),
                 reads=[PB[b]], writes=ah.bufs)
        inproj_ext("ah", j, cons_h)

        def cons_c(b, c0, n):
            P.op("dve", lambda e: e.tensor_tensor(out=ca.ap[:, c0:c0 + n], in0=banks[b][:, 0:n], in1=ah.ap[:, c0:c0 + n], op=ALU.mult),
                 reads=[PB[b]] + ah.bufs, writes=ca.bufs)
        inproj_ext("ac", j, cons_c)

        for k in range(3):
            src = ca.ap[:, HALO - 2 + k:HALO - 2 + k + T]
            wk = prm[:, P_CA + j * 3 + k:P_CA + j * 3 + k + 1]
            if k == 0:
                P.op("dve", lambda e, src=src, wk=wk: e.tensor_scalar(out=acc3.ap, in0=src, scalar1=wk, scalar2=None, op0=ALU.mult),
                     reads=ca.bufs + [B_PRM], writes=acc3.bufs)
            else:
                P.op("dve", lambda e, src=src, wk=wk: e.scalar_tensor_tensor(out=acc3.ap, in0=src, scalar=wk, in1=acc3.ap,
                                                                           op0=ALU.mult, op1=ALU.add),
                     reads=ca.bufs + [B_PRM] + acc3.bufs, writes=acc3.bufs)

        name, q = pos_in[("ab", j)]
        s, = use(name)
        lhs = w_in_chunk(s, q)
        for tt in range(2):
            b = next_bank()
            mm_group(banks[b][:, :], [(lhs[kc], nT1[:, kc, HALO + tt * 512:HALO + (tt + 1) * 512]) for kc in range(16)],
                     reads=[RS[s], B_NT], writes=[PB[b]])
            P.op("dve", lambda e, b=b, j=j, tt=tt: e.tensor_tensor(out=yaT[:, j, tt * 512:(tt + 1) * 512], in0=banks[b][:, :],
                                                                 in1=acc3.ap[:, tt * 512:(tt + 1) * 512], op=ALU.mult),
                 reads=[PB[b]] + acc3.bufs, writes=[B_YA] + xs_guard("ya"))
        if j == 0:
            ln_stats()
        else:
            ln_chunk(j - 1)
    ln_chunk(NJ - 1)

    if stop == "vT":
        return dump_bf(vT, 2, 4, [B_VT])
    if stop == "ya":
        return dump_bf(yaT, 2, 4, [B_YA])

    sga_ = [Scr(0, 2048), Scr(2048, 2048)]
    sgb2_ = [Scr(4096, 2048), Scr(6144, 2048)]
    t1_ = [Scr(8192, 2048), Scr(10240, 2048)]
    t2_ = [Scr(12288, 2048), Scr(14336, 2048)]
    it = 0
    first_mt = True
    x0s = Scr(16384, 8192)
    P.op("sp", lambda e: e.dma_start(out=x0s.ap, in_=x_d[HALO:HALO + 128, :]), writes=x0s.bufs, dma=x0s.bufs[0])
    for c in range(16):
        s, = use(f"G{c}")
        woa = ring[s][:, 0:1024].rearrange("p (k c) -> p k c", k=8)
        wpw = ring[s][:, 1024:2048].rearrange("p (k c) -> p k c", k=8)
        wga = ring[s][:, 2048:4096].rearrange("p (k c) -> p k c", k=16)
        wgb = ring[s][:, 4096:6144].rearrange("p (k c) -> p k c", k=16)
        for tt in range(2):
            sga, sgb2, t1, t2 = sga_[it % 2], sgb2_[it % 2], t1_[it % 2], t2_[it % 2]
            it += 1
            ts_ = slice(tt * 512, (tt + 1) * 512)
            te_ = slice(HALO + tt * 512, HALO + (tt + 1) * 512)
            b_ga = next_bank()
            mm_group(banks[b_ga][:, :], [(wga[:, kc, :], nT1[:, kc, te_]) for kc in range(16)], reads=[RS[s], B_NT], writes=[PB[b_ga]])
            P.op("act", lambda e, b=b_ga, sga=sga: e.activation(out=sga.ap, in_=banks[b][:, :], func=AF.Sigmoid),
                 reads=[PB[b_ga]], writes=sga.bufs)
            b_ya = next_bank()
            mm_group(banks[b_ya][:, :], [(woa[:, kc, :], yaT[:, kc, ts_]) for kc in range(8)], reads=[RS[s], B_YA], writes=[PB[b_ya]])
            P.op("dve", lambda e, b=b_ya, sga=sga, t1=t1: e.tensor_tensor(out=t1.ap, in0=banks[b][:, :], in1=sga.ap, op=ALU.mult),
                 reads=[PB[b_ya]] + sga.bufs, writes=t1.bufs)
            b_gb = next_bank()
            mm_group(banks[b_gb][:, :], [(wgb[:, kc, :], nT1[:, kc, te_]) for kc in range(16)], reads=[RS[s], B_NT], writes=[PB[b_gb]])
            P.op("act", lambda e, b=b_gb, sgb2=sgb2: e.activation(out=sgb2.ap, in_=banks[b][:, :], func=AF.Sigmoid),
                 reads=[PB[b_gb]], writes=sgb2.bufs)
            b_yb = next_bank()
            mm_group(banks[b_yb][:, :], [(wpw[:, kc, :], vT[:, kc, ts_]) for kc in range(8)], reads=[RS[s], B_VT], writes=[PB[b_yb]])
            P.op("dve", lambda e, b=b_yb, sgb2=sgb2, t2=t2, c=c: e.scalar_tensor_tensor(
                out=t2.ap, in0=banks[b][:, :], scalar=prm[:, P_BPW + c:P_BPW + c + 1], in1=sgb2.ap, op0=ALU.add, op1=ALU.mult),
                reads=[PB[b_yb], B_PRM] + sgb2.bufs, writes=t2.bufs)
            extra = [B_VPRE] if first_mt else []
            first_mt = False
            P.op("pool", lambda e, t1=t1, t2=t2, c=c, ts_=ts_: e.tensor_tensor(out=mT[:, c, ts_], in0=t1.ap, in1=t2.ap, op=ALU.add),
                 reads=t1.bufs + t2.bufs, writes=B_TT[4 * tt:4 * tt + 4] + extra)

    if stop == "mT":
        return dump_bf(mT, 4, 4, B_TT)

    P.op("sp", lambda e: e.nop(), writes=[B_NT, B_YA, B_VT])
    for t in range(1, NT_TILES):
        P.op("sp", lambda e, t=t: e.dma_start(out=h[:, t, :], in_=x_d[HALO + t * 128:HALO + (t + 1) * 128, :]),
             writes=[B_HT[t]], dma=B_HT[t])

    ntc = ntb + [Scr(24576, 4096, BF16)]

    class NormPipe:
        def __init__(self, col0):
            self.col0 = col0
            self.q1 = []
            self.q2 = []

        def push(self, t):
            norm_p1(h[:, t, :], [B_HT[t]], self.col0 + t, 128, ntc[t % 3])
            if self.q2:
                self._tr(self.q2.pop(0))
            if self.q1:
                t1 = self.q1.pop(0)
                norm_p2(h[:, t1, :], [B_HT[t1]], self.col0 + t1, 128, ntc[t1 % 3])
                self.q2.append(t1)
            self.q1.append(t)

        def _tr(self, t):
            transposes(ntc[t % 3], [nT2[:, 0:8, t * 128:(t + 1) * 128], nT2[:, 8:16, t * 128:(t + 1) * 128]], [B_TT[t]],
                       engs=("act", "act"))

        def flush(self):
            while self.q1 or self.q2:
                if self.q2:
                    self._tr(self.q2.pop(0))
                if self.q1:
                    t1 = self.q1.pop(0)
                    norm_p2(h[:, t1, :], [B_HT[t1]], self.col0 + t1, 128, ntc[t1 % 3])
                    self.q2.append(t1)

    np2 = NormPipe(9)
    for n in range(4):
        s, = use(f"O{n}", after=([B_HT[7]] if n == 0 else ()))
        wv = ring[s].rearrange("p (k c) -> p k c", k=16)
        for t in range(NT_TILES):
            b = next_bank()
            mm_group(banks[b][:, :], [(mT[:, kc, t * 128:(t + 1) * 128], wv[:, kc, :]) for kc in range(16)],
                     reads=[RS[s], B_TT[t]], writes=[PB[b]])
            if t == 0:
                P.op("dve", lambda e, b=b, n=n: e.tensor_tensor(out=h[:, 0, n * 512:(n + 1) * 512], in0=x0s.ap[:, n * 512:(n + 1) * 512],
                                                              in1=banks[b][:, :], op=ALU.add),
                     reads=[PB[b]] + x0s.bufs, writes=[B_HT[0]])
            else:
                P.op("dve", lambda e, b=b, t=t, n=n: e.tensor_tensor(out=h[:, t, n * 512:(n + 1) * 512], in0=h[:, t, n * 512:(n + 1) * 512],
                                                                   in1=banks[b][:, :], op=ALU.add),
                     reads=[PB[b], B_HT[t]], writes=[B_HT[t]])
            if n == 3 and stop != "h1":
                np2.push(t)
    if stop == "h1":
        return dump_h()
    np2.flush()
    P.op("sp", lambda e: e.dma_start(out=gb, in_=g_d[2:3, :].partition_broadcast(128)), writes=[B_GB], dma=B_GB)

    fT_ = [Scr(0, 8192, BF16), Scr(8192, 8192, BF16)]
    sl_ = [Scr(24576, 2048), Scr(26624, 2048)]
    it = 0
    np3 = NormPipe(17)

    def ffn_gu(blk):
        nonlocal it
        fT = fT_[blk % 2]
        f3 = fT.ap.rearrange("p (k t) -> p k t", k=4)
        sg_, su_ = use(f"FG{blk}", f"FU{blk}")
        wg = ring[sg_].rearrange("p (k c) -> p k c", k=16)
        wu = ring[su_].rearrange("p (k c) -> p k c", k=16)
        for cc in range(4):
            for tt in range(2):
                sl = sl_[it % 2]
                it += 1
                ts_ = slice(tt * 512, (tt + 1) * 512)
                bg = next_bank()
                mm_group(banks[bg][:, :], [(wg[:, kc, cc * 128:(cc + 1) * 128], nT2[:, kc, ts_]) for kc in range(16)],
                         reads=[RS[sg_]] + B_TT[4 * tt:4 * tt + 4], writes=[PB[bg]])
                P.op("act", lambda e, b=bg, sl=sl: e.activation(out=sl.ap, in_=banks[b][:, :], func=AF.Silu),
                     reads=[PB[bg]], writes=sl.bufs)
                bu = next_bank()
                mm_group(banks[bu][:, :], [(wu[:, kc, cc * 128:(cc + 1) * 128], nT2[:, kc, ts_]) for kc in range(16)],
                         reads=[RS[su_]] + B_TT[4 * tt:4 * tt + 4], writes=[PB[bu]])
                P.op("dve", lambda e, b=bu, sl=sl, f3=f3, cc=cc, ts_=ts_: e.tensor_tensor(out=f3[:, cc, ts_], in0=banks[b][:, :], in1=sl.ap, op=ALU.mult),
                     reads=[PB[bu]] + sl.bufs, writes=fT.bufs)

    def ffn_down(blk):
        fT = fT_[blk % 2]
        f3 = fT.ap.rearrange("p (k t) -> p k t", k=4)
        sd_, = use(f"FD{blk}")
        wd = ring[sd_].rearrange("p (k c) -> p k c", k=4)
        for t in range(NT_TILES):
            for n in range(4):
                b = next_bank()
                mm_group(banks[b][:, :], [(f3[:, kc, t * 128:(t + 1) * 128], wd[:, kc, n * 512:(n + 1) * 512]) for kc in range(4)],
                         reads=[RS[sd_]] + fT.bufs, writes=[PB[b]])
                P.op("dve", lambda e, b=b, t=t, n=n: e.tensor_tensor(out=h[:, t, n * 512:(n + 1) * 512], in0=h[:, t, n * 512:(n + 1) * 512],
                                                                   in1=banks[b][:, :], op=ALU.add),
                     reads=[PB[b], B_HT[t]], writes=[B_HT[t]])
            if blk == NFB - 1 and stop != "h2":
                np3.push(t)

    ffn_gu(0)
    for blk in range(1, NFB):
        ffn_gu(blk)
        ffn_down(blk - 1)
    ffn_down(NFB - 1)
    if stop == "h2":
        return dump_h()
    np3.flush()
    P.op("sp", lambda e: e.dma_start(out=gb, in_=g_d[3:4, :].partition_broadcast(128)), writes=[B_GB], dma=B_GB)

    sgp_ = [Scr(0, 2048), Scr(2048, 2048)]
    tp_ = [Scr(4096, 2048), Scr(6144, 2048)]
    ost_ = [Scr(8192, 8192), Scr(16384, 8192)]
    fq1, fq2 = [], []
    fjunk = Scr(24576, 4096, BF16)

    def fin_p2(t):
        ost = ost_[t % 2]
        norm_p2(h[:, t, :], [B_HT[t]], 25 + t, 128, ost, out_ap=ost.ap)
        P.op("sp", lambda e, t=t, ost=ost: e.dma_start(out=out_d[t * 128:(t + 1) * 128, :], in_=ost.ap),
             reads=ost.bufs, dma=ost.bufs[0])

    def fin_push(t):
        if fq2:
            fin_p2(fq2.pop(0))
        if fq1:
            t1 = fq1.pop(0)
            norm_p1(h[:, t1, :], [B_HT[t1]], 25 + t1, 128, ost_[t1 % 2], out_ap=ost_[t1 % 2].ap, junk=fjunk)
            fq2.append(t1)
        fq1.append(t)

    def fin_flush():
        while fq1 or fq2:
            if fq2:
                fin_p2(fq2.pop(0))
            if fq1:
                t1 = fq1.pop(0)
                norm_p1(h[:, t1, :], [B_HT[t1]], 25 + t1, 128, ost_[t1 % 2], out_ap=ost_[t1 % 2].ap, junk=fjunk)
                fq2.append(t1)

    it = 0

    def ple_group(n, t, spp, s_):
        nonlocal it
        wpp = ring[spp][:, 0:1024].rearrange("p (k c) -> p k c", k=2)
        wv = ring[s_].rearrange("p (k c) -> p k c", k=16)
        sgp, tp = sgp_[it % 2], tp_[it % 2]
        it += 1
        bgt = next_bank()
        mm_group(banks[bgt][:, :], [(nT2[:, kc, t * 128:(t + 1) * 128], wv[:, kc, :]) for kc in range(16)],
                 reads=[RS[s_], B_TT[t]], writes=[PB[bgt]])
        P.op("act", lambda e, b=bgt, sgp=sgp: e.activation(out=sgp.ap, in_=banks[b][:, :], func=AF.Sigmoid),
             reads=[PB[bgt]], writes=sgp.bufs)
        bp = next_bank()
        mm_group(banks[bp][:, :], [(pT[:, ec, t * 128:(t + 1) * 128], wpp[:, ec, :]) for ec in range(2)],
                 reads=[RS[spp], B_PT], writes=[PB[bp]])
        P.op("dve", lambda e, b=bp, sgp=sgp, tp=tp: e.tensor_tensor(out=tp.ap, in0=banks[b][:, :], in1=sgp.ap, op=ALU.mult),
             reads=[PB[bp]] + sgp.bufs, writes=tp.bufs)
        P.op("dve" if n >= 2 else "pool",
             lambda e, tp=tp, t=t, n=n: e.tensor_tensor(out=h[:, t, n * 512:(n + 1) * 512], in0=h[:, t, n * 512:(n + 1) * 512],
                                                       in1=tp.ap, op=ALU.add),
             reads=tp.bufs + [B_HT[t]], writes=[B_HT[t]])

    for n in range(2):
        spp, s_ = use(f"PP{n}", f"PG{n}")
        for t in range(NT_TILES):
            ple_group(n, t, spp, s_)
    spp2, s2, spp3, s3 = use("PP2", "PG2", "PP3", "PG3")
    SKEW = 3
    for i in range(NT_TILES + SKEW):
        if i < NT_TILES:
            ple_group(2, i, spp2, s2)
        if i - SKEW >= 0:
            ple_group(3, i - SKEW, spp3, s3)
            fin_push(i - SKEW)
    fin_flush()
    P.op("sp", lambda e: e.nop(), writes=ost_[0].bufs + ost_[1].bufs)
    P.emit()
    return nc


_NC_CACHE = {}


def _get_nc(stop=None):
    if stop not in _NC_CACHE:
        _NC_CACHE[stop] = build(stop)
    return _NC_CACHE[stop]


def _prep_inputs(x, p, g_mix, w_in, conv_a_w, w_out_a, b_glu, conf_dw_w, conf_dw_b, conf_ln_g, conf_ln_b,
                 w_pw_b, b_pw_b, w_o, g_ffn, w_gate, w_up, w_down, g_ple, w_ple_gate, w_ple_proj, g_final):
    f = lambda a: np.asarray(a, np.float32)
    x2 = f(x).reshape(SEQ, D)
    p2 = f(p).reshape(SEQ, PLE)
    wflat = _prep_weights(f(w_in)[0], f(w_out_a)[0], f(w_pw_b)[0], f(w_o)[0], f(w_gate)[0], f(w_up)[0], f(w_down)[0],
                          f(w_ple_gate)[0], f(w_ple_proj)[0])
    prm = _prep_prm(f(conv_a_w)[0], f(b_glu)[0], f(conf_dw_w)[0], f(conf_dw_b)[0], f(conf_ln_g)[0], f(conf_ln_b)[0], f(b_pw_b)[0])
    gains = np.ascontiguousarray(np.stack([f(g_mix)[0], f(g_ffn)[0], f(g_ple)[0], f(g_final)]))
    xpad = np.concatenate([np.zeros((HALO, D), np.float32), x2], axis=0)
    in_maps = []
    for c in range(NCORES):
        in_maps.append({
            "x_ext": np.ascontiguousarray(xpad[c * T:c * T + TE]),
            "p_c": np.ascontiguousarray(p2[c * T:(c + 1) * T].reshape(NT_TILES, 128, PLE).transpose(1, 0, 2)).reshape(128, NT_TILES * PLE),
            "hmask": np.full((128, HALO), 0.0 if c == 0 else 1.0, np.float32),
            "wflat": wflat,
            "prm": prm,
            "gains": gains,
        })
    return in_maps


def kernel(**inputs):
    in_maps = _prep_inputs(**inputs)
    nc = _get_nc(None)
    res = run_bass_kernel_spmd(nc, in_maps, core_ids=list(range(NCORES)))
    out = np.concatenate([np.asarray(r["out"], np.float32) for r in res.results], axis=0)
    return out.reshape(1, SEQ, D)
```
